# Optimizing a Trainium2 kernel written in Bass

```python
import math
import jax, jax.numpy as jnp
from jax import lax
import numpy as np

D_MODEL = 2048
BATCH = 4
SEQ = 2048
DEPTH = 2

GRID_W = 64
CTX_LEN = 256
HEAD_DIM = 128
N_MIX_HEADS = D_MODEL // HEAD_DIM
NA_HEADS = N_MIX_HEADS // 2
NA_WIN_R = 8
NA_WIN_C = 16
NA_WIDTH = NA_HEADS * HEAD_DIM
HG_HEADS = N_MIX_HEADS // 2
HG_DK = 128
HG_DV = HEAD_DIM
HG_KWIDTH = HG_HEADS * HG_DK
HG_VWIDTH = HG_HEADS * HG_DV
HG_CHUNK = 16
EVEN_IN = 3 * NA_WIDTH + 3 * HG_KWIDTH + 2 * HG_VWIDTH
EVEN_OUT = NA_WIDTH + HG_VWIDTH
DIFF_HEADS = D_MODEL // (2 * HEAD_DIM)
DIFF_DK = HEAD_DIM
DIFF_DV = 2 * HEAD_DIM
DIFF_QK_WIDTH = DIFF_HEADS * 2 * DIFF_DK
DIFF_V_WIDTH = DIFF_HEADS * DIFF_DV
ODD_IN = 2 * DIFF_QK_WIDTH + DIFF_V_WIDTH
ODD_OUT = DIFF_V_WIDTH
D_FF = 4 * D_MODEL
N_MOD = 6
ROPE_THETA = 10000.0
Q_BLOCK = 128
EPS = 1e-6
N_EVEN = (DEPTH + 1) // 2
N_ODD = DEPTH // 2

kernel_name = "hybrid_natten_hgrn2_diffattn_dit"


def rms_norm(x, g):
    xf = x.astype(jnp.float32)
    y = xf * lax.rsqrt(jnp.mean(xf * xf, axis=-1, keepdims=True) + EPS)
    return (y * g).astype(x.dtype)


def modulate(h, mod, i):
    return h * (1.0 + mod[:, i + 1, None, :]) + mod[:, i, None, :]


def split_cols(p, sizes):
    out, start = [], 0
    for s in sizes:
        out.append(p[..., start:start + s])
        start += s
    return out


def merge_heads(o):
    b, h, n, d = o.shape
    return o.transpose(0, 2, 1, 3).reshape(b, n, h * d)


def flip_t(t):
    return jnp.flip(t, axis=2)


def sq_relu_mlp(h, w1, w2):
    return jnp.square(jax.nn.relu(h @ w1)) @ w2


def axial_rope_tables(n_tokens):
    t = jnp.arange(n_tokens)
    row = (t // GRID_W).astype(jnp.float32)
    col = (t % GRID_W).astype(jnp.float32)
    n_freq = DIFF_DK // 4
    inv = ROPE_THETA ** (-jnp.arange(n_freq, dtype=jnp.float32) / n_freq)
    ang = jnp.concatenate([row[:, None] * inv, col[:, None] * inv], axis=-1)
    return jnp.cos(ang), jnp.sin(ang)


def apply_axial_rope(x, cos, sin):
    half = x.shape[-1] // 2
    x1, x2 = x[..., :half], x[..., half:]
    return jnp.concatenate([x1 * cos - x2 * sin, x1 * sin + x2 * cos], axis=-1).astype(x.dtype)


def dense_attention(q, k, v):
    s = jnp.einsum('bhqd,bhkd->bhqk', q, k).astype(jnp.float32) * q.shape[-1] ** -0.5
    return jnp.einsum('bhqk,bhkd->bhqd', jax.nn.softmax(s, axis=-1).astype(v.dtype), v)


def neighbourhood_attention(q, k, v, kc, vc, rpb):
    b, H, S, dh = q.shape
    rows = S // GRID_W
    wr = min(NA_WIN_R, rows)
    r = jnp.arange(rows)
    r0 = jnp.clip(r - wr // 2, 0, rows - wr)
    key_rows = r0[:, None] + jnp.arange(wr)[None, :]
    col = jnp.arange(GRID_W)
    c0 = jnp.clip(col - NA_WIN_C // 2, 0, GRID_W - NA_WIN_C)
    in_win = (col[None, :] >= c0[:, None]) & (col[None, :] < c0[:, None] + NA_WIN_C)
    dr = key_rows - r[:, None] + (NA_WIN_R - 1)
    dc = jnp.clip(col[None, :] - col[:, None] + (NA_WIN_C - 1), 0, 2 * NA_WIN_C - 2)
    bias = rpb[:, dr[:, None, :, None], dc[None, :, None, :]].astype(jnp.float32)
    bias = jnp.where(in_win[None, None, :, None, :], bias, -jnp.inf).reshape(H, rows, GRID_W, wr * GRID_W)
    qg = q.reshape(b, H, rows, GRID_W, dh)
    kb = k.reshape(b, H, rows, GRID_W, dh)[:, :, key_rows].reshape(b, H, rows, wr * GRID_W, dh)
    vb = v.reshape(b, H, rows, GRID_W, dh)[:, :, key_rows].reshape(b, H, rows, wr * GRID_W, dh)
    scale = dh ** -0.5
    s_lat = jnp.einsum('bhrqd,bhrkd->bhrqk', qg, kb).astype(jnp.float32) * scale + bias[None]
    s_ctx = jnp.einsum('bhrqd,bhld->bhrql', qg, kc).astype(jnp.float32) * scale
    p = jax.nn.softmax(jnp.concatenate([s_lat, s_ctx], axis=-1), axis=-1).astype(v.dtype)
    nk = wr * GRID_W
    o = (jnp.einsum('bhrqk,bhrkd->bhrqd', p[..., :nk], vb)
         + jnp.einsum('bhrql,bhld->bhrqd', p[..., nk:], vc))
    return o.reshape(b, H, S, dh)


def hgrn2_forget(z, lb):
    lb = lb.reshape(HG_HEADS, 1, HG_DK)
    f = lb + (1.0 - lb) * jax.nn.sigmoid(z.astype(jnp.float32))
    return jnp.log(f), 1.0 - f


def hgrn2_chunk_scan(q, k, log_f, v, s0, with_output):
    b, H, T, dk = q.shape
    dv = v.shape[-1]
    C = HG_CHUNK
    n = T // C
    f32 = jnp.float32
    qc = q.astype(f32).reshape(b, H, n, C, dk)
    kc = k.astype(f32).reshape(b, H, n, C, dk)
    vc = v.astype(f32).reshape(b, H, n, C, dv)
    cum = jnp.cumsum(log_f.astype(f32).reshape(b, H, n, C, dk), axis=3)
    last = cum[..., -1:, :]
    u = jnp.einsum('bhnsk,bhnsv->bhnkv', kc * jnp.exp(last - cum), vc)
    g = jnp.exp(last[..., 0, :])

    def step(S, inp):
        g_n, u_n = inp
        return g_n[..., None] * S + u_n, (S if with_output else None)

    s_final, s_prev = lax.scan(step, s0, (jnp.moveaxis(g, 2, 0), jnp.moveaxis(u, 2, 0)))
    if not with_output:
        return None, s_final
    s_prev = jnp.moveaxis(s_prev, 0, 2)
    tri = jnp.tril(jnp.ones((C, C), dtype=bool))[:, :, None]
    decay = jnp.exp(jnp.where(tri, cum[..., :, None, :] - cum[..., None, :, :], -jnp.inf))
    att = jnp.einsum('bhntk,bhnsk,bhntsk->bhnts', qc, kc, decay)
    o = (jnp.einsum('bhnts,bhnsv->bhntv', att, vc)
         + jnp.einsum('bhntk,bhnkv->bhntv', qc * jnp.exp(cum), s_prev))
    return o.reshape(b, H, T, dv).astype(v.dtype), s_final


def even_mixer(h, hc, w_in, w_out, q_g, k_g, rpb, lb_fwd, lb_bwd, gn_g, need_ctx):
    def project(t):
        p = t @ w_in
        bsz, n, _ = p.shape
        qa, ka, va, qb, zf, zb, ib, gb = split_cols(
            p, (NA_WIDTH,) * 3 + (HG_KWIDTH,) * 3 + (HG_VWIDTH,) * 2)
        hd = lambda z, H: z.reshape(bsz, n, H, -1).transpose(0, 2, 1, 3)
        return (rms_norm(hd(qa, NA_HEADS), q_g), rms_norm(hd(ka, NA_HEADS), k_g), hd(va, NA_HEADS),
                jax.nn.silu(hd(qb, HG_HEADS)), hd(zf, HG_HEADS), hd(zb, HG_HEADS),
                hd(ib, HG_HEADS), hd(gb, HG_HEADS))

    qa, ka, va, qb, zf, zb, ib, gb = project(h)
    qac, kac, vac, qbc, zfc, zbc, ibc, gbc = project(hc)
    o_na = neighbourhood_attention(qa, ka, va, kac, vac, rpb)
    logf_f, k_f = hgrn2_forget(zf, lb_fwd)
    logf_b, k_b = hgrn2_forget(zb, lb_bwd)
    logfc_f, kc_f = hgrn2_forget(zfc, lb_fwd)
    logfc_b, kc_b = hgrn2_forget(zbc, lb_bwd)
    s0 = jnp.zeros((hc.shape[0], HG_HEADS, HG_DK, HG_DV), jnp.float32)
    oc_f, sc_f = hgrn2_chunk_scan(qbc, kc_f, logfc_f, ibc, s0, need_ctx)
    oc_b, sc_b = hgrn2_chunk_scan(flip_t(qbc), flip_t(kc_b), flip_t(logfc_b), flip_t(ibc), s0, need_ctx)
    o_f, _ = hgrn2_chunk_scan(qb, k_f, logf_f, ib, sc_f, True)
    o_b, _ = hgrn2_chunk_scan(flip_t(qb), flip_t(k_b), flip_t(logf_b), flip_t(ib), sc_b, True)
    o_hg = rms_norm(o_f + flip_t(o_b), gn_g) * jax.nn.silu(gb)
    y = jnp.concatenate([merge_heads(o_na), merge_heads(o_hg)], axis=-1) @ w_out
    if not need_ctx:
        return y, None
    oc_na = dense_attention(qac, kac, vac)
    oc_hg = rms_norm(oc_f + flip_t(oc_b), gn_g) * jax.nn.silu(gbc)
    yc = jnp.concatenate([merge_heads(oc_na), merge_heads(oc_hg)], axis=-1) @ w_out
    return y, yc


def diff_weights(s, lam, dtype):
    p = jax.nn.softmax(s, axis=-1)
    return (p[:, :, 0] - lam * p[:, :, 1]).astype(dtype)


def diff_attention_latent(q, k, v, kc, vc, lam):
    b, H, _, S, dk = q.shape
    dv = v.shape[-1]
    nb = S // Q_BLOCK
    qb = jnp.moveaxis(q.reshape(b, H, 2, nb, Q_BLOCK, dk), 3, 0)
    scale = dk ** -0.5

    def one_block(qblk):
        s = jnp.concatenate([jnp.einsum('bhiqd,bhikd->bhiqk', qblk, k),
                             jnp.einsum('bhiqd,bhikd->bhiqk', qblk, kc)], axis=-1).astype(jnp.float32) * scale
        a = diff_weights(s, lam, v.dtype)
        return (jnp.einsum('bhqk,bhkd->bhqd', a[..., :S], v)
                + jnp.einsum('bhqk,bhkd->bhqd', a[..., S:], vc))

    o = lax.map(one_block, qb)
    return jnp.moveaxis(o, 0, 2).reshape(b, H, S, dv)


def odd_mixer(h, hc, w_in, w_out, q_g, k_g, lam_p, subln_g, layer_idx, cos, sin, need_ctx):
    def project(t):
        p = t @ w_in
        bsz, n, _ = p.shape
        q, k, v = split_cols(p, (DIFF_QK_WIDTH, DIFF_QK_WIDTH, DIFF_V_WIDTH))
        qk = lambda z, g: rms_norm(z.reshape(bsz, n, DIFF_HEADS, 2, DIFF_DK), g).transpose(0, 2, 3, 1, 4)
        return qk(q, q_g), qk(k, k_g), v.reshape(bsz, n, DIFF_HEADS, DIFF_DV).transpose(0, 2, 1, 3)

    q, k, v = project(h)
    q, k = apply_axial_rope(q, cos, sin), apply_axial_rope(k, cos, sin)
    qc, kc, vc = project(hc)
    lam_init = 0.8 - 0.6 * math.exp(-0.3 * layer_idx)
    lp = lam_p.astype(jnp.float32)
    lam = jnp.exp(jnp.sum(lp[0] * lp[1])) - jnp.exp(jnp.sum(lp[2] * lp[3])) + lam_init

    def finish(o):
        return merge_heads(rms_norm(o, subln_g) * (1.0 - lam_init)) @ w_out

    y = finish(diff_attention_latent(q, k, v, kc, vc, lam))
    if not need_ctx:
        return y, None
    s_c = jnp.einsum('bhiqd,bhikd->bhiqk', qc, kc).astype(jnp.float32) * DIFF_DK ** -0.5
    yc = finish(jnp.einsum('bhqk,bhkd->bhqd', diff_weights(s_c, lam, vc.dtype), vc))
    return y, yc


def setup_inputs(seed: int = 0) -> dict:
    key = jax.random.key(seed)
    ks = jax.random.split(key, 24)
    nrm = lambda k, shape, s: s * jax.random.normal(k, shape, jnp.float32)
    gain = lambda k, shape: 1.0 + 0.02 * jax.random.normal(k, shape, jnp.float32)
    return {
        "x": nrm(ks[0], (BATCH, SEQ, D_MODEL), 1.0),
        "c": nrm(ks[1], (BATCH, D_MODEL), 1.0),
        "ctx": nrm(ks[2], (BATCH, CTX_LEN, D_MODEL), 1.0),
        "c_ctx": nrm(ks[3], (D_MODEL,), 1.0),
        "ada_w": nrm(ks[4], (DEPTH, D_MODEL, N_MOD * D_MODEL), 0.5 * D_MODEL ** -0.5),
        "ada_b": nrm(ks[5], (DEPTH, N_MOD * D_MODEL), 0.01),
        "norm_mix_g": gain(ks[6], (DEPTH, D_MODEL)),
        "norm_mlp_g": gain(ks[7], (DEPTH, D_MODEL)),
        "mlp_w1": nrm(ks[8], (DEPTH, D_MODEL, D_FF), D_MODEL ** -0.5),
        "mlp_w2": nrm(ks[9], (DEPTH, D_FF, D_MODEL), D_FF ** -0.5),
        "ev_w_in": nrm(ks[10], (N_EVEN, D_MODEL, EVEN_IN), D_MODEL ** -0.5),
        "ev_w_out": nrm(ks[11], (N_EVEN, EVEN_OUT, D_MODEL), EVEN_OUT ** -0.5),
        "na_q_g": gain(ks[12], (N_EVEN, HEAD_DIM)),
        "na_k_g": gain(ks[13], (N_EVEN, HEAD_DIM)),
        "na_rpb": nrm(ks[14], (N_EVEN, NA_HEADS, 2 * NA_WIN_R - 1, 2 * NA_WIN_C - 1), 0.1),
        "hg_lb_logits": nrm(ks[15], (2, DEPTH + 1, HG_KWIDTH), 0.5),
        "hg_gnorm_g": gain(ks[16], (N_EVEN, HG_DV)),
        "od_w_in": nrm(ks[17], (N_ODD, D_MODEL, ODD_IN), D_MODEL ** -0.5),
        "od_w_out": nrm(ks[18], (N_ODD, ODD_OUT, D_MODEL), ODD_OUT ** -0.5),
        "df_q_g": gain(ks[19], (N_ODD, DIFF_DK)),
        "df_k_g": gain(ks[20], (N_ODD, DIFF_DK)),
        "df_lambda": nrm(ks[21], (N_ODD, 4, DIFF_DK), 0.1),
        "df_subln_g": gain(ks[22], (N_ODD, DIFF_DV)),
    }


def reference(x, c, ctx, c_ctx, ada_w, ada_b, norm_mix_g, norm_mlp_g, mlp_w1, mlp_w2,
              ev_w_in, ev_w_out, na_q_g, na_k_g, na_rpb, hg_lb_logits, hg_gnorm_g,
              od_w_in, od_w_out, df_q_g, df_k_g, df_lambda, df_subln_g):
    b, n_lat, _ = x.shape
    cos, sin = axial_rope_tables(n_lat)
    lb_all = jnp.cumsum(jax.nn.softmax(hg_lb_logits.astype(jnp.float32), axis=1), axis=1)
    silu_c = jax.nn.silu(c)
    silu_cc = jax.nn.silu(c_ctx)[None]
    for l in range(DEPTH):
        need_ctx = l < DEPTH - 1
        mod = (silu_c @ ada_w[l] + ada_b[l]).reshape(b, N_MOD, D_MODEL)
        modc = (silu_cc @ ada_w[l] + ada_b[l]).reshape(1, N_MOD, D_MODEL)
        h = modulate(rms_norm(x, norm_mix_g[l]), mod, 0)
        hc = modulate(rms_norm(ctx, norm_mix_g[l]), modc, 0)
        if l % 2 == 0:
            e = l // 2
            y, yc = even_mixer(h, hc, ev_w_in[e], ev_w_out[e], na_q_g[e], na_k_g[e], na_rpb[e],
                               lb_all[0, l], lb_all[1, l], hg_gnorm_g[e], need_ctx)
        else:
            o = l // 2
            y, yc = odd_mixer(h, hc, od_w_in[o], od_w_out[o], df_q_g[o], df_k_g[o], df_lambda[o],
                              df_subln_g[o], l, cos, sin, need_ctx)
        x = x + mod[:, 2, None, :] * y
        h2 = modulate(rms_norm(x, norm_mlp_g[l]), mod, 3)
        x = x + mod[:, 5, None, :] * sq_relu_mlp(h2, mlp_w1[l], mlp_w2[l])
        if need_ctx:
            ctx = ctx + modc[:, 2, None, :] * yc
            hc2 = modulate(rms_norm(ctx, norm_mlp_g[l]), modc, 3)
            ctx = ctx + modc[:, 5, None, :] * sq_relu_mlp(hc2, mlp_w1[l], mlp_w2[l])
    return x
```

```python
import contextlib
import math
import numpy as np
import ml_dtypes
import concourse.bass as bass
import concourse.mybir as mybir
from concourse.bass_utils import run_bass_kernel_spmd

F32 = mybir.dt.float32
BF16 = mybir.dt.bfloat16
U8 = mybir.dt.uint8
ALU = mybir.AluOpType
AF = mybir.ActivationFunctionType
NPBF = ml_dtypes.bfloat16

D = 2048
DFF = 8192
EPS = 1e-6
ENGS = ("pe", "act", "dve", "pool", "sp")
N_DMA_SEMS = 12


class Tok:
    __slots__ = ("w", "r")

    def __init__(self):
        self.w = None
        self.r = []


class Prog:
    def __init__(self, nc):
        self.nc = nc
        self.ins = []
        self.per_eng = {e: [] for e in ENGS}
        self.dma_count = {"sp": 0, "pool": 0}
        self.dma_last = {"sp": {}, "pool": {}}
        self.bar = {e: set() for e in ENGS}
        self.cc_count = 0
        self.cc_last = None

    def barrier(self):
        s = set()
        for e in ENGS:
            if self.per_eng[e]:
                s.add(self.per_eng[e][-1])
        for q in ("sp", "pool"):
            s.update(self.dma_last[q].values())
        if self.cc_last is not None:
            s.add(self.cc_last)
        for e in ENGS:
            self.bar[e] |= s

    def cc(self, fn, reads=(), writes=()):
        i = self._add("pool", fn, list(reads), list(writes), False)
        rec = self.ins[i]
        rec["cc"] = True
        self.cc_count += 1
        rec["ccval"] = self.cc_count
        if self.cc_last is not None:
            rec["deps"].add(self.cc_last)
        self.cc_last = i
        return i

    def _add(self, eng, fn, reads, writes, dma):
        i = len(self.ins)
        deps = set()
        for t in reads:
            if t.w is not None:
                deps.add(t.w)
        for t in writes:
            if t.w is not None:
                deps.add(t.w)
            deps.update(t.r)
        deps |= self.bar[eng]
        self.bar[eng] = set()
        rec = dict(eng=eng, fn=fn, deps=deps, dma=dma, sig=False, slot=None, val=None, cc=False)
        if dma:
            k = self.dma_count[eng]
            self.dma_count[eng] += 1
            slot = k % N_DMA_SEMS
            rec["slot"] = slot
            rec["val"] = 16 * (k // N_DMA_SEMS + 1)
            prev = self.dma_last[eng].get(slot)
            if prev is not None:
                deps.add(prev)
            self.dma_last[eng][slot] = i
        deps.discard(i)
        self.ins.append(rec)
        self.per_eng[eng].append(i)
        for t in reads:
            t.r.append(i)
        for t in writes:
            t.w = i
            t.r = []
        return i

    def op(self, eng, fn, reads=(), writes=()):
        return self._add(eng, fn, list(reads), list(writes), False)

    def dma(self, eng, out, in_, reads=(), writes=()):
        def fn(e, out=out, in_=in_):
            return e.dma_start(out=out, in_=in_)
        return self._add(eng, fn, list(reads), list(writes), True)

    def emit(self, final_wait=()):
        nc = self.nc
        ins = self.ins
        for rec in ins:
            if rec["eng"] == "pe" and not rec["dma"]:
                rec["deps"] = {d for d in rec["deps"] if not (ins[d]["eng"] == "pe" and not ins[d]["dma"])}
        for rec in ins:
            for d in rec["deps"]:
                ins[d]["sig"] = True
        for i in final_wait:
            ins[i]["sig"] = True
        cnt = {e: 0 for e in ENGS}
        for e in ENGS:
            for i in self.per_eng[e]:
                rec = ins[i]
                if not rec["dma"] and not rec["cc"] and rec["sig"]:
                    cnt[e] += 1
                    rec["val"] = cnt[e]
        with contextlib.ExitStack() as st:
            csem = {e: st.enter_context(nc.semaphore("c_" + e)) for e in ("pe", "act", "dve", "pool")}
            dsem = {q: [st.enter_context(nc.semaphore("d_%s%d" % (q, s))) for s in range(N_DMA_SEMS)]
                    for q in ("sp", "pool")}
            ccsem = st.enter_context(nc.semaphore("ccsem"))
            block = st.enter_context(nc.Block())

            def event(d):
                r = ins[d]
                if r["cc"]:
                    return ccsem, r["ccval"]
                if r["dma"]:
                    return dsem[r["eng"]][r["slot"]], r["val"]
                return csem[r["eng"]], r["val"]

            def run(engname, e):
                seen = {}
                for i in self.per_eng[engname]:
                    rec = ins[i]
                    need = {}
                    for d in rec["deps"]:
                        s, v = event(d)
                        if seen.get(s.num, 0) < v and need.get(s.num, (None, 0))[1] < v:
                            need[s.num] = (s, v)
                    for s, v in need.values():
                        e.wait_ge(s, v)
                        seen[s.num] = v
                    bi = rec["fn"](e)
                    if rec["cc"]:
                        bi.then_inc(ccsem, 1)
                    elif rec["dma"]:
                        bi.then_inc(dsem[engname][rec["slot"]], 16)
                    elif rec["sig"]:
                        bi.then_inc(csem[engname], 1)
                if engname == "sp":
                    for i in final_wait:
                        s, v = event(i)
                        e.wait_ge(s, v)

            block.tensor(lambda e: run("pe", e))
            block.scalar(lambda e: run("act", e))
            block.vector(lambda e: run("dve", e))
            block.gpsimd(lambda e: run("pool", e))
            block.sync(lambda e: run("sp", e))


class Ctx:
    def __init__(self, nc):
        self.nc = nc
        self.P = Prog(nc)
        self.arena_bytes = 207 * 1024
        self.arena = nc.alloc_sbuf_tensor("arena", [128, self.arena_bytes], U8)
        self.off = 0
        self.banks = [nc.alloc_psum_tensor("ps%d" % i, [128, 512], F32) for i in range(8)]
        self.bank_tok = [Tok() for _ in range(8)]
        self.bank_i = 0

    def carve(self, shape, dtype, at=None):
        esz = 2 if dtype == BF16 else 4
        n = int(np.prod(shape[1:]))
        nbytes = (n * esz + 31) // 32 * 32
        if at is None:
            at = self.off
            self.off += nbytes
            assert self.off <= self.arena_bytes, ("SBUF arena overflow", self.off)
        ap = self.arena[:, at:at + n * esz].bitcast(dtype)
        if len(shape) == 3:
            ap = ap.rearrange("p (a b) -> p a b", a=shape[1])
        elif len(shape) == 4:
            ap = ap.rearrange("p (a b c) -> p a b c", a=shape[1], b=shape[2])
        return ap

    bank_lo = 0
    base = 0

    def reset(self):
        self.off = self.base
        self.bank_lo = 0

    def bank(self):
        i = self.bank_i
        if i < self.bank_lo:
            i = self.bank_lo
        self.bank_i = i + 1 if i + 1 < 8 else self.bank_lo
        return self.banks[i], self.bank_tok[i]


class IO:
    def __init__(self, nc):
        self.nc = nc
        self.given = {}
        self.prefix = ""

    def inp(self, name, shape, dtype=F32):
        if name in self.given:
            return self.given[name]
        return dram_in(self.nc, self.prefix + name, shape, dtype)

    def out(self, name, shape, dtype=F32):
        if name in self.given:
            return self.given[name]
        return dram_out(self.nc, self.prefix + name, shape, dtype)


def dram_in(nc, name, shape, dtype=F32):
    return nc.dram_tensor(name, list(shape), dtype, kind="ExternalInput").ap()


def dram_out(nc, name, shape, dtype=F32):
    return nc.dram_tensor(name, list(shape), dtype, kind="ExternalOutput").ap()


def token_tiles(t_lat, t_ctx):
    tl = [(s, min(512, t_lat - s), 0) for s in range(0, t_lat, 512)]
    if t_ctx:
        tl += [(t_lat + s, min(512, t_ctx - s), 1) for s in range(0, t_ctx, 512)]
    return tl


def emit_stats(C, src_fn, tiles, rstd, t_rstd, ones, sq, t_sq, nchunks, dim, reads):
    P = C.P
    k = 0
    for (s, n, sel) in tiles:
        ps, tps = C.bank()
        for kc in range(nchunks):
            b = k % 2
            k += 1
            P.op("act", lambda e, kc=kc, s=s, n=n, b=b: e.activation(out=sq[b][:, :n], in_=src_fn(kc, s, n), func=AF.Square),
                 reads=reads, writes=[t_sq[b]])
            P.op("pe", lambda e, kc=kc, n=n, b=b, ps=ps: e.matmul(ps[:, :n], lhsT=ones, rhs=sq[b][:, :n],
                                                                 start=(kc == 0), stop=(kc == nchunks - 1)),
                 reads=[t_sq[b]], writes=[tps])
        P.op("dve", lambda e, s=s, n=n, ps=ps: e.tensor_scalar(out=rstd[:, s:s + n], in0=ps[:, :n], scalar1=1.0 / dim,
                                                            scalar2=EPS, op0=ALU.mult, op1=ALU.add),
             reads=[tps], writes=[t_rstd])
        P.op("act", lambda e, s=s, n=n: e.activation(out=rstd[:, s:s + n], in_=rstd[:, s:s + n], func=AF.Sqrt),
             reads=[t_rstd], writes=[t_rstd])
        P.op("dve", lambda e, s=s, n=n: e.reciprocal(out=rstd[:, s:s + n], in_=rstd[:, s:s + n]),
             reads=[t_rstd], writes=[t_rstd])


def build_post(t_lat, t_ctx, last, C=None, io=None):
    T = t_lat + t_ctx
    own = C is None
    if own:
        nc = bass.Bass("TRN2", target_bir_lowering=False)
        C = Ctx(nc)
        io = IO(nc)
    nc = C.nc
    C.reset()
    catT_d = io.inp("catT", [128, 16, T], BF16) if "cat_all" not in io.given else None
    x_d = io.inp("x_tok", [T, D])
    mod_d = io.inp("modT", [128, 96, 2])
    modn_d = io.inp("modTn", [128, 96, 2])
    mod_given = "mod_sb" in io.given
    gm_d = io.inp("g_mlp", [128, 16])
    gn_d = io.inp("g_next", [128, 16])
    ident_d = io.inp("ident", [128, 128])
    wout_d = io.inp("w_out", [D, D])
    w1_d = io.inp("w1", [D, DFF])
    w2_d = io.inp("w2", [DFF, D])
    xo_d = io.out("x_out", [T, D])
    if not last:
        hn_d = io.out("hnT", [128, 16, T], BF16)
    P = C.P
    tiles = token_tiles(t_lat, t_ctx)
    nt128 = T // 128

    xT = C.carve([128, 16, T], F32)
    actT = C.carve([128, 16, T], BF16)
    wbuf = [C.carve([128, 16, 512], BF16) for _ in range(4)]
    uT = [C.carve([128, 4, 512], BF16) for _ in range(2)]
    rstd = C.carve([128, T], F32)
    sq = [C.carve([128, 512], F32) for _ in range(2)]
    r32 = [C.carve([128, 512], F32) for _ in range(2)]
    ident = C.carve([128, 128], F32)
    ones = C.carve([128, 128], F32)
    mod = io.given["mod_sb"] if mod_given else C.carve([128, 96, 2], F32)
    modn = io.given["modn_sb"] if mod_given else C.carve([128, 96, 2], F32)
    gm = C.carve([128, 16], F32)
    gn = C.carve([128, 16], F32)
    A2 = C.carve([128, 16, 2], F32)
    An = C.carve([128, 16, 2], F32)
    xs = [wbuf[2].bitcast(F32).rearrange("p a b -> p (a b)")[:, 0:2048],
          wbuf[3].bitcast(F32).rearrange("p a b -> p (a b)")[:, 0:2048]]

    t_xT = [Tok() for _ in tiles]
    t_act = [Tok() for _ in tiles]
    t_w = [Tok() for _ in range(4)]
    t_u = [Tok(), Tok()]
    t_rstd, t_c = Tok(), Tok()
    t_sq = [Tok(), Tok()]
    t_r32 = [Tok(), Tok()]

    def tile_of(tok0):
        for i, (s, n, sel) in enumerate(tiles):
            if s <= tok0 < s + n:
                return i

    P.dma("sp", ident, ident_d, writes=[t_c])
    if not mod_given:
        P.dma("sp", mod, mod_d, writes=[t_c])
        P.dma("sp", modn, modn_d, writes=[t_c])
    P.dma("sp", gm, gm_d, writes=[t_c])
    P.dma("sp", gn, gn_d, writes=[t_c])
    P.op("dve", lambda e: e.memset(ones, 1.0), writes=[t_c])
    for sel in range(2):
        P.op("dve", lambda e, sel=sel: e.scalar_tensor_tensor(out=A2[:, :, sel], in0=mod[:, 64:80, sel], scalar=1.0, in1=gm,
                                                             op0=ALU.add, op1=ALU.mult), reads=[t_c], writes=[t_c])
        P.op("dve", lambda e, sel=sel: e.scalar_tensor_tensor(out=An[:, :, sel], in0=modn[:, 16:32, sel], scalar=1.0, in1=gn,
                                                             op0=ALU.add, op1=ALU.mult), reads=[t_c], writes=[t_c])

    for tt in range(nt128):
        b = tt % 2
        ti = tile_of(tt * 128)
        P.dma("sp", xs[b], x_d[tt * 128:(tt + 1) * 128, :], writes=[t_w[2 + b]])
        for g in range(4):
            ps, tps = C.bank()
            for j in range(4):
                kc = g * 4 + j
                P.op("pe", lambda e, ps=ps, j=j, kc=kc, b=b: e.transpose(ps[:, j * 128:(j + 1) * 128], xs[b][:, kc * 128:(kc + 1) * 128], ident),
                     reads=[t_w[2 + b], t_c], writes=[tps])
            P.op("act", lambda e, ps=ps, g=g, tt=tt: e.activation(out=xT[:, g * 4:(g + 1) * 4, tt * 128:(tt + 1) * 128],
                                                                 in_=ps[:, :].rearrange("p (a b) -> p a b", a=4), func=AF.Copy),
                 reads=[tps], writes=[t_xT[ti]])
    if "cat_all" in io.given:
        cat_all, kind, oh = io.given["cat_all"], io.given["cat_kind"], io.given["oh_pair"]
        for ti, (s, n, sel) in enumerate(tiles):
            for h in range(2):
                g0 = h * 1024 + s if sel == 0 else S_LAT + h * 128 + (s - t_lat)
                for r in range(2):
                    for c in range(4):
                        if kind == 0:
                            k0 = (4 * r + 2 * c) if c < 2 else (8 + 4 * r + 2 * (c - 2))
                        else:
                            k0 = 8 * r + 2 * c
                        P.dma("sp", wbuf[h][:, k0:k0 + 2, :n], cat_all.all_view(c)[r, :, :, g0:g0 + n], writes=[t_w[h]])
            P.op("dve", lambda e, n=n: e.tensor_scalar(out=wbuf[0][:, :, :n], in0=wbuf[0][:, :, :n], scalar1=oh[:, 0:1], scalar2=None, op0=ALU.mult),
                 reads=[t_w[0]], writes=[t_w[0]])
            P.op("dve", lambda e, s=s, n=n: e.scalar_tensor_tensor(out=actT[:, :, s:s + n], in0=wbuf[1][:, :, :n], scalar=oh[:, 1:2], in1=wbuf[0][:, :, :n],
                                                                  op0=ALU.mult, op1=ALU.add), reads=[t_w[0], t_w[1]], writes=[t_act[ti]])
    else:
        for ti, (s, n, sel) in enumerate(tiles):
            P.dma("sp", actT[:, :, s:s + n], catT_d[:, :, s:s + n], writes=[t_act[ti]])

    def load_w(slot, src_ap):
        P.dma("pool", wbuf[slot], src_ap, writes=[t_w[slot]])

    for g4 in range(4):
        slot = g4 % 2
        load_w(slot, wout_d[:, g4 * 512:(g4 + 1) * 512].rearrange("(kc p) n -> p kc n", p=128))
        for fcl in range(4):
            fc = g4 * 4 + fcl
            for ti, (s, n, sel) in enumerate(tiles):
                ps, tps = C.bank()
                for kc in range(16):
                    P.op("pe", lambda e, ps=ps, kc=kc, slot=slot, fcl=fcl, s=s, n=n: e.matmul(
                        ps[:, :n], lhsT=wbuf[slot][:, kc, fcl * 128:(fcl + 1) * 128], rhs=actT[:, kc, s:s + n],
                        start=(kc == 0), stop=(kc == 15)), reads=[t_w[slot], t_act[ti]], writes=[tps])
                P.op("dve", lambda e, ps=ps, fc=fc, s=s, n=n, sel=sel: e.scalar_tensor_tensor(
                    out=xT[:, fc, s:s + n], in0=ps[:, :n], scalar=mod[:, 32 + fc, sel:sel + 1], in1=xT[:, fc, s:s + n],
                    op0=ALU.mult, op1=ALU.add), reads=[tps, t_c, t_xT[ti]], writes=[t_xT[ti]])

    def norm_mod(Atab, shift_base, modtab):
        emit_stats(C, lambda kc, s, n: xT[:, kc, s:s + n], tiles, rstd, t_rstd, ones, sq, t_sq, 16, float(D),
                   reads=t_xT + [t_c])
        k = 0
        for ti, (s, n, sel) in enumerate(tiles):
            for kc in range(16):
                b = k % 2
                k += 1
                P.op("dve", lambda e, kc=kc, s=s, n=n, sel=sel, b=b: e.scalar_tensor_tensor(
                    out=r32[b][:, :n], in0=xT[:, kc, s:s + n], scalar=Atab[:, kc, sel:sel + 1], in1=rstd[:, s:s + n],
                    op0=ALU.mult, op1=ALU.mult), reads=[t_xT[ti], t_c, t_rstd], writes=[t_r32[b]])
                P.op("act", lambda e, kc=kc, s=s, n=n, sel=sel, b=b: e.activation(
                    out=actT[:, kc, s:s + n], in_=r32[b][:, :n], func=AF.Identity,
                    bias=modtab[:, shift_base + kc, sel:sel + 1]), reads=[t_r32[b], t_c], writes=[t_act[ti]])

    norm_mod(A2, 48, mod)

    for g in range(16):
        s1 = g % 2
        s2 = 2 + g % 2
        load_w(s1, w1_d[:, g * 512:(g + 1) * 512].rearrange("(kc p) n -> p kc n", p=128))
        w2v = wbuf[s2].rearrange("p a b -> p (a b)").rearrange("p (kc n) -> p kc n", kc=4)
        P.dma("pool", w2v, w2_d[g * 512:(g + 1) * 512, :].rearrange("(kc p) n -> p kc n", p=128), writes=[t_w[s2]])
        for ti, (s, n, sel) in enumerate(tiles):
            ub = (g * len(tiles) + ti) % 2
            for ffc in range(4):
                ps, tps = C.bank()
                for kc in range(16):
                    P.op("pe", lambda e, ps=ps, kc=kc, s1=s1, ffc=ffc, s=s, n=n: e.matmul(
                        ps[:, :n], lhsT=wbuf[s1][:, kc, ffc * 128:(ffc + 1) * 128], rhs=actT[:, kc, s:s + n],
                        start=(kc == 0), stop=(kc == 15)), reads=[t_w[s1], t_act[ti]], writes=[tps])
                rb = ffc % 2
                P.op("act", lambda e, ps=ps, n=n, rb=rb: e.activation(out=r32[rb][:, :n], in_=ps[:, :n], func=AF.Relu),
                     reads=[tps], writes=[t_r32[rb]])
                P.op("dve", lambda e, n=n, rb=rb, ub=ub, ffc=ffc: e.tensor_tensor(
                    out=uT[ub][:, ffc, :n], in0=r32[rb][:, :n], in1=r32[rb][:, :n], op=ALU.mult),
                    reads=[t_r32[rb]], writes=[t_u[ub]])
            for fc in range(16):
                ps, tps = C.bank()
                for ffc in range(4):
                    P.op("pe", lambda e, ps=ps, ffc=ffc, fc=fc, ub=ub, n=n, w2v=w2v: e.matmul(
                        ps[:, :n], lhsT=w2v[:, ffc, fc * 128:(fc + 1) * 128], rhs=uT[ub][:, ffc, :n],
                        start=(ffc == 0), stop=(ffc == 3)), reads=[t_w[s2], t_u[ub]], writes=[tps])
                P.op("dve", lambda e, ps=ps, fc=fc, s=s, n=n, sel=sel: e.scalar_tensor_tensor(
                    out=xT[:, fc, s:s + n], in0=ps[:, :n], scalar=mod[:, 80 + fc, sel:sel + 1], in1=xT[:, fc, s:s + n],
                    op0=ALU.mult, op1=ALU.add), reads=[tps, t_c, t_xT[ti]], writes=[t_xT[ti]])

    xo = [wbuf[0].bitcast(F32).rearrange("p a b -> p (a b)")[:, 0:2048],
          wbuf[1].bitcast(F32).rearrange("p a b -> p (a b)")[:, 0:2048]]
    outs = []
    for tt in range(nt128):
        b = tt % 2
        ti = tile_of(tt * 128)
        for g in range(4):
            ps, tps = C.bank()
            for j in range(4):
                kc = g * 4 + j
                P.op("pe", lambda e, ps=ps, j=j, kc=kc, tt=tt: e.transpose(ps[:, j * 128:(j + 1) * 128], xT[:, kc, tt * 128:(tt + 1) * 128], ident),
                     reads=[t_xT[ti], t_c], writes=[tps])
            P.op("act", lambda e, ps=ps, g=g, b=b: e.activation(out=xo[b][:, g * 512:(g + 1) * 512], in_=ps[:, :], func=AF.Copy),
                 reads=[tps], writes=[t_w[b]])
        outs.append(P.dma("sp", xo_d[tt * 128:(tt + 1) * 128, :], xo[b], reads=[t_w[b]]))
    if not last:
        norm_mod(An, 0, modn)
        for ti, (s, n, sel) in enumerate(tiles):
            if "hn_ch" in io.given:
                for c in range(4):
                    outs.append(P.dma("sp", io.given["hn_ch"].loc_view(c)[:, :, s:s + n], actT[:, 4 * c:4 * c + 4, s:s + n], reads=[t_act[ti]]))
            else:
                outs.append(P.dma("sp", hn_d[:, :, s:s + n], actT[:, :, s:s + n], reads=[t_act[ti]]))
    if own:
        P.emit(final_wait=outs)
        return nc
    return outs


class QKFinish:
    def __init__(self, C, ones, t_c, rope=True):
        self.C = C
        self.ones = ones
        self.t_c = t_c
        self.qraw = [C.carve([128, 512], F32) for _ in range(2)]
        self.sq = [C.carve([128, 512], F32) for _ in range(2)]
        self.rs = [C.carve([128, 512], F32) for _ in range(2)]
        self.t1 = [C.carve([128, 512], F32) for _ in range(2)]
        self.t2 = [C.carve([128, 512], F32) for _ in range(2)] if rope else None
        self.tk = [[Tok() for _ in range(2)] for _ in range(5)]
        self.k = 0

    def rstd_from(self, ps_raw, tps, n, b, dim):
        C, P = self.C, self.C.P
        tq, tsq, trs = self.tk[0][b], self.tk[1][b], self.tk[2][b]
        P.op("act", lambda e: e.activation(out=self.qraw[b][:, :n], in_=ps_raw[:, :n], func=AF.Copy), reads=[tps], writes=[tq])
        P.op("act", lambda e: e.activation(out=self.sq[b][:, :n], in_=ps_raw[:, :n], func=AF.Square), reads=[tps], writes=[tsq])
        ps2, tps2 = C.bank()
        P.op("pe", lambda e: e.matmul(ps2[:, :n], lhsT=self.ones, rhs=self.sq[b][:, :n], start=True, stop=True),
             reads=[tsq, self.t_c], writes=[tps2])
        P.op("dve", lambda e: e.tensor_scalar(out=self.rs[b][:, :n], in0=ps2[:, :n], scalar1=1.0 / dim, scalar2=EPS,
                                              op0=ALU.mult, op1=ALU.add), reads=[tps2], writes=[trs])
        if getattr(self, "use_ln", False):
            P.op("act", lambda e: e.activation(out=self.rs[b][:, :n], in_=self.rs[b][:, :n], func=AF.Ln), reads=[trs], writes=[trs])
            P.op("act", lambda e: e.activation(out=self.rs[b][:, :n], in_=self.rs[b][:, :n], func=AF.Exp, scale=-0.5), reads=[trs], writes=[trs])
            return
        P.op("act", lambda e: e.activation(out=self.rs[b][:, :n], in_=self.rs[b][:, :n], func=AF.Sqrt), reads=[trs], writes=[trs])
        P.op("dve", lambda e: e.reciprocal(out=self.rs[b][:, :n], in_=self.rs[b][:, :n]), reads=[trs], writes=[trs])

    def finish(self, ps_raw, tps, n, out_ap, t_out, gcol, rope=None):
        C, P = self.C, self.C.P
        b = self.k % 2
        self.k += 1
        tq, trs, tt1, tt2 = self.tk[0][b], self.tk[2][b], self.tk[3][b], self.tk[4][b]
        self.rstd_from(ps_raw, tps, n, b, 128.0)
        if rope is None:
            P.op("dve", lambda e: e.scalar_tensor_tensor(out=out_ap, in0=self.qraw[b][:, :n], scalar=gcol, in1=self.rs[b][:, :n],
                                                         op0=ALU.mult, op1=ALU.mult), reads=[tq, trs, self.t_c], writes=[t_out])
            return
        ctab, stab, pswg = rope
        ps3, tps3 = C.bank()
        P.op("pe", lambda e: e.matmul(ps3[:, :n], lhsT=pswg, rhs=self.qraw[b][:, :n], start=True, stop=True),
             reads=[tq, self.t_c], writes=[tps3])
        P.op("dve", lambda e: e.scalar_tensor_tensor(out=self.t1[b][:, :n], in0=self.qraw[b][:, :n], scalar=gcol, in1=ctab,
                                                     op0=ALU.mult, op1=ALU.mult), reads=[tq, self.t_c], writes=[tt1])
        P.op("dve", lambda e: e.tensor_tensor(out=self.t2[b][:, :n], in0=ps3[:, :n], in1=stab, op=ALU.mult),
             reads=[tps3, self.t_c], writes=[tt2])
        P.op("pool", lambda e: e.tensor_tensor(out=self.t1[b][:, :n], in0=self.t1[b][:, :n], in1=self.t2[b][:, :n], op=ALU.add),
             reads=[tt1, tt2], writes=[tt1])
        P.op("pool", lambda e: e.tensor_tensor(out=out_ap, in0=self.t1[b][:, :n], in1=self.rs[b][:, :n], op=ALU.mult),
             reads=[tt1, trs], writes=[t_out])


S_LAT = 2048
S_CTX = 256
S_ALL = S_LAT + S_CTX
LAM_INIT1 = 0.8 - 0.6 * math.exp(-0.3 * 1)


def build_mix1(C=None, io=None):
    own = C is None
    if own:
        nc = bass.Bass("TRN2", target_bir_lowering=False)
        C = Ctx(nc)
        io = IO(nc)
    nc = C.nc
    C.reset()
    hT_d = io.inp("hT", [128, 16, S_ALL], BF16) if "h_all" not in io.given else None
    w_d = io.inp("w_in", [D, 4 * 768])
    gq_d = io.inp("gq", [128, 1])
    gk_d = io.inp("gk", [128, 1])
    lamp_d = io.inp("lamp", [128, 4])
    sg_d = io.inp("sublng", [128, 2])
    rc_d = io.inp("ropeC", [128, S_LAT])
    rs_d = io.inp("ropeS", [128, S_LAT])
    psw_d = io.inp("psw", [128, 128])
    cat_d = io.out("catT", [128, 8, S_LAT], BF16)
    P = C.P
    C.bank_lo = 3

    hT = C.carve([128, 16, S_ALL], BF16)
    wb = [C.carve([128, 16, 256], BF16) for _ in range(4)]
    QT = C.carve([128, 2, S_LAT], BF16)
    KT = C.carve([128, 2, S_ALL], BF16)
    V = C.carve([128, 18, 256], BF16)
    ropeC = C.carve([128, S_LAT], F32)
    ropeS = C.carve([128, S_LAT], F32)
    catT = [C.carve([128, 2, S_LAT], BF16) for _ in range(1)]
    PT = [C.carve([128, 512], BF16) for _ in range(4)]
    on0 = C.carve([128, 2, 512], F32)
    on1 = C.carve([128, 2, 512], F32)
    rec = C.carve([128, 512], F32)
    ones = C.carve([128, 128], F32)
    onesb = C.carve([128, 128], BF16)
    psw = C.carve([128, 128], F32)
    pswq = C.carve([128, 128], F32)
    pswk = C.carve([128, 128], F32)
    small = C.carve([128, 32], F32)
    gq, gk, lamp, sg, prod, ee, nlam, gqs = (small[:, 0:1], small[:, 1:2], small[:, 2:6], small[:, 6:8], small[:, 8:10],
                                             small[:, 10:12], small[:, 12:13], small[:, 13:14])
    t_c = Tok()
    F = QKFinish(C, ones, t_c)

    t_h = [Tok() for _ in range(5)]
    t_wb = [Tok() for _ in range(4)]
    t_Q = [[Tok() for _ in range(4)] for _ in range(2)]
    t_K = [[Tok() for _ in range(5)] for _ in range(2)]
    t_V = [Tok() for _ in range(18)]
    t_cat = [Tok() for _ in range(4)]
    t_PT = [Tok() for _ in range(4)]
    t_on0, t_on1, t_rec = Tok(), Tok(), Tok()
    acc = C.banks[0:3]
    t_acc = C.bank_tok[0:3]

    ltiles = [(s, 512) for s in range(0, S_LAT, 512)]
    atiles = ltiles + [(S_LAT, 256)]

    for src, dst in ((gq_d, gq), (gk_d, gk), (lamp_d, lamp), (sg_d, sg), (rc_d, ropeC), (rs_d, ropeS), (psw_d, psw)):
        P.dma("sp", dst, src, writes=[t_c])
    if "h_all" in io.given:
        h_all = io.given["h_all"]
        for r in range(2):
            for c in range(4):
                for k in range(2):
                    P.dma("sp", hT[:, 4 * c:4 * c + 4, r * 1024 + k * 512:r * 1024 + (k + 1) * 512], h_all.all_view(c)[r, :, :, k * 512:(k + 1) * 512],
                          writes=[t_h[2 * r + k]])
                P.dma("sp", hT[:, 4 * c:4 * c + 4, S_LAT + r * 128:S_LAT + (r + 1) * 128], h_all.all_view(c)[r, :, :, 1024:1152], writes=[t_h[4]])
    else:
        for ti, (s, n) in enumerate(atiles):
            P.dma("sp", hT[:, :, s:s + n], hT_d[:, :, s:s + n], writes=[t_h[ti]])
    P.op("dve", lambda e: e.memset(ones, 1.0), writes=[t_c])
    P.op("dve", lambda e: e.memset(onesb, 1.0), writes=[t_c])
    P.op("dve", lambda e: e.tensor_scalar(out=gqs, in0=gq, scalar1=128.0 ** -0.5, scalar2=None, op0=ALU.mult), reads=[t_c], writes=[t_c])
    P.op("dve", lambda e: e.tensor_scalar(out=pswq, in0=psw, scalar1=gqs, scalar2=None, op0=ALU.mult), reads=[t_c], writes=[t_c])
    P.op("dve", lambda e: e.tensor_scalar(out=pswk, in0=psw, scalar1=gk, scalar2=None, op0=ALU.mult), reads=[t_c], writes=[t_c])
    P.op("dve", lambda e: e.tensor_tensor(out=prod[:, 0:1], in0=lamp[:, 0:1], in1=lamp[:, 1:2], op=ALU.mult), reads=[t_c], writes=[t_c])
    P.op("dve", lambda e: e.tensor_tensor(out=prod[:, 1:2], in0=lamp[:, 2:3], in1=lamp[:, 3:4], op=ALU.mult), reads=[t_c], writes=[t_c])
    psl, tpsl = C.bank()
    P.op("pe", lambda e: e.matmul(psl[:, 0:2], lhsT=ones, rhs=prod, start=True, stop=True), reads=[t_c], writes=[tpsl])
    P.op("act", lambda e: e.activation(out=ee, in_=psl[:, 0:2], func=AF.Exp), reads=[tpsl], writes=[t_c])
    P.op("dve", lambda e: e.tensor_tensor(out=nlam, in0=ee[:, 1:2], in1=ee[:, 0:1], op=ALU.subtract), reads=[t_c], writes=[t_c])
    P.op("dve", lambda e: e.tensor_scalar(out=nlam, in0=nlam, scalar1=-LAM_INIT1, scalar2=None, op0=ALU.add), reads=[t_c], writes=[t_c])
    P.op("dve", lambda e: e.tensor_scalar(out=sg, in0=sg, scalar1=1.0 - LAM_INIT1, scalar2=None, op0=ALU.mult), reads=[t_c], writes=[t_c])

    wk = 0
    outs = []
    for hl in range(4):
        bq, bk, bv = (wk) % 4, (wk + 1) % 4, (wk + 2) % 4
        wk += 3
        for j, bb in enumerate((bq, bk, bv)):
            P.dma("pool", wb[bb], w_d[:, hl * 768 + j * 256: hl * 768 + (j + 1) * 256].rearrange("(kc p) n -> p kc n", p=128),
                  writes=[t_wb[bb]])
        for (wbuf_i, dstT, t_dst, tl, gcol, pswg) in ((bq, QT, t_Q, ltiles, gqs, pswq), (bk, KT, t_K, atiles, gk, pswk)):
            for i in range(2):
                for ti, (s, n) in enumerate(tl):
                    ps, tps = C.bank()
                    for kc in range(16):
                        P.op("pe", lambda e, ps=ps, kc=kc, wbuf_i=wbuf_i, i=i, s=s, n=n: e.matmul(
                            ps[:, :n], lhsT=wb[wbuf_i][:, kc, i * 128:(i + 1) * 128], rhs=hT[:, kc, s:s + n],
                            start=(kc == 0), stop=(kc == 15)), reads=[t_wb[wbuf_i], t_h[ti]], writes=[tps])
                    rope = (ropeC[:, s:s + n], ropeS[:, s:s + n], pswg) if s < S_LAT else None
                    F.finish(ps, tps, n, dstT[:, i, s:s + n], t_dst[i][ti], gcol, rope)
        for tt in range(18):
            ps, tps = C.bank()
            for kc in range(16):
                P.op("pe", lambda e, ps=ps, kc=kc, tt=tt, bv=bv: e.matmul(
                    ps[:, 0:256], lhsT=hT[:, kc, tt * 128:(tt + 1) * 128], rhs=wb[bv][:, kc, :],
                    start=(kc == 0), stop=(kc == 15)), reads=[t_wb[bv], t_h[min(tt // 4, 4)]], writes=[tps])
            P.op("act", lambda e, ps=ps, tt=tt: e.activation(out=V[:, tt, :], in_=ps[:, 0:256], func=AF.Copy),
                 reads=[tps], writes=[t_V[tt]])
        pk = 0
        for qb, (qs, qn) in enumerate(ltiles):
            for i in range(2):
                pend = None
                for c in range(19):
                    if c < 18:
                        ps, tps = C.bank()
                        P.op("pe", lambda e, ps=ps, c=c, i=i, qs=qs: e.matmul(
                            ps[:, :], lhsT=KT[:, i, c * 128:(c + 1) * 128], rhs=QT[:, i, qs:qs + 512], start=True, stop=True),
                            reads=[t_K[i][min(c // 4, 4)], t_Q[i][qb]], writes=[tps])
                        pb = pk % 4
                        pk += 1
                        P.op("act", lambda e, ps=ps, pb=pb: e.activation(out=PT[pb], in_=ps[:, :], func=AF.Exp),
                             reads=[tps], writes=[t_PT[pb]])
                    if pend is not None:
                        pc, ppb = pend
                        for dvc in range(2):
                            P.op("pe", lambda e, pc=pc, ppb=ppb, dvc=dvc: e.matmul(
                                acc[dvc][:, :], lhsT=V[:, pc, dvc * 128:(dvc + 1) * 128], rhs=PT[ppb], start=(pc == 0), stop=(pc == 17)),
                                reads=[t_V[pc], t_PT[ppb]], writes=[t_acc[dvc]])
                        P.op("pe", lambda e, pc=pc, ppb=ppb: e.matmul(
                            acc[2][:, :], lhsT=onesb, rhs=PT[ppb], start=(pc == 0), stop=(pc == 17)),
                            reads=[t_PT[ppb], t_c], writes=[t_acc[2]])
                    pend = (c, pb) if c < 18 else None
                on, t_on = (on0, t_on0) if i == 0 else (on1, t_on1)
                P.op("dve", lambda e: e.reciprocal(out=rec, in_=acc[2][:, :]), reads=[t_acc[2]], writes=[t_rec])
                for dvc in range(2):
                    P.op("dve", lambda e, dvc=dvc, on=on: e.tensor_tensor(out=on[:, dvc, :], in0=acc[dvc][:, :], in1=rec, op=ALU.mult),
                         reads=[t_acc[dvc], t_rec], writes=[t_on])
            for dvc in range(2):
                P.op("dve", lambda e, dvc=dvc: e.scalar_tensor_tensor(out=on0[:, dvc, :], in0=on1[:, dvc, :], scalar=nlam, in1=on0[:, dvc, :],
                                                                       op0=ALU.mult, op1=ALU.add), reads=[t_on0, t_on1, t_c], writes=[t_on0])
            ps2, tps2 = C.bank()
            for dvc in range(2):
                b = F.k % 2
                F.k += 1
                P.op("act", lambda e, dvc=dvc, b=b: e.activation(out=F.sq[b], in_=on0[:, dvc, :], func=AF.Square),
                     reads=[t_on0], writes=[F.tk[1][b]])
                P.op("pe", lambda e, dvc=dvc, b=b, ps2=ps2: e.matmul(ps2[:, :], lhsT=ones, rhs=F.sq[b], start=(dvc == 0), stop=(dvc == 1)),
                     reads=[F.tk[1][b], t_c], writes=[tps2])
            P.op("dve", lambda e, ps2=ps2: e.tensor_scalar(out=rec, in0=ps2[:, :], scalar1=1.0 / 256.0, scalar2=EPS, op0=ALU.mult, op1=ALU.add),
                 reads=[tps2], writes=[t_rec])
            P.op("act", lambda e: e.activation(out=rec, in_=rec, func=AF.Sqrt), reads=[t_rec], writes=[t_rec])
            P.op("dve", lambda e: e.reciprocal(out=rec, in_=rec), reads=[t_rec], writes=[t_rec])
            for dvc in range(2):
                P.op("dve", lambda e, dvc=dvc, qs=qs: e.scalar_tensor_tensor(
                    out=catT[0][:, dvc, qs:qs + 512], in0=on0[:, dvc, :], scalar=sg[:, dvc:dvc + 1], in1=rec, op0=ALU.mult, op1=ALU.mult),
                    reads=[t_on0, t_rec, t_c], writes=[t_cat[qb]])
            if "cat_ch" in io.given:
                for dvc in range(2):
                    outs.append(P.dma("sp", io.given["cat_ch"].loc_j(hl * 2 + dvc, qs, qs + 512), catT[0][:, dvc, qs:qs + 512], reads=[t_cat[qb]]))
            else:
                outs.append(P.dma("sp", cat_d[:, hl * 2:(hl + 1) * 2, qs:qs + 512], catT[0][:, :, qs:qs + 512], reads=[t_cat[qb]]))
    if own:
        P.emit(final_wait=outs)
        return nc
    return outs


def rope_consts():
    t = np.arange(S_LAT)
    row = (t // 64).astype(np.float32)
    col = (t % 64).astype(np.float32)
    inv = (10000.0 ** (-np.arange(32, dtype=np.float32) / 32)).astype(np.float32)
    ang = np.concatenate([row[:, None] * inv, col[:, None] * inv], axis=-1).astype(np.float32)
    cos, sin = np.cos(ang).astype(np.float32), np.sin(ang).astype(np.float32)
    ropeC = np.ascontiguousarray(np.concatenate([cos, cos], 1).T)
    ropeS = np.ascontiguousarray(np.concatenate([-sin, sin], 1).T)
    psw = np.zeros((128, 128), np.float32)
    for m in range(128):
        psw[(m + 64) % 128, m] = 1.0
    return {"ropeC": ropeC, "ropeS": ropeS, "psw": psw}


def mix1_inputs(od_w_in, q_g, k_g, lam_p, subln_g, hh):
    cols = []
    for hl in range(4):
        hd = 4 * hh + hl
        cols += list(range(hd * 256, (hd + 1) * 256))
        cols += list(range(2048 + hd * 256, 2048 + (hd + 1) * 256))
        cols += list(range(4096 + hd * 256, 4096 + (hd + 1) * 256))
    return {
        "w_in": np.ascontiguousarray(od_w_in[:, cols]),
        "gq": np.ascontiguousarray(q_g.reshape(128, 1)),
        "gk": np.ascontiguousarray(k_g.reshape(128, 1)),
        "lamp": np.ascontiguousarray(lam_p.T),
        "sublng": np.ascontiguousarray(subln_g.reshape(2, 128).T),
    }


def na_r0(r):
    return min(max(r - 4, 0), 24)


def build_mix0(C=None, io=None):
    own = C is None
    if own:
        nc = bass.Bass("TRN2", target_bir_lowering=False)
        C = Ctx(nc)
        io = IO(nc)
    nc = C.nc
    C.reset()
    x_d = io.inp("x_all", [S_ALL, D])
    mod_d = io.inp("modT", [128, 96, 2])
    gmix_d = io.inp("g_mix", [128, 16])
    w_d = io.inp("w_in", [D, 4096])
    gq_d = io.inp("gq", [128, 1])
    gk_d = io.inp("gk", [128, 1])
    gn_d = io.inp("gn", [128, 1])
    rpbg_d = io.inp("rpbG", [64, 4, 960])
    cm_d = io.inp("colmask", [128, 960])
    lbl_d = io.inp("lbl", [1, 2 * 3 * 512])
    cst_d = io.inp("hgc", [128, 7, 128])
    cat_d = io.out("catT", [128, 8, S_ALL], BF16)
    P = C.P
    C.bank_lo = 2

    hT = C.carve([128, 16, S_ALL], BF16)
    wb = [C.carve([128, 16, 128], BF16) for _ in range(4)]
    wbig = [C.carve([128, 16, 512], BF16) for _ in range(1)]
    ones = C.carve([128, 128], F32)
    onesb = C.carve([128, 128], BF16)
    identb = C.carve([128, 128], BF16)
    cst = C.carve([128, 7, 128], F32)
    cmaskb = C.carve([128, 8, 128], BF16)
    mod_given = "mod_sb" in io.given
    mod = io.given["mod_sb"] if mod_given else C.carve([128, 96, 2], F32)
    gmix = C.carve([128, 16], F32)
    A1 = C.carve([128, 16, 2], F32)
    small = C.carve([128, 16], F32)
    gq, gk, gn, gqs = small[:, 0:1], small[:, 1:2], small[:, 2:3], small[:, 3:4]
    ss, rs1 = small[:, 4:5], small[:, 5:6]
    lb = C.carve([128, 2, 512], F32)
    oml = C.carve([128, 2, 512], F32)
    t_c = Tok()
    F = QKFinish(C, ones, t_c, rope=False)
    F.use_ln = True
    xs_off = C.off
    xs = [C.carve([128, 2048], F32) for _ in range(2)]
    sqx = C.carve([128, 2048], F32)
    na_off = C.off
    QT = C.carve([128, S_ALL], BF16)
    KT = C.carve([128, S_ALL], BF16)
    V = C.carve([128, 18, 128], BF16)
    G32 = C.carve([128, 960], F32)
    CM = C.carve([128, 960], BF16)
    CM32 = G32
    E = C.carve([128, 960], BF16)
    PT = [C.carve([128, 512], BF16) for _ in range(3)]
    rec = C.carve([128, 512], F32)
    catb = [C.carve([128, 512], BF16) for _ in range(2)]
    qs_all = C.carve([128, 18, 128], F32, at=xs_off)
    zb_all = C.carve([128, 18, 128], F32, at=xs_off + 9216)
    v_all = C.carve([128, 18, 128], BF16, at=xs_off + 18432)
    oacc_off = C.off
    oacc = C.carve([128, S_ALL], F32)
    sgb = C.carve([128, S_ALL], BF16)
    lbraw = C.carve([128, 2, 3, 512], F32, at=oacc_off)
    NS = 12
    hoff = [na_off]

    def hcarve(shape, dtype):
        ap = C.carve(shape, dtype, at=hoff[0])
        hoff[0] += int(np.prod(shape[1:])) * (2 if dtype == BF16 else 4)
        return ap
    scr = [hcarve([128, 128], F32) for _ in range(NS)]
    scrb = [hcarve([128, 128], BF16) for _ in range(8)]
    vexp = [hcarve([128, 8, 128], BF16) for _ in range(2)]
    Sb = [hcarve([128, 8, 128], BF16) for _ in range(2)]
    S32 = [hcarve([128, 128], F32) for _ in range(2)]
    gch = [hcarve([128, 8], F32) for _ in range(2)]
    assert hoff[0] <= na_off + 3 * 4608 + 3840 + 1920
    t_scr = [Tok() for _ in range(NS)]
    t_scrb = [Tok() for _ in range(8)]
    t_vexp, t_Sb, t_S32, t_gch = [Tok(), Tok()], [Tok(), Tok()], [Tok(), Tok()], [Tok(), Tok()]

    t_h = [Tok() for _ in range(18)]
    t_wb = [Tok() for _ in range(4)]
    t_wbig = [Tok()]
    t_x = [Tok(), Tok()]
    t_sqx, t_small = Tok(), Tok()
    t_Q, t_K = [Tok() for _ in range(5)], [Tok() for _ in range(5)]
    t_V = [Tok() for _ in range(18)]
    t_E, t_G = Tok(), Tok()
    t_PT = [Tok() for _ in range(3)]
    t_rec = Tok()
    t_catb = [Tok(), Tok()]
    t_qs, t_va, t_zb = [Tok() for _ in range(18)], [Tok() for _ in range(18)], [Tok() for _ in range(18)]
    t_oacc = [Tok() for _ in range(18)]
    t_sgb = [Tok() for _ in range(5)]
    acc = C.banks[0:2]
    t_acc = C.bank_tok[0:2]
    atiles = [(s, 512) for s in range(0, S_LAT, 512)] + [(S_LAT, 256)]
    outs = []

    for src, dst in ((mod_d, mod), (gmix_d, gmix), (gq_d, gq), (gk_d, gk), (gn_d, gn), (cst_d, cst), (cm_d, CM32)):
        if mod_given and dst is mod:
            continue
        P.dma("sp", dst, src, writes=[t_c])
    P.dma("sp", lbraw.rearrange("p a b c -> p (a b c)"), lbl_d.partition_broadcast(128), writes=[t_c])
    P.op("dve", lambda e: e.memset(ones, 1.0), writes=[t_c])
    P.op("dve", lambda e: e.memset(onesb, 1.0), writes=[t_c])
    P.op("dve", lambda e: e.tensor_copy(out=CM, in_=CM32), reads=[t_c], writes=[t_c, t_G])
    P.op("dve", lambda e: e.tensor_copy(out=identb, in_=cst[:, 5, :]), reads=[t_c], writes=[t_c])
    for c8 in range(8):
        P.op("dve", lambda e, c8=c8: e.tensor_scalar(out=cmaskb[:, c8, :], in0=ones, scalar1=cst[:, 6, c8:c8 + 1], scalar2=None, op0=ALU.mult),
             reads=[t_c], writes=[t_c])
    P.op("dve", lambda e: e.tensor_scalar(out=gqs, in0=gq, scalar1=128.0 ** -0.5, scalar2=None, op0=ALU.mult), reads=[t_c], writes=[t_c])
    for sel in range(2):
        P.op("dve", lambda e, sel=sel: e.scalar_tensor_tensor(out=A1[:, :, sel], in0=mod[:, 16:32, sel], scalar=1.0, in1=gmix,
                                                             op0=ALU.add, op1=ALU.mult), reads=[t_c], writes=[t_c])
    P.op("act", lambda e: e.activation(out=lbraw.rearrange("p a b c -> p (a b c)"), in_=lbraw.rearrange("p a b c -> p (a b c)"), func=AF.Exp),
         reads=[t_c], writes=[t_c])
    P.op("dve", lambda e: e.tensor_tensor(out=oml, in0=lbraw[:, :, 1, :], in1=lbraw[:, :, 2, :], op=ALU.add), reads=[t_c], writes=[t_c])
    P.op("dve", lambda e: e.tensor_tensor(out=lb, in0=oml, in1=lbraw[:, :, 0, :], op=ALU.add), reads=[t_c], writes=[t_c])
    P.op("dve", lambda e: e.reciprocal(out=lb, in_=lb), reads=[t_c], writes=[t_c])
    P.op("dve", lambda e: e.tensor_tensor(out=oml, in0=oml, in1=lb, op=ALU.mult), reads=[t_c], writes=[t_c])
    P.op("dve", lambda e: e.tensor_tensor(out=lb, in0=lbraw[:, :, 0, :], in1=lb, op=ALU.mult), reads=[t_c], writes=[t_c])

    for tt in range(18):
        b = tt % 2
        sel = 0 if tt < 16 else 1
        P.dma("sp", xs[b], x_d[tt * 128:(tt + 1) * 128, :], writes=[t_x[b]])
        P.op("act", lambda e, b=b: e.activation(out=sqx, in_=xs[b], func=AF.Square), reads=[t_x[b]], writes=[t_sqx])
        P.op("dve", lambda e: e.reduce_sum(out=ss, in_=sqx, axis=mybir.AxisListType.X), reads=[t_sqx], writes=[t_small])
        P.op("dve", lambda e: e.tensor_scalar(out=rs1, in0=ss, scalar1=1.0 / D, scalar2=EPS, op0=ALU.mult, op1=ALU.add),
             reads=[t_small], writes=[t_small])
        P.op("act", lambda e: e.activation(out=rs1, in_=rs1, func=AF.Ln), reads=[t_small], writes=[t_small])
        P.op("act", lambda e: e.activation(out=rs1, in_=rs1, func=AF.Exp, scale=-0.5), reads=[t_small], writes=[t_small])
        P.op("dve", lambda e, b=b: e.tensor_scalar(out=xs[b], in0=xs[b], scalar1=rs1, scalar2=None, op0=ALU.mult),
             reads=[t_x[b], t_small], writes=[t_x[b]])
        for g in range(4):
            ps, tps = C.bank()
            for j in range(4):
                kc = g * 4 + j
                P.op("pe", lambda e, ps=ps, j=j, kc=kc, b=b: e.transpose(ps[:, j * 128:(j + 1) * 128], xs[b][:, kc * 128:(kc + 1) * 128], cst[:, 5, :]),
                     reads=[t_x[b], t_c], writes=[tps])
            for j in range(4):
                kc = g * 4 + j
                eng = "act" if j % 2 == 0 else "dve"
                if eng == "act":
                    P.op("act", lambda e, ps=ps, j=j, kc=kc, tt=tt, sel=sel: e.activation(
                        out=hT[:, kc, tt * 128:(tt + 1) * 128], in_=ps[:, j * 128:(j + 1) * 128], func=AF.Identity,
                        scale=A1[:, kc, sel:sel + 1], bias=mod[:, kc, sel:sel + 1]), reads=[tps, t_c], writes=[t_h[tt]])
                else:
                    P.op("dve", lambda e, ps=ps, j=j, kc=kc, tt=tt, sel=sel: e.tensor_scalar(
                        out=hT[:, kc, tt * 128:(tt + 1) * 128], in0=ps[:, j * 128:(j + 1) * 128],
                        scalar1=A1[:, kc, sel:sel + 1], scalar2=mod[:, kc, sel:sel + 1], op0=ALU.mult, op1=ALU.add),
                        reads=[tps, t_c], writes=[t_h[tt]])

    wk = [0]

    def load_cols(c0):
        i = wk[0] % 4
        wk[0] += 1
        P.dma("pool", wb[i], w_d[:, c0:c0 + 128].rearrange("(kc p) n -> p kc n", p=128), writes=[t_wb[i]])
        return i

    def proj_fm(wi, s, n):
        ps, tps = C.bank()
        for kc in range(16):
            P.op("pe", lambda e, kc=kc: e.matmul(ps[:, :n], lhsT=wb[wi][:, kc, :], rhs=hT[:, kc, s:s + n], start=(kc == 0), stop=(kc == 15)),
                 reads=[t_wb[wi]] + t_h[s // 128:(s + n) // 128], writes=[tps])
        return ps, tps

    for hl in range(4):
        wq, wkk, wv = load_cols(hl * 128), load_cols(512 + hl * 128), load_cols(1024 + hl * 128)
        P.dma("sp", G32[0:64, :], rpbg_d[:, hl, :], writes=[t_G])
        P.dma("sp", G32[64:128, :], rpbg_d[:, hl, :], writes=[t_G])
        P.op("act", lambda e: e.activation(out=G32, in_=G32, func=AF.Exp), reads=[t_G], writes=[t_G])
        P.op("dve", lambda e: e.tensor_tensor(out=E, in0=G32, in1=CM, op=ALU.mult), reads=[t_G, t_c], writes=[t_E])
        for (wi, dstT, t_dst, gcol) in ((wq, QT, t_Q, gqs), (wkk, KT, t_K, gk)):
            for ti, (s, n) in enumerate(atiles):
                ps, tps = proj_fm(wi, s, n)
                F.finish(ps, tps, n, dstT[:, s:s + n], t_dst[ti], gcol, None)
        for tt in range(18):
            ps, tps = C.bank()
            for kc in range(16):
                P.op("pe", lambda e, ps=ps, kc=kc, tt=tt, wv=wv: e.matmul(ps[:, 0:128], lhsT=hT[:, kc, tt * 128:(tt + 1) * 128], rhs=wb[wv][:, kc, :],
                                                                 start=(kc == 0), stop=(kc == 15)), reads=[t_wb[wv], t_h[tt]], writes=[tps])
            P.op("act", lambda e, ps=ps, tt=tt: e.activation(out=V[:, tt, :], in_=ps[:, 0:128], func=AF.Copy), reads=[tps], writes=[t_V[tt]])
        pk = 0
        for g in range(5):
            qs0, qn = atiles[g]
            if g < 4:
                lo = na_r0(8 * g)
                hi = na_r0(8 * g + 7) + 7
                chunks = [("lat", c) for c in range(lo // 2, hi // 2 + 1)] + [("ctx", 16), ("ctx", 17)]
            else:
                chunks = [("ctx", 16), ("ctx", 17)]
            for ci, (kind, c) in enumerate(chunks):
                ps, tps = C.bank()
                P.op("pe", lambda e, ps=ps, c=c, qs0=qs0, qn=qn: e.matmul(ps[:, :qn], lhsT=KT[:, c * 128:(c + 1) * 128], rhs=QT[:, qs0:qs0 + qn],
                                                                        start=True, stop=True), reads=[t_K[min(c // 4, 4)], t_Q[g]], writes=[tps])
                pb = pk % 3
                pk += 1
                P.op("act", lambda e, ps=ps, pb=pb, qn=qn: e.activation(out=PT[pb][:, :qn], in_=ps[:, :qn], func=AF.Exp), reads=[tps], writes=[t_PT[pb]])
                if kind == "lat":
                    for half in range(2):
                        kr = 2 * c + half
                        valid = [r8 for r8 in range(8) if na_r0(8 * g + r8) <= kr <= na_r0(8 * g + r8) + 7]
                        rows = slice(half * 64, (half + 1) * 64)
                        if not valid:
                            P.op("pool", lambda e, pb=pb, rows=rows: e.memset(PT[pb][rows, :], 0.0), reads=[t_PT[pb]], writes=[t_PT[pb]])
                            continue
                        a, bnd = valid[0], valid[-1] + 1
                        off = 7 + 8 * g - kr
                        P.op("dve", lambda e, pb=pb, rows=rows, a=a, bnd=bnd, off=off: e.tensor_tensor(
                            out=PT[pb][rows, a * 64:bnd * 64], in0=PT[pb][rows, a * 64:bnd * 64], in1=E[rows, (a + off) * 64:(bnd + off) * 64],
                            op=ALU.mult), reads=[t_PT[pb], t_E], writes=[t_PT[pb]])
                        if a > 0:
                            P.op("pool", lambda e, pb=pb, rows=rows, a=a: e.memset(PT[pb][rows, 0:a * 64], 0.0), reads=[t_PT[pb]], writes=[t_PT[pb]])
                        if bnd < 8:
                            P.op("pool", lambda e, pb=pb, rows=rows, bnd=bnd: e.memset(PT[pb][rows, bnd * 64:512], 0.0), reads=[t_PT[pb]], writes=[t_PT[pb]])
                first, lastc = ci == 0, ci == len(chunks) - 1
                P.op("pe", lambda e, c=c, pb=pb, qn=qn, first=first, lastc=lastc: e.matmul(
                    acc[0][:, :qn], lhsT=V[:, c, :], rhs=PT[pb][:, :qn], start=first, stop=lastc), reads=[t_V[c], t_PT[pb]], writes=[t_acc[0]])
                P.op("pe", lambda e, pb=pb, qn=qn, first=first, lastc=lastc: e.matmul(
                    acc[1][:, :qn], lhsT=onesb, rhs=PT[pb][:, :qn], start=first, stop=lastc), reads=[t_PT[pb], t_c], writes=[t_acc[1]])
            P.op("dve", lambda e, qn=qn: e.reciprocal(out=rec[:, :qn], in_=acc[1][:, :qn]), reads=[t_acc[1]], writes=[t_rec])
            cb = g % 2
            P.op("dve", lambda e, qn=qn, cb=cb: e.tensor_tensor(out=catb[cb][:, :qn], in0=acc[0][:, :qn], in1=rec[:, :qn], op=ALU.mult),
                 reads=[t_acc[0], t_rec], writes=[t_catb[cb]])
            dst = io.given["cat_ch"].loc_j(hl, qs0, qs0 + qn) if "cat_ch" in io.given else cat_d[:, hl, qs0:qs0 + qn]
            outs.append(P.dma("sp", dst, catb[cb][:, :qn], reads=[t_catb[cb]]))

    sk = [0]

    def S(n=1):
        r = []
        for _ in range(n):
            i = sk[0] % NS
            sk[0] += 1
            r.append(i)
        return r if n > 1 else r[0]

    sbk = [0]

    def SBn():
        i = sbk[0] % 8
        sbk[0] += 1
        return i

    def tt_op(eng, out, t_out, in0, t0, in1, t1, op):
        P.op(eng, lambda e: e.tensor_tensor(out=out, in0=in0, in1=in1, op=op), reads=[t0, t1], writes=[t_out])

    for hl in range(4):
        base = 1536 + hl * 640
        P.dma("pool", wbig[0], w_d[:, base:base + 512].rearrange("(kc p) n -> p kc n", p=128), writes=[t_wbig[0]])
        wg = load_cols(base + 512)
        hcol = slice(hl * 128, (hl + 1) * 128)
        for ti, (s, n) in enumerate(atiles):
            ps, tps = proj_fm(wg, s, n)
            tb = ti % 2
            tmp, ttmp = F.t1[tb], F.tk[3][tb]
            P.op("act", lambda e, ps=ps, n=n, tmp=tmp: e.activation(out=tmp[:, :n], in_=ps[:, :n], func=AF.Exp, scale=-1.0), reads=[tps], writes=[ttmp])
            P.op("dve", lambda e, n=n, tmp=tmp: e.tensor_scalar(out=tmp[:, :n], in0=tmp[:, :n], scalar1=1.0, scalar2=None, op0=ALU.add),
                 reads=[ttmp], writes=[ttmp])
            P.op("dve", lambda e, n=n, tmp=tmp: e.reciprocal(out=tmp[:, :n], in_=tmp[:, :n]), reads=[ttmp], writes=[ttmp])
            P.op("dve", lambda e, ps=ps, s=s, n=n, tmp=tmp: e.tensor_tensor(out=sgb[:, s:s + n], in0=ps[:, :n], in1=tmp[:, :n], op=ALU.mult),
                 reads=[tps, ttmp], writes=[t_sgb[ti]])
        for d in range(2):
            order = [16, 17] + list(range(16)) if d == 0 else [17, 16] + list(range(15, -1, -1))
            tri, dmat = (cst[:, 0, :], cst[:, 3, :]) if d == 0 else (cst[:, 1, :], cst[:, 4, :])
            blk = cst[:, 2, :]
            scur = 0
            P.op("dve", lambda e: e.memset(S32[0], 0.0), reads=[], writes=[t_S32[0]])
            for it, tt in enumerate(order):
                if d == 0:
                    ps, tps = C.bank()
                    for kc in range(16):
                        P.op("pe", lambda e, ps=ps, kc=kc, tt=tt: e.matmul(ps[:, :], lhsT=hT[:, kc, tt * 128:(tt + 1) * 128], rhs=wbig[0][:, kc, :],
                                                                         start=(kc == 0), stop=(kc == 15)), reads=[t_wbig[0], t_h[tt]], writes=[tps])
                    i0 = S()
                    P.op("act", lambda e, ps=ps, i0=i0: e.activation(out=scr[i0], in_=ps[:, 0:128], func=AF.Exp, scale=-1.0), reads=[tps], writes=[t_scr[i0]])
                    P.op("dve", lambda e, i0=i0: e.tensor_scalar(out=scr[i0], in0=scr[i0], scalar1=1.0, scalar2=None, op0=ALU.add), reads=[t_scr[i0]], writes=[t_scr[i0]])
                    P.op("dve", lambda e, i0=i0: e.reciprocal(out=scr[i0], in_=scr[i0]), reads=[t_scr[i0]], writes=[t_scr[i0]])
                    P.op("dve", lambda e, ps=ps, i0=i0, tt=tt: e.tensor_tensor(out=qs_all[:, tt, :], in0=ps[:, 0:128], in1=scr[i0], op=ALU.mult),
                         reads=[tps, t_scr[i0]], writes=[t_qs[tt]])
                    P.op("act", lambda e, ps=ps, tt=tt: e.activation(out=v_all[:, tt, :], in_=ps[:, 384:512], func=AF.Copy), reads=[tps], writes=[t_va[tt]])
                    P.op("act", lambda e, ps=ps, tt=tt: e.activation(out=zb_all[:, tt, :], in_=ps[:, 256:384], func=AF.Copy), reads=[tps], writes=[t_zb[tt]])
                    z_ap, t_z = ps[:, 128:256], tps
                else:
                    z_ap, t_z = zb_all[:, tt, :], t_zb[tt]
                i1, i2, i3 = S(3)
                P.op("act", lambda e, z_ap=z_ap, i1=i1: e.activation(out=scr[i1], in_=z_ap, func=AF.Exp, scale=-1.0), reads=[t_z], writes=[t_scr[i1]])
                P.op("dve", lambda e, i1=i1: e.tensor_scalar(out=scr[i1], in0=scr[i1], scalar1=1.0, scalar2=None, op0=ALU.add), reads=[t_scr[i1]], writes=[t_scr[i1]])
                P.op("dve", lambda e, i1=i1: e.reciprocal(out=scr[i1], in_=scr[i1]), reads=[t_scr[i1]], writes=[t_scr[i1]])
                P.op("dve", lambda e, i1=i1, d=d, hcol=hcol: e.tensor_tensor(out=scr[i1], in0=scr[i1], in1=oml[:, d, hcol], op=ALU.mult), reads=[t_scr[i1], t_c], writes=[t_scr[i1]])
                P.op("dve", lambda e, i1=i1, d=d, hcol=hcol: e.tensor_tensor(out=scr[i1], in0=scr[i1], in1=lb[:, d, hcol], op=ALU.add), reads=[t_scr[i1], t_c], writes=[t_scr[i1]])
                P.op("act", lambda e, i1=i1, i2=i2: e.activation(out=scr[i2], in_=scr[i1], func=AF.Ln), reads=[t_scr[i1]], writes=[t_scr[i2]])
                P.op("pool", lambda e, i1=i1, i3=i3: e.tensor_scalar(out=scr[i3], in0=scr[i1], scalar1=-1.0, scalar2=1.0, op0=ALU.mult, op1=ALU.add),
                     reads=[t_scr[i1]], writes=[t_scr[i3]])
                pc, tpc = C.bank()
                P.op("pe", lambda e, pc=pc, i2=i2, tri=tri: e.matmul(pc[:, 0:128], lhsT=tri, rhs=scr[i2], start=True, stop=True), reads=[t_scr[i2], t_c], writes=[tpc])
                P.op("pe", lambda e, pc=pc, i2=i2, dmat=dmat: e.matmul(pc[:, 128:256], lhsT=dmat, rhs=scr[i2], start=True, stop=True), reads=[t_scr[i2], t_c], writes=[tpc])
                P.op("pe", lambda e, pc=pc, i2=i2: e.matmul(pc[:, 256:264], lhsT=scr[i2], rhs=cst[:, 6, 0:8], start=True, stop=True), reads=[t_scr[i2], t_c], writes=[tpc])
                gb_ = it % 2
                P.op("act", lambda e, pc=pc, gb_=gb_: e.activation(out=gch[gb_], in_=pc[:, 256:264], func=AF.Exp), reads=[tpc], writes=[t_gch[gb_]])
                i4, i5, i6 = S(3)
                P.op("act", lambda e, pc=pc, i4=i4: e.activation(out=scr[i4], in_=pc[:, 0:128], func=AF.Exp), reads=[tpc], writes=[t_scr[i4]])
                P.op("act", lambda e, pc=pc, i5=i5: e.activation(out=scr[i5], in_=pc[:, 0:128], func=AF.Exp, scale=-1.0), reads=[tpc], writes=[t_scr[i5]])
                P.op("act", lambda e, pc=pc, i6=i6: e.activation(out=scr[i6], in_=pc[:, 128:256], func=AF.Exp), reads=[tpc], writes=[t_scr[i6]])
                bq, bk, bh = SBn(), SBn(), SBn()
                tt_op("dve", scrb[bq], t_scrb[bq], qs_all[:, tt, :], t_qs[tt], scr[i4], t_scr[i4], ALU.mult)
                tt_op("pool", scrb[bk], t_scrb[bk], scr[i3], t_scr[i3], scr[i5], t_scr[i5], ALU.mult)
                tt_op("pool", scrb[bh], t_scrb[bh], scr[i3], t_scr[i3], scr[i6], t_scr[i6], ALU.mult)
                ptr, tptr = C.bank()
                ptb = ptr[:, :].bitcast(BF16)
                P.op("pe", lambda e, ptb=ptb, bq=bq: e.transpose(ptb[:, 0:128], scrb[bq], identb), reads=[t_scrb[bq], t_c], writes=[tptr])
                P.op("pe", lambda e, ptb=ptb, bk=bk: e.transpose(ptb[:, 128:256], scrb[bk], identb), reads=[t_scrb[bk], t_c], writes=[tptr])
                bqT, bkT = SBn(), SBn()
                P.op("act", lambda e, ptb=ptb, bqT=bqT: e.activation(out=scrb[bqT], in_=ptb[:, 0:128], func=AF.Copy), reads=[tptr], writes=[t_scrb[bqT]])
                P.op("act", lambda e, ptb=ptb, bkT=bkT: e.activation(out=scrb[bkT], in_=ptb[:, 128:256], func=AF.Copy), reads=[tptr], writes=[t_scrb[bkT]])
                pa, tpa = C.bank()
                P.op("pe", lambda e, pa=pa, bkT=bkT, bqT=bqT: e.matmul(pa[:, 0:128], lhsT=scrb[bkT], rhs=scrb[bqT], start=True, stop=True),
                     reads=[t_scrb[bkT], t_scrb[bqT]], writes=[tpa])
                bA = SBn()
                P.op("dve", lambda e, pa=pa, bA=bA, tri=tri: e.tensor_tensor(out=scrb[bA], in0=pa[:, 0:128], in1=tri, op=ALU.mult), reads=[tpa, t_c], writes=[t_scrb[bA]])
                vb = it % 2
                P.op("pool", lambda e, vb=vb, tt=tt: e.tensor_tensor(out=vexp[vb], in0=cmaskb, in1=v_all[:, tt, :].unsqueeze(1).to_broadcast([128, 8, 128]),
                                                                   op=ALU.mult), reads=[t_va[tt], t_c], writes=[t_vexp[vb]])
                pu0, tpu0 = C.bank()
                pu1, tpu1 = C.bank()
                vflat = vexp[vb].rearrange("p a b -> p (a b)")
                P.op("pe", lambda e, pu0=pu0, bh=bh, vflat=vflat: e.matmul(pu0[:, :], lhsT=scrb[bh], rhs=vflat[:, 0:512], start=True, stop=True),
                     reads=[t_scrb[bh], t_vexp[vb]], writes=[tpu0])
                P.op("pe", lambda e, pu1=pu1, bh=bh, vflat=vflat: e.matmul(pu1[:, :], lhsT=scrb[bh], rhs=vflat[:, 512:1024], start=True, stop=True),
                     reads=[t_scrb[bh], t_vexp[vb]], writes=[tpu1])
                corder = list(range(8)) if d == 0 else list(range(7, -1, -1))
                sbb = it % 2
                for c8 in corder:
                    pu, tpu = (pu0, tpu0) if c8 < 4 else (pu1, tpu1)
                    P.op("act", lambda e, sbb=sbb, c8=c8, scur=scur: e.activation(out=Sb[sbb][:, c8, :], in_=S32[scur], func=AF.Copy),
                         reads=[t_S32[scur]], writes=[t_Sb[sbb]])
                    nxt = 1 - scur
                    P.op("dve", lambda e, scur=scur, nxt=nxt, gb_=gb_, c8=c8, pu=pu: e.scalar_tensor_tensor(
                        out=S32[nxt], in0=S32[scur], scalar=gch[gb_][:, c8:c8 + 1], in1=pu[:, (c8 % 4) * 128:(c8 % 4 + 1) * 128],
                        op0=ALU.mult, op1=ALU.add), reads=[t_S32[scur], t_gch[gb_], tpu], writes=[t_S32[nxt]])
                    scur = nxt
                po, tpo = C.bank()
                P.op("pe", lambda e, po=po, tt=tt, bA=bA: e.matmul(po[:, 0:128], lhsT=v_all[:, tt, :], rhs=scrb[bA], start=True, stop=False),
                     reads=[t_va[tt], t_scrb[bA]], writes=[tpo])
                for c8 in range(8):
                    P.op("pe", lambda e, po=po, c8=c8, sbb=sbb, bqT=bqT: e.matmul(po[:, c8 * 16:(c8 + 1) * 16], lhsT=Sb[sbb][:, c8, :], rhs=scrb[bqT][:, c8 * 16:(c8 + 1) * 16],
                                                                               start=False, stop=(c8 == 7)), reads=[t_Sb[sbb], t_scrb[bqT]], writes=[tpo])
                if d == 0:
                    P.op("act", lambda e, po=po, tt=tt: e.activation(out=oacc[:, tt * 128:(tt + 1) * 128], in_=po[:, 0:128], func=AF.Copy), reads=[tpo], writes=[t_oacc[tt]])
                else:
                    P.op("dve", lambda e, po=po, tt=tt: e.tensor_tensor(out=oacc[:, tt * 128:(tt + 1) * 128], in0=po[:, 0:128], in1=oacc[:, tt * 128:(tt + 1) * 128], op=ALU.add),
                         reads=[tpo, t_oacc[tt]], writes=[t_oacc[tt]])
        for ti, (s, n) in enumerate(atiles):
            b = F.k % 2
            F.k += 1
            P.op("act", lambda e, s=s, n=n, b=b: e.activation(out=F.sq[b][:, :n], in_=oacc[:, s:s + n], func=AF.Square), reads=t_oacc, writes=[F.tk[1][b]])
            ps2, tps2 = C.bank()
            P.op("pe", lambda e, ps2=ps2, n=n, b=b: e.matmul(ps2[:, :n], lhsT=ones, rhs=F.sq[b][:, :n], start=True, stop=True), reads=[F.tk[1][b], t_c], writes=[tps2])
            P.op("dve", lambda e, ps2=ps2, n=n: e.tensor_scalar(out=rec[:, :n], in0=ps2[:, :n], scalar1=1.0 / 128.0, scalar2=EPS, op0=ALU.mult, op1=ALU.add),
                 reads=[tps2], writes=[t_rec])
            P.op("act", lambda e, n=n: e.activation(out=rec[:, :n], in_=rec[:, :n], func=AF.Ln), reads=[t_rec], writes=[t_rec])
            P.op("act", lambda e, n=n: e.activation(out=rec[:, :n], in_=rec[:, :n], func=AF.Exp, scale=-0.5), reads=[t_rec], writes=[t_rec])
            P.op("dve", lambda e, s=s, n=n: e.scalar_tensor_tensor(out=rec[:, :n], in0=oacc[:, s:s + n], scalar=gn, in1=rec[:, :n], op0=ALU.mult, op1=ALU.mult),
                 reads=t_oacc + [t_rec, t_c], writes=[t_rec])
            cb = ti % 2
            P.op("dve", lambda e, s=s, n=n, cb=cb: e.tensor_tensor(out=catb[cb][:, :n], in0=rec[:, :n], in1=sgb[:, s:s + n], op=ALU.mult),
                 reads=[t_rec, t_sgb[ti]], writes=[t_catb[cb]])
            dst = io.given["cat_ch"].loc_j(4 + hl, s, s + n) if "cat_ch" in io.given else cat_d[:, 4 + hl, s:s + n]
            outs.append(P.dma("sp", dst, catb[cb][:, :n], reads=[t_catb[cb]]))
    if own:
        P.emit(final_wait=outs)
        return nc
    return outs


def mix0_consts():
    s = np.arange(128)
    same = (s[:, None] // 16) == (s[None, :] // 16)
    tri_f = (same & (s[:, None] <= s[None, :])).astype(np.float32)
    tri_b = (same & (s[:, None] >= s[None, :])).astype(np.float32)
    blk = same.astype(np.float32)
    ci = np.zeros((128, 128), np.float32)
    ci[s, s // 16] = 1.0
    hgc = np.stack([tri_f, tri_b, blk, blk - tri_f, blk - tri_b, np.eye(128, dtype=np.float32), ci], 1)
    col = np.arange(64)
    c0 = np.clip(col - 8, 0, 48)
    cm = ((col[:, None] >= c0[None, :]) & (col[:, None] < c0[None, :] + 16)).astype(np.float32)
    cm = np.tile(cm[:, None, :], (2, 15, 1)).reshape(128, 960)
    return {"hgc": np.ascontiguousarray(hgc), "colmask": np.ascontiguousarray(cm)}


def mix0_inputs(ev_w_in, na_q_g, na_k_g, na_rpb, lb_logits, gn_g, hh):
    hs = [4 * hh + j for j in range(4)]
    cols = []
    for sec in range(3):
        for h in hs:
            cols += list(range(sec * 1024 + h * 128, sec * 1024 + (h + 1) * 128))
    for h in hs:
        for sec in (3, 4, 5, 6, 7):
            cols += list(range(sec * 1024 + h * 128, sec * 1024 + (h + 1) * 128))
    kc = np.arange(64)[:, None]
    qc = np.arange(64)[None, :]
    dc = np.clip(kc - qc + 15, 0, 30)
    G = np.empty((64, 4, 15, 64), np.float32)
    for j, h in enumerate(hs):
        for jp in range(15):
            G[:, j, jp, :] = na_rpb[h, 14 - jp][dc]
    lbl = np.ascontiguousarray(lb_logits[:, :, hh * 512:(hh + 1) * 512]).reshape(1, -1)
    return {
        "w_in": np.ascontiguousarray(ev_w_in[:, cols]),
        "gq": np.ascontiguousarray(na_q_g.reshape(128, 1)), "gk": np.ascontiguousarray(na_k_g.reshape(128, 1)),
        "gn": np.ascontiguousarray(gn_g.reshape(128, 1)),
        "rpbG": np.ascontiguousarray(G.reshape(64, 4, 960)), "lbl": lbl,
    }


def build_mod(C=None, io=None):
    own = C is None
    if own:
        nc = bass.Bass("TRN2", target_bir_lowering=False)
        C = Ctx(nc)
        io = IO(nc)
    nc = C.nc
    C.reset()
    P = C.P
    cT = io.inp("cT", [128, 80])
    adaw = io.inp("adaw", [2, D, 1536])
    adab = io.inp("adab", [2, 128, 12])
    modT = io.out("modT", [2, 128, 60])
    c32 = C.carve([128, 80], F32)
    sc = C.carve([128, 80], BF16)
    bias = C.carve([128, 24], F32)
    w = [C.carve([128, 16, 1536], BF16) for l in range(2)]
    res = C.carve([128, 2, 60], F32)
    t_c32, t_sc, t_bias, t_res = Tok(), Tok(), Tok(), Tok()
    t_w = [[Tok() for _ in range(3)] for _ in range(2)]
    P.dma("sp", c32, cT, writes=[t_c32])
    P.dma("sp", bias.rearrange("p (l j) -> p l j", l=2), adab.rearrange("l p j -> p l j"), writes=[t_bias])
    P.op("act", lambda e: e.activation(out=sc, in_=c32, func=AF.Silu), reads=[t_c32], writes=[t_sc])
    for l in range(2):
        for j in range(3):
            P.dma("pool", w[l][:, :, j * 512:(j + 1) * 512],
                  adaw[l, :, j * 512:(j + 1) * 512].rearrange("(kc p) n -> p kc n", p=128), writes=[t_w[l][j]])
    for l in range(2):
        for jj in range(12):
            ps, tps = C.bank()
            for kc in range(16):
                P.op("pe", lambda e, l=l, jj=jj, kc=kc, ps=ps: e.matmul(
                    ps[:, 0:5], lhsT=w[l][:, kc, jj * 128:(jj + 1) * 128], rhs=sc[:, kc * 5:(kc + 1) * 5],
                    start=(kc == 0), stop=(kc == 15)), reads=[t_w[l][jj // 4], t_sc], writes=[tps])
            P.op("dve", lambda e, l=l, jj=jj, ps=ps: e.tensor_scalar(
                out=res[:, l, jj * 5:(jj + 1) * 5], in0=ps[:, 0:5], scalar1=bias[:, l * 12 + jj:l * 12 + jj + 1],
                scalar2=None, op0=ALU.add), reads=[tps, t_bias], writes=[t_res])
    outs = [P.dma("sp", modT.rearrange("l p j -> p l j"), res, reads=[t_res])]
    if own:
        P.emit(final_wait=outs)
        return nc
    return outs


PAIRS = [[0, 1], [2, 3], [4, 5], [6, 7]]


class Chunked:
    def __init__(self, nc, name, nj, per, T, dtype):
        self.per, self.T, self.n = per, T, nj // per
        self.loc = [nc.dram_tensor("%s_l%d" % (name, c), [128, per * T], dtype).ap() for c in range(self.n)]
        self.all = [nc.dram_tensor("%s_a%d" % (name, c), [256, per * T], dtype).ap() for c in range(self.n)]

    def loc_view(self, c):
        return self.loc[c].rearrange("p (k t) -> p k t", k=self.per)

    def all_view(self, c):
        return self.all[c].rearrange("(r p) (k t) -> r p k t", r=2, k=self.per)

    def loc_j(self, j, t0, t1):
        return self.loc_view(j // self.per)[:, j % self.per, t0:t1]


def build_fused(stop=99):
    nc = bass.Bass("TRN2", target_bir_lowering=False)
    C = Ctx(nc)
    io = IO(nc)
    P = C.P
    modtab = [C.carve([128, 96, 2], F32) for _ in range(2)]
    ohs = C.carve([128, 8], F32)
    C.base = C.off
    ohs_d = dram_in(nc, "ohs", [128, 8])
    mod_part = nc.dram_tensor("mod_part", [256, 60], F32).ap()
    mod_all = nc.dram_tensor("mod_all", [2048, 60], F32).ap()
    cat0 = Chunked(nc, "cat0", 8, 2, S_ALL, BF16)
    h1 = Chunked(nc, "h1", 16, 4, 1152, BF16)
    cat1 = Chunked(nc, "cat1", 8, 2, S_LAT, BF16)
    x1_loc = nc.dram_tensor("x1_loc", [1152, D], F32).ap()

    def gather_ch(ch):
        P.barrier()
        for c in range(ch.n):
            P.cc(lambda e, c=c: e.collective_compute("AllGather", ALU.bypass, replica_groups=PAIRS, ins=[ch.loc[c]], outs=[ch.all[c]]))
        P.barrier()

    def gather(src, dst, groups):
        P.barrier()
        P.cc(lambda e: e.collective_compute("AllGather", ALU.bypass, replica_groups=groups, ins=[src], outs=[dst]))
        P.barrier()

    io.prefix = "p0_"
    io.given = {"modT": mod_part.rearrange("(l p) c -> l p c", l=2)}
    build_mod(C, io)
    gather(mod_part, mod_all, [list(range(8))])
    C.reset()
    t_m = Tok()
    P.dma("sp", ohs, ohs_d, writes=[t_m])
    for l in range(2):
        mod5 = C.carve([128, 96, 5], F32)
        P.dma("sp", mod5.rearrange("p (r j) f -> p r (j f)", r=8), mod_all.rearrange("(r l p) c -> l p r c", r=8, l=2)[l], writes=[t_m])
        P.op("dve", lambda e, l=l, mod5=mod5: e.tensor_copy(out=modtab[l][:, :, 1], in_=mod5[:, :, 4]), reads=[t_m], writes=[t_m])
        P.op("dve", lambda e, l=l, mod5=mod5: e.tensor_scalar(out=modtab[l][:, :, 0], in0=mod5[:, :, 0], scalar1=ohs[:, 0:1], scalar2=None, op0=ALU.mult),
             reads=[t_m], writes=[t_m])
        for r in range(1, 4):
            P.op("dve", lambda e, l=l, mod5=mod5, r=r: e.scalar_tensor_tensor(out=modtab[l][:, :, 0], in0=mod5[:, :, r], scalar=ohs[:, r:r + 1],
                                                                              in1=modtab[l][:, :, 0], op0=ALU.mult, op1=ALU.add), reads=[t_m], writes=[t_m])
    P.barrier()
    if stop == 0:
        dbg = dram_out(nc, "dbg", [2, 128, 192])
        outs = [P.dma("sp", dbg[l], modtab[l].rearrange("p a b -> p (a b)")) for l in range(2)]
        P.emit(final_wait=outs)
        return nc
    io.prefix = "p1_"
    io.given = {"mod_sb": modtab[0], "modT": None, "catT": None, "cat_ch": cat0}
    build_mix0(C, io)
    gather_ch(cat0)
    io.prefix = "p2_"
    io.given = {"mod_sb": modtab[0], "modn_sb": modtab[1], "modT": None, "modTn": None,
                "cat_all": cat0, "cat_kind": 0, "oh_pair": ohs[:, 4:6],
                "x_out": x1_loc, "hnT": None, "hn_ch": h1}
    build_post(1024, 128, False, C, io)
    gather_ch(h1)
    io.prefix = "p3_"
    io.given = {"h_all": h1, "catT": None, "cat_ch": cat1}
    build_mix1(C, io)
    gather_ch(cat1)
    io.prefix = "p4_"
    io.given = {"mod_sb": modtab[1], "modn_sb": modtab[1], "modT": None, "modTn": None,
                "cat_all": cat1, "cat_kind": 1, "oh_pair": ohs[:, 4:6],
                "x_tok": x1_loc}
    outs = build_post(1024, 0, True, C, io)
    P.emit(final_wait=outs)
    return nc


_PROGS = {}


def _prog(name, fn, *args):
    key = (name,) + args
    if key not in _PROGS:
        _PROGS[key] = fn(*args)
    return _PROGS[key]


def _gtab(g):
    return np.ascontiguousarray(np.asarray(g, np.float32).reshape(16, 128).T)


def _run(nc, in_maps):
    res = run_bass_kernel_spmd(nc, in_maps, core_ids=list(range(8)))
    return res.results


def kernel_unfused(x, c, ctx, c_ctx, ada_w, ada_b, norm_mix_g, norm_mlp_g, mlp_w1, mlp_w2,
           ev_w_in, ev_w_out, na_q_g, na_k_g, na_rpb, hg_lb_logits, hg_gnorm_g,
           od_w_in, od_w_out, df_q_g, df_k_g, df_lambda, df_subln_g):
    f32 = lambda a: np.asarray(a, dtype=np.float32)
    x, c, ctx, c_ctx, ada_w, ada_b = map(f32, (x, c, ctx, c_ctx, ada_w, ada_b))
    norm_mix_g, norm_mlp_g, mlp_w1, mlp_w2 = map(f32, (norm_mix_g, norm_mlp_g, mlp_w1, mlp_w2))
    ev_w_in, ev_w_out, na_q_g, na_k_g, na_rpb, hg_lb_logits, hg_gnorm_g = map(
        f32, (ev_w_in, ev_w_out, na_q_g, na_k_g, na_rpb, hg_lb_logits, hg_gnorm_g))
    od_w_in, od_w_out, df_q_g, df_k_g, df_lambda, df_subln_g = map(
        f32, (od_w_in, od_w_out, df_q_g, df_k_g, df_lambda, df_subln_g))
    ident = np.eye(128, dtype=np.float32)

    c_all = np.concatenate([c, c_ctx[None]], 0)
    cT = np.ascontiguousarray(c_all.reshape(5, 16, 128).transpose(2, 1, 0)).reshape(128, 80)
    in_maps = []
    for i in range(8):
        cols = slice(i * 1536, (i + 1) * 1536)
        in_maps.append({"cT": cT, "adaw": np.ascontiguousarray(ada_w[:, :, cols]),
                        "adab": np.ascontiguousarray(ada_b[:, cols].reshape(2, 12, 128).transpose(0, 2, 1))})
    r = _run(_prog("mod", build_mod), in_maps)
    mod_all = np.concatenate([r[i]["modT"].reshape(2, 128, 12, 5) for i in range(8)], axis=2)

    def modtab(l, b):
        return np.ascontiguousarray(mod_all[l][:, :, [b, 4]])

    c0 = mix0_consts()
    in_maps = []
    for core in range(8):
        b, hh = core // 2, core % 2
        m = {"x_all": np.ascontiguousarray(np.concatenate([x[b], ctx[b]], 0)), "modT": modtab(0, b), "g_mix": _gtab(norm_mix_g[0])}
        m.update(mix0_inputs(ev_w_in[0], na_q_g[0], na_k_g[0], na_rpb[0], hg_lb_logits, hg_gnorm_g[0], hh))
        m.update(c0)
        in_maps.append(m)
    r = _run(_prog("mix0", build_mix0), in_maps)
    cat0 = np.empty((4, 128, 16, S_ALL), NPBF)
    for core in range(8):
        b, hh = core // 2, core % 2
        o = r[core]["catT"]
        cat0[b][:, 4 * hh:4 * hh + 4] = o[:, 0:4]
        cat0[b][:, 8 + 4 * hh:8 + 4 * hh + 4] = o[:, 4:8]

    in_maps = []
    for core in range(8):
        b, hh = core // 2, core % 2
        lat = slice(hh * 1024, (hh + 1) * 1024)
        cx = slice(hh * 128, (hh + 1) * 128)
        cxa = slice(S_LAT + hh * 128, S_LAT + (hh + 1) * 128)
        in_maps.append({
            "catT": np.ascontiguousarray(np.concatenate([cat0[b][:, :, lat], cat0[b][:, :, cxa]], 2)),
            "x_tok": np.ascontiguousarray(np.concatenate([x[b][lat], ctx[b][cx]], 0)),
            "modT": modtab(0, b), "modTn": modtab(1, b), "g_mlp": _gtab(norm_mlp_g[0]), "g_next": _gtab(norm_mix_g[1]),
            "ident": ident, "w_out": ev_w_out[0], "w1": mlp_w1[0], "w2": mlp_w2[0]})
    r = _run(_prog("post", build_post, 1024, 128, False), in_maps)
    h1 = np.empty((4, 128, 16, S_ALL), NPBF)
    x1 = []
    for core in range(8):
        b, hh = core // 2, core % 2
        hn = r[core]["hnT"]
        h1[b][:, :, hh * 1024:(hh + 1) * 1024] = hn[:, :, 0:1024]
        h1[b][:, :, S_LAT + hh * 128:S_LAT + (hh + 1) * 128] = hn[:, :, 1024:1152]
        x1.append(r[core]["x_out"][0:1024])

    rc = rope_consts()
    in_maps = []
    for core in range(8):
        b, hh = core // 2, core % 2
        m = {"hT": np.ascontiguousarray(h1[b])}
        m.update(mix1_inputs(od_w_in[0], df_q_g[0], df_k_g[0], df_lambda[0], df_subln_g[0], hh))
        m.update(rc)
        in_maps.append(m)
    r = _run(_prog("mix1", build_mix1), in_maps)
    cat1 = np.empty((4, 128, 16, S_LAT), NPBF)
    for core in range(8):
        b, hh = core // 2, core % 2
        cat1[b][:, 8 * hh:8 * hh + 8] = r[core]["catT"]

    in_maps = []
    for core in range(8):
        b, hh = core // 2, core % 2
        lat = slice(hh * 1024, (hh + 1) * 1024)
        in_maps.append({
            "catT": np.ascontiguousarray(cat1[b][:, :, lat]), "x_tok": np.ascontiguousarray(x1[core]),
            "modT": modtab(1, b), "modTn": modtab(1, b), "g_mlp": _gtab(norm_mlp_g[1]), "g_next": _gtab(norm_mix_g[1]),
            "ident": ident, "w_out": od_w_out[0], "w1": mlp_w1[1], "w2": mlp_w2[1]})
    r = _run(_prog("post", build_post, 1024, 0, True), in_maps)
    out = np.empty((4, S_LAT, D), np.float32)
    for core in range(8):
        b, hh = core // 2, core % 2
        out[b, hh * 1024:(hh + 1) * 1024] = r[core]["x_out"]
    return out


def kernel(x, c, ctx, c_ctx, ada_w, ada_b, norm_mix_g, norm_mlp_g, mlp_w1, mlp_w2,
           ev_w_in, ev_w_out, na_q_g, na_k_g, na_rpb, hg_lb_logits, hg_gnorm_g,
           od_w_in, od_w_out, df_q_g, df_k_g, df_lambda, df_subln_g):
    f32 = lambda a: np.asarray(a, dtype=np.float32)
    x, c, ctx, c_ctx, ada_w, ada_b = map(f32, (x, c, ctx, c_ctx, ada_w, ada_b))
    norm_mix_g, norm_mlp_g, mlp_w1, mlp_w2 = map(f32, (norm_mix_g, norm_mlp_g, mlp_w1, mlp_w2))
    ev_w_in, ev_w_out, na_q_g, na_k_g, na_rpb, hg_lb_logits, hg_gnorm_g = map(
        f32, (ev_w_in, ev_w_out, na_q_g, na_k_g, na_rpb, hg_lb_logits, hg_gnorm_g))
    od_w_in, od_w_out, df_q_g, df_k_g, df_lambda, df_subln_g = map(
        f32, (od_w_in, od_w_out, df_q_g, df_k_g, df_lambda, df_subln_g))
    ident = np.eye(128, dtype=np.float32)
    c_all = np.concatenate([c, c_ctx[None]], 0)
    cT = np.ascontiguousarray(c_all.reshape(5, 16, 128).transpose(2, 1, 0)).reshape(128, 80)
    c0 = mix0_consts()
    rc = rope_consts()
    in_maps = []
    for core in range(8):
        b, hh = core // 2, core % 2
        lat = slice(hh * 1024, (hh + 1) * 1024)
        cx = slice(hh * 128, (hh + 1) * 128)
        cols = slice(core * 1536, (core + 1) * 1536)
        ohs = np.zeros((128, 8), np.float32)
        ohs[:, b] = 1.0
        ohs[:, 4 + hh] = 1.0
        m = {"ohs": ohs,
             "p0_cT": cT, "p0_adaw": np.ascontiguousarray(ada_w[:, :, cols]),
             "p0_adab": np.ascontiguousarray(ada_b[:, cols].reshape(2, 12, 128).transpose(0, 2, 1)),
             "p1_x_all": np.ascontiguousarray(np.concatenate([x[b], ctx[b]], 0)), "p1_g_mix": _gtab(norm_mix_g[0]),
             "p2_x_tok": np.ascontiguousarray(np.concatenate([x[b][lat], ctx[b][cx]], 0)),
             "p2_g_mlp": _gtab(norm_mlp_g[0]), "p2_g_next": _gtab(norm_mix_g[1]), "p2_ident": ident,
             "p2_w_out": ev_w_out[0], "p2_w1": mlp_w1[0], "p2_w2": mlp_w2[0],
             "p4_g_mlp": _gtab(norm_mlp_g[1]), "p4_g_next": _gtab(norm_mix_g[1]), "p4_ident": ident,
             "p4_w_out": od_w_out[0], "p4_w1": mlp_w1[1], "p4_w2": mlp_w2[1]}
        for k, v in mix0_inputs(ev_w_in[0], na_q_g[0], na_k_g[0], na_rpb[0], hg_lb_logits, hg_gnorm_g[0], hh).items():
            m["p1_" + k] = v
        for k, v in c0.items():
            m["p1_" + k] = v
        for k, v in mix1_inputs(od_w_in[0], df_q_g[0], df_k_g[0], df_lambda[0], df_subln_g[0], hh).items():
            m["p3_" + k] = v
        for k, v in rc.items():
            m["p3_" + k] = v
        in_maps.append(m)
    r = _run(_prog("fused", build_fused), in_maps)
    out = np.empty((4, S_LAT, D), np.float32)
    for core in range(8):
        b, hh = core // 2, core % 2
        out[b, hh * 1024:(hh + 1) * 1024] = r[core]["p4_x_out"]
    return out
```

```python
import contextlib
import math
import numpy as np
import ml_dtypes
import concourse.bass as bass
import concourse.mybir as mybir
from concourse.bass_utils import run_bass_kernel_spmd

F32 = mybir.dt.float32
BF16 = mybir.dt.bfloat16
U8 = mybir.dt.uint8
ALU = mybir.AluOpType
AF = mybir.ActivationFunctionType
NPBF = ml_dtypes.bfloat16

D = 2048
DFF = 8192
EPS = 1e-6
ENGS = ("pe", "act", "dve", "pool", "sp")
N_DMA_SEMS = 12


class Tok:
    __slots__ = ("w", "r")

    def __init__(self):
        self.w = None
        self.r = []


class Prog:
    def __init__(self, nc):
        self.nc = nc
        self.ins = []
        self.per_eng = {e: [] for e in ENGS}
        self.dma_count = {"sp": 0, "pool": 0}
        self.dma_last = {"sp": {}, "pool": {}}
        self.bar = {e: set() for e in ENGS}
        self.cc_count = 0
        self.cc_last = None

    def barrier(self):
        s = set()
        for e in ENGS:
            if self.per_eng[e]:
                s.add(self.per_eng[e][-1])
        for q in ("sp", "pool"):
            s.update(self.dma_last[q].values())
        if self.cc_last is not None:
            s.add(self.cc_last)
        for e in ENGS:
            self.bar[e] |= s

    def cc(self, fn, reads=(), writes=()):
        i = self._add("pool", fn, list(reads), list(writes), False)
        rec = self.ins[i]
        rec["cc"] = True
        self.cc_count += 1
        rec["ccval"] = self.cc_count
        if self.cc_last is not None:
            rec["deps"].add(self.cc_last)
        self.cc_last = i
        return i

    def _add(self, eng, fn, reads, writes, dma):
        i = len(self.ins)
        deps = set()
        for t in reads:
            if t.w is not None:
                deps.add(t.w)
        for t in writes:
            if t.w is not None:
                deps.add(t.w)
            deps.update(t.r)
        deps |= self.bar[eng]
        self.bar[eng] = set()
        rec = dict(eng=eng, fn=fn, deps=deps, dma=dma, sig=False, slot=None, val=None, cc=False)
        if dma:
            k = self.dma_count[eng]
            self.dma_count[eng] += 1
            slot = k % N_DMA_SEMS
            rec["slot"] = slot
            rec["val"] = 16 * (k // N_DMA_SEMS + 1)
            prev = self.dma_last[eng].get(slot)
            if prev is not None:
                deps.add(prev)
            self.dma_last[eng][slot] = i
        deps.discard(i)
        self.ins.append(rec)
        self.per_eng[eng].append(i)
        for t in reads:
            t.r.append(i)
        for t in writes:
            t.w = i
            t.r = []
        return i

    def op(self, eng, fn, reads=(), writes=()):
        return self._add(eng, fn, list(reads), list(writes), False)

    def dma(self, eng, out, in_, reads=(), writes=()):
        def fn(e, out=out, in_=in_):
            return e.dma_start(out=out, in_=in_)
        return self._add(eng, fn, list(reads), list(writes), True)

    def emit(self, final_wait=()):
        nc = self.nc
        ins = self.ins
        for rec in ins:
            if rec["eng"] == "pe" and not rec["dma"]:
                rec["deps"] = {d for d in rec["deps"] if not (ins[d]["eng"] == "pe" and not ins[d]["dma"])}
        for rec in ins:
            for d in rec["deps"]:
                ins[d]["sig"] = True
        for i in final_wait:
            ins[i]["sig"] = True
        cnt = {e: 0 for e in ENGS}
        for e in ENGS:
            for i in self.per_eng[e]:
                rec = ins[i]
                if not rec["dma"] and not rec["cc"] and rec["sig"]:
                    cnt[e] += 1
                    rec["val"] = cnt[e]
        with contextlib.ExitStack() as st:
            csem = {e: st.enter_context(nc.semaphore("c_" + e)) for e in ("pe", "act", "dve", "pool")}
            dsem = {q: [st.enter_context(nc.semaphore("d_%s%d" % (q, s))) for s in range(N_DMA_SEMS)]
                    for q in ("sp", "pool")}
            ccsem = st.enter_context(nc.semaphore("ccsem"))
            block = st.enter_context(nc.Block())

            def event(d):
                r = ins[d]
                if r["cc"]:
                    return ccsem, r["ccval"]
                if r["dma"]:
                    return dsem[r["eng"]][r["slot"]], r["val"]
                return csem[r["eng"]], r["val"]

            def run(engname, e):
                seen = {}
                for i in self.per_eng[engname]:
                    rec = ins[i]
                    need = {}
                    for d in rec["deps"]:
                        s, v = event(d)
                        if seen.get(s.num, 0) < v and need.get(s.num, (None, 0))[1] < v:
                            need[s.num] = (s, v)
                    for s, v in need.values():
                        e.wait_ge(s, v)
                        seen[s.num] = v
                    bi = rec["fn"](e)
                    if rec["cc"]:
                        bi.then_inc(ccsem, 1)
                    elif rec["dma"]:
                        bi.then_inc(dsem[engname][rec["slot"]], 16)
                    elif rec["sig"]:
                        bi.then_inc(csem[engname], 1)
                if engname == "sp":
                    for i in final_wait:
                        s, v = event(i)
                        e.wait_ge(s, v)

            block.tensor(lambda e: run("pe", e))
            block.scalar(lambda e: run("act", e))
            block.vector(lambda e: run("dve", e))
            block.gpsimd(lambda e: run("pool", e))
            block.sync(lambda e: run("sp", e))


class Ctx:
    def __init__(self, nc):
        self.nc = nc
        self.P = Prog(nc)
        self.arena_bytes = 207 * 1024
        self.arena = nc.alloc_sbuf_tensor("arena", [128, self.arena_bytes], U8)
        self.off = 0
        self.banks = [nc.alloc_psum_tensor("ps%d" % i, [128, 512], F32) for i in range(8)]
        self.bank_tok = [Tok() for _ in range(8)]
        self.bank_i = 0

    def carve(self, shape, dtype, at=None):
        esz = 2 if dtype == BF16 else 4
        n = int(np.prod(shape[1:]))
        nbytes = (n * esz + 31) // 32 * 32
        if at is None:
            at = self.off
            self.off += nbytes
            assert self.off <= self.arena_bytes, ("SBUF arena overflow", self.off)
        ap = self.arena[:, at:at + n * esz].bitcast(dtype)
        if len(shape) == 3:
            ap = ap.rearrange("p (a b) -> p a b", a=shape[1])
        elif len(shape) == 4:
            ap = ap.rearrange("p (a b c) -> p a b c", a=shape[1], b=shape[2])
        return ap

    bank_lo = 0
    base = 0

    def reset(self):
        self.off = self.base
        self.bank_lo = 0

    def bank(self):
        i = self.bank_i
        if i < self.bank_lo:
            i = self.bank_lo
        self.bank_i = i + 1 if i + 1 < 8 else self.bank_lo
        return self.banks[i], self.bank_tok[i]


class IO:
    def __init__(self, nc):
        self.nc = nc
        self.given = {}
        self.prefix = ""

    def inp(self, name, shape, dtype=F32):
        if name in self.given:
            return self.given[name]
        return dram_in(self.nc, self.prefix + name, shape, dtype)

    def out(self, name, shape, dtype=F32):
        if name in self.given:
            return self.given[name]
        return dram_out(self.nc, self.prefix + name, shape, dtype)


def dram_in(nc, name, shape, dtype=F32):
    return nc.dram_tensor(name, list(shape), dtype, kind="ExternalInput").ap()


def dram_out(nc, name, shape, dtype=F32):
    return nc.dram_tensor(name, list(shape), dtype, kind="ExternalOutput").ap()


def token_tiles(t_lat, t_ctx):
    tl = [(s, min(512, t_lat - s), 0) for s in range(0, t_lat, 512)]
    if t_ctx:
        tl += [(t_lat + s, min(512, t_ctx - s), 1) for s in range(0, t_ctx, 512)]
    return tl


def emit_stats(C, src_fn, tiles, rstd, t_rstd, ones, sq, t_sq, nchunks, dim, reads):
    P = C.P
    k = 0
    for (s, n, sel) in tiles:
        ps, tps = C.bank()
        for kc in range(nchunks):
            b = k % 2
            k += 1
            P.op("act", lambda e, kc=kc, s=s, n=n, b=b: e.activation(out=sq[b][:, :n], in_=src_fn(kc, s, n), func=AF.Square),
                 reads=reads, writes=[t_sq[b]])
            P.op("pe", lambda e, kc=kc, n=n, b=b, ps=ps: e.matmul(ps[:, :n], lhsT=ones, rhs=sq[b][:, :n],
                                                                 start=(kc == 0), stop=(kc == nchunks - 1)),
                 reads=[t_sq[b]], writes=[tps])
        P.op("dve", lambda e, s=s, n=n, ps=ps: e.tensor_scalar(out=rstd[:, s:s + n], in0=ps[:, :n], scalar1=1.0 / dim,
                                                            scalar2=EPS, op0=ALU.mult, op1=ALU.add),
             reads=[tps], writes=[t_rstd])
        P.op("act", lambda e, s=s, n=n: e.activation(out=rstd[:, s:s + n], in_=rstd[:, s:s + n], func=AF.Sqrt),
             reads=[t_rstd], writes=[t_rstd])
        P.op("dve", lambda e, s=s, n=n: e.reciprocal(out=rstd[:, s:s + n], in_=rstd[:, s:s + n]),
             reads=[t_rstd], writes=[t_rstd])


def build_post(t_lat, t_ctx, last, C=None, io=None):
    T = t_lat + t_ctx
    own = C is None
    if own:
        nc = bass.Bass("TRN2", target_bir_lowering=False)
        C = Ctx(nc)
        io = IO(nc)
    nc = C.nc
    C.reset()
    catT_d = io.inp("catT", [128, 16, T], BF16) if "cat_all" not in io.given else None
    x_d = io.inp("x_tok", [T, D])
    mod_d = io.inp("modT", [128, 96, 2])
    modn_d = io.inp("modTn", [128, 96, 2])
    mod_given = "mod_sb" in io.given
    gm_d = io.inp("g_mlp", [128, 16])
    gn_d = io.inp("g_next", [128, 16])
    ident_d = io.inp("ident", [128, 128])
    wout_d = io.inp("w_out", [D, D])
    w1_d = io.inp("w1", [D, DFF])
    w2_d = io.inp("w2", [DFF, D])
    xo_d = io.out("x_out", [T, D])
    if not last:
        hn_d = io.out("hnT", [128, 16, T], BF16)
    P = C.P
    tiles = token_tiles(t_lat, t_ctx)
    nt128 = T // 128

    xT = C.carve([128, 16, T], F32)
    actT = C.carve([128, 16, T], BF16)
    wbuf = [C.carve([128, 16, 512], BF16) for _ in range(4)]
    uT = [C.carve([128, 4, 512], BF16) for _ in range(2)]
    rstd = C.carve([128, T], F32)
    sq = [C.carve([128, 512], F32) for _ in range(2)]
    r32 = [C.carve([128, 512], F32) for _ in range(2)]
    ident = C.carve([128, 128], F32)
    ones = C.carve([128, 128], F32)
    mod = io.given["mod_sb"] if mod_given else C.carve([128, 96, 2], F32)
    modn = io.given["modn_sb"] if mod_given else C.carve([128, 96, 2], F32)
    gm = C.carve([128, 16], F32)
    gn = C.carve([128, 16], F32)
    A2 = C.carve([128, 16, 2], F32)
    An = C.carve([128, 16, 2], F32)
    xs = [wbuf[2].bitcast(F32).rearrange("p a b -> p (a b)")[:, 0:2048],
          wbuf[3].bitcast(F32).rearrange("p a b -> p (a b)")[:, 0:2048]]

    t_xT = [Tok() for _ in tiles]
    t_act = [Tok() for _ in tiles]
    t_w = [Tok() for _ in range(4)]
    t_u = [Tok(), Tok()]
    t_rstd, t_c = Tok(), Tok()
    t_sq = [Tok(), Tok()]
    t_r32 = [Tok(), Tok()]

    def tile_of(tok0):
        for i, (s, n, sel) in enumerate(tiles):
            if s <= tok0 < s + n:
                return i

    P.dma("sp", ident, ident_d, writes=[t_c])
    if not mod_given:
        P.dma("sp", mod, mod_d, writes=[t_c])
        P.dma("sp", modn, modn_d, writes=[t_c])
    P.dma("sp", gm, gm_d, writes=[t_c])
    P.dma("sp", gn, gn_d, writes=[t_c])
    P.op("dve", lambda e: e.memset(ones, 1.0), writes=[t_c])
    for sel in range(2):
        P.op("dve", lambda e, sel=sel: e.scalar_tensor_tensor(out=A2[:, :, sel], in0=mod[:, 64:80, sel], scalar=1.0, in1=gm,
                                                             op0=ALU.add, op1=ALU.mult), reads=[t_c], writes=[t_c])
        P.op("dve", lambda e, sel=sel: e.scalar_tensor_tensor(out=An[:, :, sel], in0=modn[:, 16:32, sel], scalar=1.0, in1=gn,
                                                             op0=ALU.add, op1=ALU.mult), reads=[t_c], writes=[t_c])

    for tt in range(nt128):
        b = tt % 2
        ti = tile_of(tt * 128)
        P.dma("sp", xs[b], x_d[tt * 128:(tt + 1) * 128, :], writes=[t_w[2 + b]])
        for g in range(4):
            ps, tps = C.bank()
            for j in range(4):
                kc = g * 4 + j
                P.op("pe", lambda e, ps=ps, j=j, kc=kc, b=b: e.transpose(ps[:, j * 128:(j + 1) * 128], xs[b][:, kc * 128:(kc + 1) * 128], ident),
                     reads=[t_w[2 + b], t_c], writes=[tps])
            P.op("act", lambda e, ps=ps, g=g, tt=tt: e.activation(out=xT[:, g * 4:(g + 1) * 4, tt * 128:(tt + 1) * 128],
                                                                 in_=ps[:, :].rearrange("p (a b) -> p a b", a=4), func=AF.Copy),
                 reads=[tps], writes=[t_xT[ti]])
    if "cat_all" in io.given:
        cat_all, kind, oh = io.given["cat_all"], io.given["cat_kind"], io.given["oh_pair"]
        for ti, (s, n, sel) in enumerate(tiles):
            for h in range(2):
                g0 = h * 1024 + s if sel == 0 else S_LAT + h * 128 + (s - t_lat)
                for r in range(2):
                    for c in range(4):
                        if kind == 0:
                            k0 = (4 * r + 2 * c) if c < 2 else (8 + 4 * r + 2 * (c - 2))
                        else:
                            k0 = 8 * r + 2 * c
                        P.dma("sp", wbuf[h][:, k0:k0 + 2, :n], cat_all.all_view(c)[r, :, :, g0:g0 + n], writes=[t_w[h]])
            P.op("dve", lambda e, n=n: e.tensor_scalar(out=wbuf[0][:, :, :n], in0=wbuf[0][:, :, :n], scalar1=oh[:, 0:1], scalar2=None, op0=ALU.mult),
                 reads=[t_w[0]], writes=[t_w[0]])
            P.op("dve", lambda e, s=s, n=n: e.scalar_tensor_tensor(out=actT[:, :, s:s + n], in0=wbuf[1][:, :, :n], scalar=oh[:, 1:2], in1=wbuf[0][:, :, :n],
                                                                  op0=ALU.mult, op1=ALU.add), reads=[t_w[0], t_w[1]], writes=[t_act[ti]])
    else:
        for ti, (s, n, sel) in enumerate(tiles):
            P.dma("sp", actT[:, :, s:s + n], catT_d[:, :, s:s + n], writes=[t_act[ti]])

    def load_w(slot, src_ap):
        P.dma("pool", wbuf[slot], src_ap, writes=[t_w[slot]])

    for g4 in range(4):
        slot = g4 % 2
        load_w(slot, wout_d[:, g4 * 512:(g4 + 1) * 512].rearrange("(kc p) n -> p kc n", p=128))
        for fcl in range(4):
            fc = g4 * 4 + fcl
            for ti, (s, n, sel) in enumerate(tiles):
                ps, tps = C.bank()
                for kc in range(16):
                    P.op("pe", lambda e, ps=ps, kc=kc, slot=slot, fcl=fcl, s=s, n=n: e.matmul(
                        ps[:, :n], lhsT=wbuf[slot][:, kc, fcl * 128:(fcl + 1) * 128], rhs=actT[:, kc, s:s + n],
                        start=(kc == 0), stop=(kc == 15)), reads=[t_w[slot], t_act[ti]], writes=[tps])
                P.op("dve", lambda e, ps=ps, fc=fc, s=s, n=n, sel=sel: e.scalar_tensor_tensor(
                    out=xT[:, fc, s:s + n], in0=ps[:, :n], scalar=mod[:, 32 + fc, sel:sel + 1], in1=xT[:, fc, s:s + n],
                    op0=ALU.mult, op1=ALU.add), reads=[tps, t_c, t_xT[ti]], writes=[t_xT[ti]])

    def norm_mod(Atab, shift_base, modtab):
        emit_stats(C, lambda kc, s, n: xT[:, kc, s:s + n], tiles, rstd, t_rstd, ones, sq, t_sq, 16, float(D),
                   reads=t_xT + [t_c])
        k = 0
        for ti, (s, n, sel) in enumerate(tiles):
            for kc in range(16):
                b = k % 2
                k += 1
                P.op("dve", lambda e, kc=kc, s=s, n=n, sel=sel, b=b: e.scalar_tensor_tensor(
                    out=r32[b][:, :n], in0=xT[:, kc, s:s + n], scalar=Atab[:, kc, sel:sel + 1], in1=rstd[:, s:s + n],
                    op0=ALU.mult, op1=ALU.mult), reads=[t_xT[ti], t_c, t_rstd], writes=[t_r32[b]])
                P.op("act", lambda e, kc=kc, s=s, n=n, sel=sel, b=b: e.activation(
                    out=actT[:, kc, s:s + n], in_=r32[b][:, :n], func=AF.Identity,
                    bias=modtab[:, shift_base + kc, sel:sel + 1]), reads=[t_r32[b], t_c], writes=[t_act[ti]])

    norm_mod(A2, 48, mod)

    for g in range(16):
        s1 = g % 2
        s2 = 2 + g % 2
        load_w(s1, w1_d[:, g * 512:(g + 1) * 512].rearrange("(kc p) n -> p kc n", p=128))
        w2v = wbuf[s2].rearrange("p a b -> p (a b)").rearrange("p (kc n) -> p kc n", kc=4)
        P.dma("pool", w2v, w2_d[g * 512:(g + 1) * 512, :].rearrange("(kc p) n -> p kc n", p=128), writes=[t_w[s2]])
        for ti, (s, n, sel) in enumerate(tiles):
            ub = (g * len(tiles) + ti) % 2
            for ffc in range(4):
                ps, tps = C.bank()
                for kc in range(16):
                    P.op("pe", lambda e, ps=ps, kc=kc, s1=s1, ffc=ffc, s=s, n=n: e.matmul(
                        ps[:, :n], lhsT=wbuf[s1][:, kc, ffc * 128:(ffc + 1) * 128], rhs=actT[:, kc, s:s + n],
                        start=(kc == 0), stop=(kc == 15)), reads=[t_w[s1], t_act[ti]], writes=[tps])
                rb = ffc % 2
                P.op("act", lambda e, ps=ps, n=n, rb=rb: e.activation(out=r32[rb][:, :n], in_=ps[:, :n], func=AF.Relu),
                     reads=[tps], writes=[t_r32[rb]])
                P.op("dve", lambda e, n=n, rb=rb, ub=ub, ffc=ffc: e.tensor_tensor(
                    out=uT[ub][:, ffc, :n], in0=r32[rb][:, :n], in1=r32[rb][:, :n], op=ALU.mult),
                    reads=[t_r32[rb]], writes=[t_u[ub]])
            for fc in range(16):
                ps, tps = C.bank()
                for ffc in range(4):
                    P.op("pe", lambda e, ps=ps, ffc=ffc, fc=fc, ub=ub, n=n, w2v=w2v: e.matmul(
                        ps[:, :n], lhsT=w2v[:, ffc, fc * 128:(fc + 1) * 128], rhs=uT[ub][:, ffc, :n],
                        start=(ffc == 0), stop=(ffc == 3)), reads=[t_w[s2], t_u[ub]], writes=[tps])
                P.op("dve", lambda e, ps=ps, fc=fc, s=s, n=n, sel=sel: e.scalar_tensor_tensor(
                    out=xT[:, fc, s:s + n], in0=ps[:, :n], scalar=mod[:, 80 + fc, sel:sel + 1], in1=xT[:, fc, s:s + n],
                    op0=ALU.mult, op1=ALU.add), reads=[tps, t_c, t_xT[ti]], writes=[t_xT[ti]])

    xo = [wbuf[0].bitcast(F32).rearrange("p a b -> p (a b)")[:, 0:2048],
          wbuf[1].bitcast(F32).rearrange("p a b -> p (a b)")[:, 0:2048]]
    outs = []
    for tt in range(nt128):
        b = tt % 2
        ti = tile_of(tt * 128)
        for g in range(4):
            ps, tps = C.bank()
            for j in range(4):
                kc = g * 4 + j
                P.op("pe", lambda e, ps=ps, j=j, kc=kc, tt=tt: e.transpose(ps[:, j * 128:(j + 1) * 128], xT[:, kc, tt * 128:(tt + 1) * 128], ident),
                     reads=[t_xT[ti], t_c], writes=[tps])
            P.op("act", lambda e, ps=ps, g=g, b=b: e.activation(out=xo[b][:, g * 512:(g + 1) * 512], in_=ps[:, :], func=AF.Copy),
                 reads=[tps], writes=[t_w[b]])
        outs.append(P.dma("sp", xo_d[tt * 128:(tt + 1) * 128, :], xo[b], reads=[t_w[b]]))
    if not last:
        norm_mod(An, 0, modn)
        for ti, (s, n, sel) in enumerate(tiles):
            if "hn_ch" in io.given:
                for c in range(4):
                    outs.append(P.dma("sp", io.given["hn_ch"].loc_view(c)[:, :, s:s + n], actT[:, 4 * c:4 * c + 4, s:s + n], reads=[t_act[ti]]))
            else:
                outs.append(P.dma("sp", hn_d[:, :, s:s + n], actT[:, :, s:s + n], reads=[t_act[ti]]))
    if own:
        P.emit(final_wait=outs)
        return nc
    return outs


class QKFinish:
    def __init__(self, C, ones, t_c, rope=True):
        self.C = C
        self.ones = ones
        self.t_c = t_c
        self.qraw = [C.carve([128, 512], F32) for _ in range(2)]
        self.sq = [C.carve([128, 512], F32) for _ in range(2)]
        self.rs = [C.carve([128, 512], F32) for _ in range(2)]
        self.t1 = [C.carve([128, 512], F32) for _ in range(2)]
        self.t2 = [C.carve([128, 512], F32) for _ in range(2)] if rope else None
        self.tk = [[Tok() for _ in range(2)] for _ in range(5)]
        self.k = 0

    def rstd_from(self, ps_raw, tps, n, b, dim):
        C, P = self.C, self.C.P
        tq, tsq, trs = self.tk[0][b], self.tk[1][b], self.tk[2][b]
        P.op("act", lambda e: e.activation(out=self.qraw[b][:, :n], in_=ps_raw[:, :n], func=AF.Copy), reads=[tps], writes=[tq])
        P.op("act", lambda e: e.activation(out=self.sq[b][:, :n], in_=ps_raw[:, :n], func=AF.Square), reads=[tps], writes=[tsq])
        ps2, tps2 = C.bank()
        P.op("pe", lambda e: e.matmul(ps2[:, :n], lhsT=self.ones, rhs=self.sq[b][:, :n], start=True, stop=True),
             reads=[tsq, self.t_c], writes=[tps2])
        P.op("dve", lambda e: e.tensor_scalar(out=self.rs[b][:, :n], in0=ps2[:, :n], scalar1=1.0 / dim, scalar2=EPS,
                                              op0=ALU.mult, op1=ALU.add), reads=[tps2], writes=[trs])
        if getattr(self, "use_ln", False):
            P.op("act", lambda e: e.activation(out=self.rs[b][:, :n], in_=self.rs[b][:, :n], func=AF.Ln), reads=[trs], writes=[trs])
            P.op("act", lambda e: e.activation(out=self.rs[b][:, :n], in_=self.rs[b][:, :n], func=AF.Exp, scale=-0.5), reads=[trs], writes=[trs])
            return
        P.op("act", lambda e: e.activation(out=self.rs[b][:, :n], in_=self.rs[b][:, :n], func=AF.Sqrt), reads=[trs], writes=[trs])
        P.op("dve", lambda e: e.reciprocal(out=self.rs[b][:, :n], in_=self.rs[b][:, :n]), reads=[trs], writes=[trs])

    def finish(self, ps_raw, tps, n, out_ap, t_out, gcol, rope=None):
        C, P = self.C, self.C.P
        b = self.k % 2
        self.k += 1
        tq, trs, tt1, tt2 = self.tk[0][b], self.tk[2][b], self.tk[3][b], self.tk[4][b]
        self.rstd_from(ps_raw, tps, n, b, 128.0)
        if rope is None:
            P.op("dve", lambda e: e.scalar_tensor_tensor(out=out_ap, in0=self.qraw[b][:, :n], scalar=gcol, in1=self.rs[b][:, :n],
                                                         op0=ALU.mult, op1=ALU.mult), reads=[tq, trs, self.t_c], writes=[t_out])
            return
        ctab, stab, pswg = rope
        ps3, tps3 = C.bank()
        P.op("pe", lambda e: e.matmul(ps3[:, :n], lhsT=pswg, rhs=self.qraw[b][:, :n], start=True, stop=True),
             reads=[tq, self.t_c], writes=[tps3])
        P.op("dve", lambda e: e.scalar_tensor_tensor(out=self.t1[b][:, :n], in0=self.qraw[b][:, :n], scalar=gcol, in1=ctab,
                                                     op0=ALU.mult, op1=ALU.mult), reads=[tq, self.t_c], writes=[tt1])
        P.op("dve", lambda e: e.tensor_tensor(out=self.t2[b][:, :n], in0=ps3[:, :n], in1=stab, op=ALU.mult),
             reads=[tps3, self.t_c], writes=[tt2])
        P.op("pool", lambda e: e.tensor_tensor(out=self.t1[b][:, :n], in0=self.t1[b][:, :n], in1=self.t2[b][:, :n], op=ALU.add),
             reads=[tt1, tt2], writes=[tt1])
        P.op("pool", lambda e: e.tensor_tensor(out=out_ap, in0=self.t1[b][:, :n], in1=self.rs[b][:, :n], op=ALU.mult),
             reads=[tt1, trs], writes=[t_out])


S_LAT = 2048
S_CTX = 256
S_ALL = S_LAT + S_CTX
LAM_INIT1 = 0.8 - 0.6 * math.exp(-0.3 * 1)


def build_mix1(C=None, io=None):
    own = C is None
    if own:
        nc = bass.Bass("TRN2", target_bir_lowering=False)
        C = Ctx(nc)
        io = IO(nc)
    nc = C.nc
    C.reset()
    hT_d = io.inp("hT", [128, 16, S_ALL], BF16) if "h_all" not in io.given else None
    w_d = io.inp("w_in", [D, 4 * 768])
    gq_d = io.inp("gq", [128, 1])
    gk_d = io.inp("gk", [128, 1])
    lamp_d = io.inp("lamp", [128, 4])
    sg_d = io.inp("sublng", [128, 2])
    rc_d = io.inp("ropeC", [128, S_LAT])
    rs_d = io.inp("ropeS", [128, S_LAT])
    psw_d = io.inp("psw", [128, 128])
    cat_d = io.out("catT", [128, 8, S_LAT], BF16)
    P = C.P
    C.bank_lo = 3

    hT = C.carve([128, 16, S_ALL], BF16)
    wb = [C.carve([128, 16, 256], BF16) for _ in range(4)]
    QT = C.carve([128, 2, S_LAT], BF16)
    KT = C.carve([128, 2, S_ALL], BF16)
    V = C.carve([128, 18, 256], BF16)
    ropeC = C.carve([128, S_LAT], F32)
    ropeS = C.carve([128, S_LAT], F32)
    catT = [C.carve([128, 2, S_LAT], BF16) for _ in range(1)]
    PT = [C.carve([128, 512], BF16) for _ in range(4)]
    on0 = C.carve([128, 2, 512], F32)
    on1 = C.carve([128, 2, 512], F32)
    rec = C.carve([128, 512], F32)
    ones = C.carve([128, 128], F32)
    onesb = C.carve([128, 128], BF16)
    psw = C.carve([128, 128], F32)
    pswq = C.carve([128, 128], F32)
    pswk = C.carve([128, 128], F32)
    small = C.carve([128, 32], F32)
    gq, gk, lamp, sg, prod, ee, nlam, gqs = (small[:, 0:1], small[:, 1:2], small[:, 2:6], small[:, 6:8], small[:, 8:10],
                                             small[:, 10:12], small[:, 12:13], small[:, 13:14])
    t_c = Tok()
    F = QKFinish(C, ones, t_c)

    t_h = [Tok() for _ in range(5)]
    t_wb = [Tok() for _ in range(4)]
    t_Q = [[Tok() for _ in range(4)] for _ in range(2)]
    t_K = [[Tok() for _ in range(5)] for _ in range(2)]
    t_V = [Tok() for _ in range(18)]
    t_cat = [Tok() for _ in range(4)]
    t_PT = [Tok() for _ in range(4)]
    t_on0, t_on1, t_rec = Tok(), Tok(), Tok()
    acc = C.banks[0:3]
    t_acc = C.bank_tok[0:3]

    ltiles = [(s, 512) for s in range(0, S_LAT, 512)]
    atiles = ltiles + [(S_LAT, 256)]

    for src, dst in ((gq_d, gq), (gk_d, gk), (lamp_d, lamp), (sg_d, sg), (rc_d, ropeC), (rs_d, ropeS), (psw_d, psw)):
        P.dma("sp", dst, src, writes=[t_c])
    if "h_all" in io.given:
        h_all = io.given["h_all"]
        for r in range(2):
            for c in range(4):
                for k in range(2):
                    P.dma("sp", hT[:, 4 * c:4 * c + 4, r * 1024 + k * 512:r * 1024 + (k + 1) * 512], h_all.all_view(c)[r, :, :, k * 512:(k + 1) * 512],
                          writes=[t_h[2 * r + k]])
                P.dma("sp", hT[:, 4 * c:4 * c + 4, S_LAT + r * 128:S_LAT + (r + 1) * 128], h_all.all_view(c)[r, :, :, 1024:1152], writes=[t_h[4]])
    else:
        for ti, (s, n) in enumerate(atiles):
            P.dma("sp", hT[:, :, s:s + n], hT_d[:, :, s:s + n], writes=[t_h[ti]])
    P.op("dve", lambda e: e.memset(ones, 1.0), writes=[t_c])
    P.op("dve", lambda e: e.memset(onesb, 1.0), writes=[t_c])
    P.op("dve", lambda e: e.tensor_scalar(out=gqs, in0=gq, scalar1=128.0 ** -0.5, scalar2=None, op0=ALU.mult), reads=[t_c], writes=[t_c])
    P.op("dve", lambda e: e.tensor_scalar(out=pswq, in0=psw, scalar1=gqs, scalar2=None, op0=ALU.mult), reads=[t_c], writes=[t_c])
    P.op("dve", lambda e: e.tensor_scalar(out=pswk, in0=psw, scalar1=gk, scalar2=None, op0=ALU.mult), reads=[t_c], writes=[t_c])
    P.op("dve", lambda e: e.tensor_tensor(out=prod[:, 0:1], in0=lamp[:, 0:1], in1=lamp[:, 1:2], op=ALU.mult), reads=[t_c], writes=[t_c])
    P.op("dve", lambda e: e.tensor_tensor(out=prod[:, 1:2], in0=lamp[:, 2:3], in1=lamp[:, 3:4], op=ALU.mult), reads=[t_c], writes=[t_c])
    psl, tpsl = C.bank()
    P.op("pe", lambda e: e.matmul(psl[:, 0:2], lhsT=ones, rhs=prod, start=True, stop=True), reads=[t_c], writes=[tpsl])
    P.op("act", lambda e: e.activation(out=ee, in_=psl[:, 0:2], func=AF.Exp), reads=[tpsl], writes=[t_c])
    P.op("dve", lambda e: e.tensor_tensor(out=nlam, in0=ee[:, 1:2], in1=ee[:, 0:1], op=ALU.subtract), reads=[t_c], writes=[t_c])
    P.op("dve", lambda e: e.tensor_scalar(out=nlam, in0=nlam, scalar1=-LAM_INIT1, scalar2=None, op0=ALU.add), reads=[t_c], writes=[t_c])
    P.op("dve", lambda e: e.tensor_scalar(out=sg, in0=sg, scalar1=1.0 - LAM_INIT1, scalar2=None, op0=ALU.mult), reads=[t_c], writes=[t_c])

    wk = 0
    outs = []
    for hl in range(4):
        bq, bk, bv = (wk) % 4, (wk + 1) % 4, (wk + 2) % 4
        wk += 3
        for j, bb in enumerate((bq, bk, bv)):
            P.dma("pool", wb[bb], w_d[:, hl * 768 + j * 256: hl * 768 + (j + 1) * 256].rearrange("(kc p) n -> p kc n", p=128),
                  writes=[t_wb[bb]])
        fpend = None
        for (wbuf_i, dstT, t_dst, tl, gcol, pswg) in ((bq, QT, t_Q, ltiles, gqs, pswq), (bk, KT, t_K, atiles, gk, pswk)):
            for i in range(2):
                for ti, (s, n) in enumerate(tl):
                    ps, tps = C.bank()
                    for kc in range(16):
                        P.op("pe", lambda e, ps=ps, kc=kc, wbuf_i=wbuf_i, i=i, s=s, n=n: e.matmul(
                            ps[:, :n], lhsT=wb[wbuf_i][:, kc, i * 128:(i + 1) * 128], rhs=hT[:, kc, s:s + n],
                            start=(kc == 0), stop=(kc == 15)), reads=[t_wb[wbuf_i], t_h[ti]], writes=[tps])
                    rope = (ropeC[:, s:s + n], ropeS[:, s:s + n], pswg) if s < S_LAT else None
                    if fpend is not None:
                        F.finish(*fpend)
                    fpend = (ps, tps, n, dstT[:, i, s:s + n], t_dst[i][ti], gcol, rope)
        F.finish(*fpend)
        for tt in range(18):
            ps, tps = C.bank()
            for kc in range(16):
                P.op("pe", lambda e, ps=ps, kc=kc, tt=tt, bv=bv: e.matmul(
                    ps[:, 0:256], lhsT=hT[:, kc, tt * 128:(tt + 1) * 128], rhs=wb[bv][:, kc, :],
                    start=(kc == 0), stop=(kc == 15)), reads=[t_wb[bv], t_h[min(tt // 4, 4)]], writes=[tps])
            P.op("act", lambda e, ps=ps, tt=tt: e.activation(out=V[:, tt, :], in_=ps[:, 0:256], func=AF.Copy),
                 reads=[tps], writes=[t_V[tt]])
        pk = 0
        for qb, (qs, qn) in enumerate(ltiles):
            for i in range(2):
                pend = None
                for c in range(19):
                    if c < 18:
                        ps, tps = C.bank()
                        P.op("pe", lambda e, ps=ps, c=c, i=i, qs=qs: e.matmul(
                            ps[:, :], lhsT=KT[:, i, c * 128:(c + 1) * 128], rhs=QT[:, i, qs:qs + 512], start=True, stop=True),
                            reads=[t_K[i][min(c // 4, 4)], t_Q[i][qb]], writes=[tps])
                        pb = pk % 4
                        pk += 1
                        P.op("act", lambda e, ps=ps, pb=pb: e.activation(out=PT[pb], in_=ps[:, :], func=AF.Exp),
                             reads=[tps], writes=[t_PT[pb]])
                    if pend is not None:
                        pc, ppb = pend
                        for dvc in range(2):
                            P.op("pe", lambda e, pc=pc, ppb=ppb, dvc=dvc: e.matmul(
                                acc[dvc][:, :], lhsT=V[:, pc, dvc * 128:(dvc + 1) * 128], rhs=PT[ppb], start=(pc == 0), stop=(pc == 17)),
                                reads=[t_V[pc], t_PT[ppb]], writes=[t_acc[dvc]])
                        P.op("pe", lambda e, pc=pc, ppb=ppb: e.matmul(
                            acc[2][:, :], lhsT=onesb, rhs=PT[ppb], start=(pc == 0), stop=(pc == 17)),
                            reads=[t_PT[ppb], t_c], writes=[t_acc[2]])
                    pend = (c, pb) if c < 18 else None
                on, t_on = (on0, t_on0) if i == 0 else (on1, t_on1)
                P.op("dve", lambda e: e.reciprocal(out=rec, in_=acc[2][:, :]), reads=[t_acc[2]], writes=[t_rec])
                for dvc in range(2):
                    P.op("dve", lambda e, dvc=dvc, on=on: e.tensor_tensor(out=on[:, dvc, :], in0=acc[dvc][:, :], in1=rec, op=ALU.mult),
                         reads=[t_acc[dvc], t_rec], writes=[t_on])
            for dvc in range(2):
                P.op("dve", lambda e, dvc=dvc: e.scalar_tensor_tensor(out=on0[:, dvc, :], in0=on1[:, dvc, :], scalar=nlam, in1=on0[:, dvc, :],
                                                                       op0=ALU.mult, op1=ALU.add), reads=[t_on0, t_on1, t_c], writes=[t_on0])
            ps2, tps2 = C.bank()
            for dvc in range(2):
                b = F.k % 2
                F.k += 1
                P.op("act", lambda e, dvc=dvc, b=b: e.activation(out=F.sq[b], in_=on0[:, dvc, :], func=AF.Square),
                     reads=[t_on0], writes=[F.tk[1][b]])
                P.op("pe", lambda e, dvc=dvc, b=b, ps2=ps2: e.matmul(ps2[:, :], lhsT=ones, rhs=F.sq[b], start=(dvc == 0), stop=(dvc == 1)),
                     reads=[F.tk[1][b], t_c], writes=[tps2])
            P.op("dve", lambda e, ps2=ps2: e.tensor_scalar(out=rec, in0=ps2[:, :], scalar1=1.0 / 256.0, scalar2=EPS, op0=ALU.mult, op1=ALU.add),
                 reads=[tps2], writes=[t_rec])
            P.op("act", lambda e: e.activation(out=rec, in_=rec, func=AF.Sqrt), reads=[t_rec], writes=[t_rec])
            P.op("dve", lambda e: e.reciprocal(out=rec, in_=rec), reads=[t_rec], writes=[t_rec])
            for dvc in range(2):
                P.op("dve", lambda e, dvc=dvc, qs=qs: e.scalar_tensor_tensor(
                    out=catT[0][:, dvc, qs:qs + 512], in0=on0[:, dvc, :], scalar=sg[:, dvc:dvc + 1], in1=rec, op0=ALU.mult, op1=ALU.mult),
                    reads=[t_on0, t_rec, t_c], writes=[t_cat[qb]])
            if "cat_ch" in io.given:
                for dvc in range(2):
                    outs.append(P.dma("sp", io.given["cat_ch"].loc_j(hl * 2 + dvc, qs, qs + 512), catT[0][:, dvc, qs:qs + 512], reads=[t_cat[qb]]))
            else:
                outs.append(P.dma("sp", cat_d[:, hl * 2:(hl + 1) * 2, qs:qs + 512], catT[0][:, :, qs:qs + 512], reads=[t_cat[qb]]))
    if own:
        P.emit(final_wait=outs)
        return nc
    return outs


def rope_consts():
    t = np.arange(S_LAT)
    row = (t // 64).astype(np.float32)
    col = (t % 64).astype(np.float32)
    inv = (10000.0 ** (-np.arange(32, dtype=np.float32) / 32)).astype(np.float32)
    ang = np.concatenate([row[:, None] * inv, col[:, None] * inv], axis=-1).astype(np.float32)
    cos, sin = np.cos(ang).astype(np.float32), np.sin(ang).astype(np.float32)
    ropeC = np.ascontiguousarray(np.concatenate([cos, cos], 1).T)
    ropeS = np.ascontiguousarray(np.concatenate([-sin, sin], 1).T)
    psw = np.zeros((128, 128), np.float32)
    for m in range(128):
        psw[(m + 64) % 128, m] = 1.0
    return {"ropeC": ropeC, "ropeS": ropeS, "psw": psw}


def mix1_inputs(od_w_in, q_g, k_g, lam_p, subln_g, hh):
    cols = []
    for hl in range(4):
        hd = 4 * hh + hl
        cols += list(range(hd * 256, (hd + 1) * 256))
        cols += list(range(2048 + hd * 256, 2048 + (hd + 1) * 256))
        cols += list(range(4096 + hd * 256, 4096 + (hd + 1) * 256))
    return {
        "w_in": np.ascontiguousarray(od_w_in[:, cols]),
        "gq": np.ascontiguousarray(q_g.reshape(128, 1)),
        "gk": np.ascontiguousarray(k_g.reshape(128, 1)),
        "lamp": np.ascontiguousarray(lam_p.T),
        "sublng": np.ascontiguousarray(subln_g.reshape(2, 128).T),
    }


def na_r0(r):
    return min(max(r - 4, 0), 24)


def build_mix0(C=None, io=None):
    own = C is None
    if own:
        nc = bass.Bass("TRN2", target_bir_lowering=False)
        C = Ctx(nc)
        io = IO(nc)
    nc = C.nc
    C.reset()
    x_d = io.inp("x_all", [S_ALL, D])
    mod_d = io.inp("modT", [128, 96, 2])
    gmix_d = io.inp("g_mix", [128, 16])
    w_d = io.inp("w_in", [D, 4096])
    gq_d = io.inp("gq", [128, 1])
    gk_d = io.inp("gk", [128, 1])
    gn_d = io.inp("gn", [128, 1])
    rpbg_d = io.inp("rpbG", [64, 4, 960])
    cm_d = io.inp("colmask", [128, 960])
    lbl_d = io.inp("lbl", [1, 2 * 3 * 512])
    cst_d = io.inp("hgc", [128, 7, 128])
    cat_d = io.out("catT", [128, 8, S_ALL], BF16)
    P = C.P
    C.bank_lo = 2

    hT = C.carve([128, 16, S_ALL], BF16)
    wb = [C.carve([128, 16, 128], BF16) for _ in range(4)]
    wbig = [C.carve([128, 16, 512], BF16) for _ in range(1)]
    ones = C.carve([128, 128], F32)
    onesb = C.carve([128, 128], BF16)
    identb = C.carve([128, 128], BF16)
    cst = C.carve([128, 7, 128], F32)
    cmaskb = C.carve([128, 8, 128], BF16)
    mod_given = "mod_sb" in io.given
    mod = io.given["mod_sb"] if mod_given else C.carve([128, 96, 2], F32)
    gmix = C.carve([128, 16], F32)
    A1 = C.carve([128, 16, 2], F32)
    small = C.carve([128, 16], F32)
    gq, gk, gn, gqs = small[:, 0:1], small[:, 1:2], small[:, 2:3], small[:, 3:4]
    ss, rs1 = small[:, 4:5], small[:, 5:6]
    lb = C.carve([128, 2, 512], F32)
    oml = C.carve([128, 2, 512], F32)
    t_c = Tok()
    F = QKFinish(C, ones, t_c, rope=False)
    F.use_ln = True
    xs_off = C.off
    xs = [C.carve([128, 2048], F32) for _ in range(2)]
    sqx = C.carve([128, 2048], F32)
    na_off = C.off
    QT = C.carve([128, S_ALL], BF16)
    KT = C.carve([128, S_ALL], BF16)
    V = C.carve([128, 18, 128], BF16)
    G32 = C.carve([128, 960], F32)
    CM = C.carve([128, 960], BF16)
    CM32 = G32
    E = C.carve([128, 960], BF16)
    PT = [C.carve([128, 512], BF16) for _ in range(3)]
    rec = C.carve([128, 512], F32)
    catb = [C.carve([128, 512], BF16) for _ in range(2)]
    zf_all = C.carve([128, 18, 128], F32, at=xs_off)
    zb_all = C.carve([128, 18, 128], F32, at=xs_off + 9216)
    qs_all = C.carve([128, 18, 128], BF16, at=xs_off + 18432)
    oacc_off = C.off
    oacc = C.carve([128, S_ALL], F32)
    sgb = C.carve([128, S_ALL], BF16)
    lbraw = C.carve([128, 2, 3, 512], F32, at=oacc_off)
    hoff = [na_off]

    def hcarve(shape, dtype):
        ap = C.carve(shape, dtype, at=hoff[0])
        hoff[0] += int(np.prod(shape[1:])) * (2 if dtype == BF16 else 4)
        return ap
    v_all = hcarve([128, 18, 128], BF16)
    NS = 8
    CH = []
    for ch in range(2):
        B = dict(scr=[hcarve([128, 128], F32) for _ in range(NS)], scrb=[hcarve([128, 128], BF16) for _ in range(8)],
                 S32=[hcarve([128, 128], F32) for _ in range(2)], gch=[hcarve([128, 8], F32) for _ in range(2)])
        if ch == 0:
            B["vexp"] = hcarve([128, 8, 128], BF16)
            B["Sb"] = hcarve([128, 8, 128], BF16)
        else:
            B["vexp"] = F.qraw[0].bitcast(BF16).rearrange("p (a b) -> p a b", a=8)
            B["Sb"] = F.qraw[1].bitcast(BF16).rearrange("p (a b) -> p a b", a=8)
        B.update(t_scr=[Tok() for _ in range(NS)], t_scrb=[Tok() for _ in range(8)], t_vexp=Tok(), t_Sb=Tok(),
                 t_S32=[Tok(), Tok()], t_gch=[Tok(), Tok()], sk=0, sbk=0,
                 bs=(C.banks[3 * ch], C.bank_tok[3 * ch]), bu0=(C.banks[3 * ch + 1], C.bank_tok[3 * ch + 1]),
                 bu1=(C.banks[3 * ch + 2], C.bank_tok[3 * ch + 2]))
        CH.append(B)
    assert hoff[0] <= na_off + 3 * 4608 + 3840 + 1920 + 1920 + 3072, hoff[0] - na_off
    t_zf = [Tok() for _ in range(18)]

    t_h = [Tok() for _ in range(18)]
    t_wb = [Tok() for _ in range(4)]
    t_wbig = [Tok()]
    t_x = [Tok(), Tok()]
    t_sqx, t_small = Tok(), Tok()
    t_Q, t_K = [Tok() for _ in range(5)], [Tok() for _ in range(5)]
    t_V = [Tok() for _ in range(18)]
    t_E, t_G = Tok(), Tok()
    t_PT = [Tok() for _ in range(3)]
    t_rec = Tok()
    t_catb = [Tok(), Tok()]
    t_qs, t_va, t_zb = [Tok() for _ in range(18)], [Tok() for _ in range(18)], [Tok() for _ in range(18)]
    t_oacc = [Tok() for _ in range(18)]
    t_sgb = [Tok() for _ in range(5)]
    acc = C.banks[0:2]
    t_acc = C.bank_tok[0:2]
    atiles = [(s, 512) for s in range(0, S_LAT, 512)] + [(S_LAT, 256)]
    outs = []

    for src, dst in ((mod_d, mod), (gmix_d, gmix), (gq_d, gq), (gk_d, gk), (gn_d, gn), (cst_d, cst), (cm_d, CM32)):
        if mod_given and dst is mod:
            continue
        P.dma("sp", dst, src, writes=[t_c])
    P.dma("sp", lbraw.rearrange("p a b c -> p (a b c)"), lbl_d.partition_broadcast(128), writes=[t_c])
    P.op("dve", lambda e: e.memset(ones, 1.0), writes=[t_c])
    P.op("dve", lambda e: e.memset(onesb, 1.0), writes=[t_c])
    P.op("dve", lambda e: e.tensor_copy(out=CM, in_=CM32), reads=[t_c], writes=[t_c, t_G])
    P.op("dve", lambda e: e.tensor_copy(out=identb, in_=cst[:, 5, :]), reads=[t_c], writes=[t_c])
    for c8 in range(8):
        P.op("dve", lambda e, c8=c8: e.tensor_scalar(out=cmaskb[:, c8, :], in0=ones, scalar1=cst[:, 6, c8:c8 + 1], scalar2=None, op0=ALU.mult),
             reads=[t_c], writes=[t_c])
    P.op("dve", lambda e: e.tensor_scalar(out=gqs, in0=gq, scalar1=128.0 ** -0.5, scalar2=None, op0=ALU.mult), reads=[t_c], writes=[t_c])
    for sel in range(2):
        P.op("dve", lambda e, sel=sel: e.scalar_tensor_tensor(out=A1[:, :, sel], in0=mod[:, 16:32, sel], scalar=1.0, in1=gmix,
                                                             op0=ALU.add, op1=ALU.mult), reads=[t_c], writes=[t_c])
    P.op("act", lambda e: e.activation(out=lbraw.rearrange("p a b c -> p (a b c)"), in_=lbraw.rearrange("p a b c -> p (a b c)"), func=AF.Exp),
         reads=[t_c], writes=[t_c])
    P.op("dve", lambda e: e.tensor_tensor(out=oml, in0=lbraw[:, :, 1, :], in1=lbraw[:, :, 2, :], op=ALU.add), reads=[t_c], writes=[t_c])
    P.op("dve", lambda e: e.tensor_tensor(out=lb, in0=oml, in1=lbraw[:, :, 0, :], op=ALU.add), reads=[t_c], writes=[t_c])
    P.op("dve", lambda e: e.reciprocal(out=lb, in_=lb), reads=[t_c], writes=[t_c])
    P.op("dve", lambda e: e.tensor_tensor(out=oml, in0=oml, in1=lb, op=ALU.mult), reads=[t_c], writes=[t_c])
    P.op("dve", lambda e: e.tensor_tensor(out=lb, in0=lbraw[:, :, 0, :], in1=lb, op=ALU.mult), reads=[t_c], writes=[t_c])

    for tt in range(18):
        b = tt % 2
        sel = 0 if tt < 16 else 1
        P.dma("sp", xs[b], x_d[tt * 128:(tt + 1) * 128, :], writes=[t_x[b]])
        P.op("act", lambda e, b=b: e.activation(out=sqx, in_=xs[b], func=AF.Square), reads=[t_x[b]], writes=[t_sqx])
        P.op("dve", lambda e: e.reduce_sum(out=ss, in_=sqx, axis=mybir.AxisListType.X), reads=[t_sqx], writes=[t_small])
        P.op("dve", lambda e: e.tensor_scalar(out=rs1, in0=ss, scalar1=1.0 / D, scalar2=EPS, op0=ALU.mult, op1=ALU.add),
             reads=[t_small], writes=[t_small])
        P.op("act", lambda e: e.activation(out=rs1, in_=rs1, func=AF.Ln), reads=[t_small], writes=[t_small])
        P.op("act", lambda e: e.activation(out=rs1, in_=rs1, func=AF.Exp, scale=-0.5), reads=[t_small], writes=[t_small])
        P.op("dve", lambda e, b=b: e.tensor_scalar(out=xs[b], in0=xs[b], scalar1=rs1, scalar2=None, op0=ALU.mult),
             reads=[t_x[b], t_small], writes=[t_x[b]])
        for g in range(4):
            ps, tps = C.bank()
            for j in range(4):
                kc = g * 4 + j
                P.op("pe", lambda e, ps=ps, j=j, kc=kc, b=b: e.transpose(ps[:, j * 128:(j + 1) * 128], xs[b][:, kc * 128:(kc + 1) * 128], cst[:, 5, :]),
                     reads=[t_x[b], t_c], writes=[tps])
            for j in range(4):
                kc = g * 4 + j
                eng = "act" if j % 2 == 0 else "dve"
                if eng == "act":
                    P.op("act", lambda e, ps=ps, j=j, kc=kc, tt=tt, sel=sel: e.activation(
                        out=hT[:, kc, tt * 128:(tt + 1) * 128], in_=ps[:, j * 128:(j + 1) * 128], func=AF.Identity,
                        scale=A1[:, kc, sel:sel + 1], bias=mod[:, kc, sel:sel + 1]), reads=[tps, t_c], writes=[t_h[tt]])
                else:
                    P.op("dve", lambda e, ps=ps, j=j, kc=kc, tt=tt, sel=sel: e.tensor_scalar(
                        out=hT[:, kc, tt * 128:(tt + 1) * 128], in0=ps[:, j * 128:(j + 1) * 128],
                        scalar1=A1[:, kc, sel:sel + 1], scalar2=mod[:, kc, sel:sel + 1], op0=ALU.mult, op1=ALU.add),
                        reads=[tps, t_c], writes=[t_h[tt]])

    wk = [0]

    def load_cols(c0):
        i = wk[0] % 4
        wk[0] += 1
        P.dma("pool", wb[i], w_d[:, c0:c0 + 128].rearrange("(kc p) n -> p kc n", p=128), writes=[t_wb[i]])
        return i

    def proj_fm(wi, s, n):
        ps, tps = C.bank()
        for kc in range(16):
            P.op("pe", lambda e, kc=kc: e.matmul(ps[:, :n], lhsT=wb[wi][:, kc, :], rhs=hT[:, kc, s:s + n], start=(kc == 0), stop=(kc == 15)),
                 reads=[t_wb[wi]] + t_h[s // 128:(s + n) // 128], writes=[tps])
        return ps, tps

    for hl in range(4):
        wq, wkk, wv = load_cols(hl * 128), load_cols(512 + hl * 128), load_cols(1024 + hl * 128)
        P.dma("sp", G32[0:64, :], rpbg_d[:, hl, :], writes=[t_G])
        P.dma("sp", G32[64:128, :], rpbg_d[:, hl, :], writes=[t_G])
        P.op("act", lambda e: e.activation(out=G32, in_=G32, func=AF.Exp), reads=[t_G], writes=[t_G])
        P.op("dve", lambda e: e.tensor_tensor(out=E, in0=G32, in1=CM, op=ALU.mult), reads=[t_G, t_c], writes=[t_E])
        fpend = None
        for (wi, dstT, t_dst, gcol) in ((wq, QT, t_Q, gqs), (wkk, KT, t_K, gk)):
            for ti, (s, n) in enumerate(atiles):
                ps, tps = proj_fm(wi, s, n)
                if fpend is not None:
                    F.finish(*fpend)
                fpend = (ps, tps, n, dstT[:, s:s + n], t_dst[ti], gcol, None)
        F.finish(*fpend)
        for tt in range(18):
            ps, tps = C.bank()
            for kc in range(16):
                P.op("pe", lambda e, ps=ps, kc=kc, tt=tt, wv=wv: e.matmul(ps[:, 0:128], lhsT=hT[:, kc, tt * 128:(tt + 1) * 128], rhs=wb[wv][:, kc, :],
                                                                 start=(kc == 0), stop=(kc == 15)), reads=[t_wb[wv], t_h[tt]], writes=[tps])
            P.op("act", lambda e, ps=ps, tt=tt: e.activation(out=V[:, tt, :], in_=ps[:, 0:128], func=AF.Copy), reads=[tps], writes=[t_V[tt]])
        pk = 0
        for g in range(5):
            qs0, qn = atiles[g]
            if g < 4:
                lo = na_r0(8 * g)
                hi = na_r0(8 * g + 7) + 7
                chunks = [("lat", c) for c in range(lo // 2, hi // 2 + 1)] + [("ctx", 16), ("ctx", 17)]
            else:
                chunks = [("ctx", 16), ("ctx", 17)]
            pend = None

            def na_back(ci, c, pb, qn, nch):
                first, lastc = ci == 0, ci == nch - 1
                P.op("pe", lambda e: e.matmul(acc[0][:, :qn], lhsT=V[:, c, :], rhs=PT[pb][:, :qn], start=first, stop=lastc),
                     reads=[t_V[c], t_PT[pb]], writes=[t_acc[0]])
                P.op("pe", lambda e: e.matmul(acc[1][:, :qn], lhsT=onesb, rhs=PT[pb][:, :qn], start=first, stop=lastc),
                     reads=[t_PT[pb], t_c], writes=[t_acc[1]])
            for ci, (kind, c) in enumerate(chunks):
                ps, tps = C.bank()
                P.op("pe", lambda e, ps=ps, c=c, qs0=qs0, qn=qn: e.matmul(ps[:, :qn], lhsT=KT[:, c * 128:(c + 1) * 128], rhs=QT[:, qs0:qs0 + qn],
                                                                        start=True, stop=True), reads=[t_K[min(c // 4, 4)], t_Q[g]], writes=[tps])
                pb = pk % 3
                pk += 1
                P.op("act", lambda e, ps=ps, pb=pb, qn=qn: e.activation(out=PT[pb][:, :qn], in_=ps[:, :qn], func=AF.Exp), reads=[tps], writes=[t_PT[pb]])
                if kind == "lat":
                    for half in range(2):
                        kr = 2 * c + half
                        valid = [r8 for r8 in range(8) if na_r0(8 * g + r8) <= kr <= na_r0(8 * g + r8) + 7]
                        rows = slice(half * 64, (half + 1) * 64)
                        if not valid:
                            P.op("pool", lambda e, pb=pb, rows=rows: e.memset(PT[pb][rows, :], 0.0), reads=[t_PT[pb]], writes=[t_PT[pb]])
                            continue
                        a, bnd = valid[0], valid[-1] + 1
                        off = 7 + 8 * g - kr
                        P.op("dve", lambda e, pb=pb, rows=rows, a=a, bnd=bnd, off=off: e.tensor_tensor(
                            out=PT[pb][rows, a * 64:bnd * 64], in0=PT[pb][rows, a * 64:bnd * 64], in1=E[rows, (a + off) * 64:(bnd + off) * 64],
                            op=ALU.mult), reads=[t_PT[pb], t_E], writes=[t_PT[pb]])
                        if a > 0:
                            P.op("pool", lambda e, pb=pb, rows=rows, a=a: e.memset(PT[pb][rows, 0:a * 64], 0.0), reads=[t_PT[pb]], writes=[t_PT[pb]])
                        if bnd < 8:
                            P.op("pool", lambda e, pb=pb, rows=rows, bnd=bnd: e.memset(PT[pb][rows, bnd * 64:512], 0.0), reads=[t_PT[pb]], writes=[t_PT[pb]])
                if pend is not None:
                    na_back(*pend)
                pend = (ci, c, pb, qn, len(chunks))
            na_back(*pend)
            P.op("dve", lambda e, qn=qn: e.reciprocal(out=rec[:, :qn], in_=acc[1][:, :qn]), reads=[t_acc[1]], writes=[t_rec])
            cb = g % 2
            P.op("dve", lambda e, qn=qn, cb=cb: e.tensor_tensor(out=catb[cb][:, :qn], in0=acc[0][:, :qn], in1=rec[:, :qn], op=ALU.mult),
                 reads=[t_acc[0], t_rec], writes=[t_catb[cb]])
            dst = io.given["cat_ch"].loc_j(hl, qs0, qs0 + qn) if "cat_ch" in io.given else cat_d[:, hl, qs0:qs0 + qn]
            outs.append(P.dma("sp", dst, catb[cb][:, :qn], reads=[t_catb[cb]]))

    C.bank_lo = 6
    C.bank_i = 6

    def hg_chain(hl, d, B):
        hcol = slice(hl * 128, (hl + 1) * 128)
        scr, scrb, S32, gch, vexp, Sb = B["scr"], B["scrb"], B["S32"], B["gch"], B["vexp"], B["Sb"]
        t_scr, t_scrb, t_S32, t_gch, t_vexp, t_Sb = B["t_scr"], B["t_scrb"], B["t_S32"], B["t_gch"], B["t_vexp"], B["t_Sb"]
        (bs, tbs), (bu0, tbu0), (bu1, tbu1) = B["bs"], B["bu0"], B["bu1"]
        bsb = bs[:, :].bitcast(BF16)

        def S():
            i = B["sk"] % NS
            B["sk"] += 1
            return i

        def SBn():
            i = B["sbk"] % 8
            B["sbk"] += 1
            return i
        order = [16, 17] + list(range(16)) if d == 0 else [17, 16] + list(range(15, -1, -1))
        tri, dmat = (cst[:, 0, :], cst[:, 3, :]) if d == 0 else (cst[:, 1, :], cst[:, 4, :])
        scur = 0
        P.op("pool", lambda e: e.memset(S32[0], 0.0), reads=[t_zf[order[0]], t_zb[order[0]]], writes=[t_S32[0]])
        for it, tt in enumerate(order):
            z_ap, t_z = (zf_all[:, tt, :], t_zf[tt]) if d == 0 else (zb_all[:, tt, :], t_zb[tt])
            i1, i2, i3 = S(), S(), S()
            P.op("act", lambda e, z_ap=z_ap, i1=i1: e.activation(out=scr[i1], in_=z_ap, func=AF.Exp, scale=-1.0), reads=[t_z], writes=[t_scr[i1]])
            yield
            P.op("dve", lambda e, i1=i1: e.tensor_scalar(out=scr[i1], in0=scr[i1], scalar1=1.0, scalar2=None, op0=ALU.add), reads=[t_scr[i1]], writes=[t_scr[i1]])
            P.op("dve", lambda e, i1=i1: e.reciprocal(out=scr[i1], in_=scr[i1]), reads=[t_scr[i1]], writes=[t_scr[i1]])
            P.op("dve", lambda e, i1=i1: e.tensor_tensor(out=scr[i1], in0=scr[i1], in1=oml[:, d, hcol], op=ALU.mult), reads=[t_scr[i1], t_c], writes=[t_scr[i1]])
            P.op("dve", lambda e, i1=i1: e.tensor_tensor(out=scr[i1], in0=scr[i1], in1=lb[:, d, hcol], op=ALU.add), reads=[t_scr[i1], t_c], writes=[t_scr[i1]])
            yield
            P.op("act", lambda e, i1=i1, i2=i2: e.activation(out=scr[i2], in_=scr[i1], func=AF.Ln), reads=[t_scr[i1]], writes=[t_scr[i2]])
            P.op("pool", lambda e, i1=i1, i3=i3: e.tensor_scalar(out=scr[i3], in0=scr[i1], scalar1=-1.0, scalar2=1.0, op0=ALU.mult, op1=ALU.add),
                 reads=[t_scr[i1]], writes=[t_scr[i3]])
            yield
            P.op("pe", lambda e, i2=i2: e.matmul(bs[:, 0:128], lhsT=tri, rhs=scr[i2], start=True, stop=True), reads=[t_scr[i2], t_c], writes=[tbs])
            P.op("pe", lambda e, i2=i2: e.matmul(bs[:, 128:256], lhsT=dmat, rhs=scr[i2], start=True, stop=True), reads=[t_scr[i2], t_c], writes=[tbs])
            P.op("pe", lambda e, i2=i2: e.matmul(bs[:, 256:264], lhsT=scr[i2], rhs=cst[:, 6, 0:8], start=True, stop=True), reads=[t_scr[i2], t_c], writes=[tbs])
            yield
            gb_ = it % 2
            i4, i5, i6 = S(), S(), S()
            P.op("act", lambda e, gb_=gb_: e.activation(out=gch[gb_], in_=bs[:, 256:264], func=AF.Exp), reads=[tbs], writes=[t_gch[gb_]])
            P.op("act", lambda e, i4=i4: e.activation(out=scr[i4], in_=bs[:, 0:128], func=AF.Exp), reads=[tbs], writes=[t_scr[i4]])
            P.op("act", lambda e, i5=i5: e.activation(out=scr[i5], in_=bs[:, 0:128], func=AF.Exp, scale=-1.0), reads=[tbs], writes=[t_scr[i5]])
            P.op("act", lambda e, i6=i6: e.activation(out=scr[i6], in_=bs[:, 128:256], func=AF.Exp), reads=[tbs], writes=[t_scr[i6]])
            yield
            bq, bk, bh = SBn(), SBn(), SBn()
            P.op("dve", lambda e, bq=bq, i4=i4, tt=tt: e.tensor_tensor(out=scrb[bq], in0=qs_all[:, tt, :], in1=scr[i4], op=ALU.mult),
                 reads=[t_qs[tt], t_scr[i4]], writes=[t_scrb[bq]])
            P.op("pool", lambda e, bk=bk, i3=i3, i5=i5: e.tensor_tensor(out=scrb[bk], in0=scr[i3], in1=scr[i5], op=ALU.mult),
                 reads=[t_scr[i3], t_scr[i5]], writes=[t_scrb[bk]])
            P.op("pool", lambda e, bh=bh, i3=i3, i6=i6: e.tensor_tensor(out=scrb[bh], in0=scr[i3], in1=scr[i6], op=ALU.mult),
                 reads=[t_scr[i3], t_scr[i6]], writes=[t_scrb[bh]])
            P.op("pool", lambda e, tt=tt: e.tensor_tensor(out=vexp, in0=cmaskb, in1=v_all[:, tt, :].unsqueeze(1).to_broadcast([128, 8, 128]), op=ALU.mult),
                 reads=[t_va[tt], t_c], writes=[t_vexp])
            yield
            P.op("pe", lambda e, bq=bq: e.transpose(bsb[:, 0:128], scrb[bq], identb), reads=[t_scrb[bq], t_c], writes=[tbs])
            P.op("pe", lambda e, bk=bk: e.transpose(bsb[:, 128:256], scrb[bk], identb), reads=[t_scrb[bk], t_c], writes=[tbs])
            vflat = vexp.rearrange("p a b -> p (a b)")
            P.op("pe", lambda e, bh=bh: e.matmul(bu0[:, :], lhsT=scrb[bh], rhs=vflat[:, 0:512], start=True, stop=True), reads=[t_scrb[bh], t_vexp], writes=[tbu0])
            P.op("pe", lambda e, bh=bh: e.matmul(bu1[:, :], lhsT=scrb[bh], rhs=vflat[:, 512:1024], start=True, stop=True), reads=[t_scrb[bh], t_vexp], writes=[tbu1])
            yield
            bqT, bkT = SBn(), SBn()
            P.op("act", lambda e, bqT=bqT: e.activation(out=scrb[bqT], in_=bsb[:, 0:128], func=AF.Copy), reads=[tbs], writes=[t_scrb[bqT]])
            P.op("act", lambda e, bkT=bkT: e.activation(out=scrb[bkT], in_=bsb[:, 128:256], func=AF.Copy), reads=[tbs], writes=[t_scrb[bkT]])
            yield
            P.op("pe", lambda e, bkT=bkT, bqT=bqT: e.matmul(bs[:, 0:128], lhsT=scrb[bkT], rhs=scrb[bqT], start=True, stop=True),
                 reads=[t_scrb[bkT], t_scrb[bqT]], writes=[tbs])
            yield
            bA = SBn()
            P.op("dve", lambda e, bA=bA: e.tensor_tensor(out=scrb[bA], in0=bs[:, 0:128], in1=tri, op=ALU.mult), reads=[tbs, t_c], writes=[t_scrb[bA]])
            yield
            corder = list(range(8)) if d == 0 else list(range(7, -1, -1))
            for c8 in corder:
                pu, tpu = (bu0, tbu0) if c8 < 4 else (bu1, tbu1)
                P.op("act", lambda e, c8=c8, scur=scur: e.activation(out=Sb[:, c8, :], in_=S32[scur], func=AF.Copy), reads=[t_S32[scur]], writes=[t_Sb])
                nxt = 1 - scur
                P.op("dve", lambda e, scur=scur, nxt=nxt, gb_=gb_, c8=c8, pu=pu: e.scalar_tensor_tensor(
                    out=S32[nxt], in0=S32[scur], scalar=gch[gb_][:, c8:c8 + 1], in1=pu[:, (c8 % 4) * 128:(c8 % 4 + 1) * 128],
                    op0=ALU.mult, op1=ALU.add), reads=[t_S32[scur], t_gch[gb_], tpu], writes=[t_S32[nxt]])
                scur = nxt
                yield
            P.op("pe", lambda e, tt=tt, bA=bA: e.matmul(bs[:, 0:128], lhsT=v_all[:, tt, :], rhs=scrb[bA], start=True, stop=False),
                 reads=[t_va[tt], t_scrb[bA]], writes=[tbs])
            for c8 in range(8):
                P.op("pe", lambda e, c8=c8, bqT=bqT: e.matmul(bs[:, c8 * 16:(c8 + 1) * 16], lhsT=Sb[:, c8, :], rhs=scrb[bqT][:, c8 * 16:(c8 + 1) * 16],
                                                               start=False, stop=(c8 == 7)), reads=[t_Sb, t_scrb[bqT]], writes=[tbs])
            yield
            P.op("dve", lambda e, tt=tt: e.tensor_tensor(out=oacc[:, tt * 128:(tt + 1) * 128], in0=bs[:, 0:128], in1=oacc[:, tt * 128:(tt + 1) * 128], op=ALU.add),
                 reads=[tbs, t_oacc[tt]], writes=[t_oacc[tt]])
            yield

    for hl in range(4):
        base = 1536 + hl * 640
        P.dma("pool", wbig[0], w_d[:, base:base + 512].rearrange("(kc p) n -> p kc n", p=128), writes=[t_wbig[0]])
        wg = load_cols(base + 512)
        for ti, (s, n) in enumerate(atiles):
            ps, tps = proj_fm(wg, s, n)
            tb = ti % 2
            tmp, ttmp = F.t1[tb], F.tk[3][tb]
            P.op("act", lambda e, ps=ps, n=n, tmp=tmp: e.activation(out=tmp[:, :n], in_=ps[:, :n], func=AF.Exp, scale=-1.0), reads=[tps], writes=[ttmp])
            P.op("dve", lambda e, n=n, tmp=tmp: e.tensor_scalar(out=tmp[:, :n], in0=tmp[:, :n], scalar1=1.0, scalar2=None, op0=ALU.add),
                 reads=[ttmp], writes=[ttmp])
            P.op("dve", lambda e, n=n, tmp=tmp: e.reciprocal(out=tmp[:, :n], in_=tmp[:, :n]), reads=[ttmp], writes=[ttmp])
            P.op("dve", lambda e, ps=ps, s=s, n=n, tmp=tmp: e.tensor_tensor(out=sgb[:, s:s + n], in0=ps[:, :n], in1=tmp[:, :n], op=ALU.mult),
                 reads=[tps, ttmp], writes=[t_sgb[ti]])
        P.op("pool", lambda e: e.memset(oacc, 0.0), reads=[], writes=t_oacc)
        for tt in range(18):
            ps, tps = C.bank()
            for kc in range(16):
                P.op("pe", lambda e, ps=ps, kc=kc, tt=tt: e.matmul(ps[:, :], lhsT=hT[:, kc, tt * 128:(tt + 1) * 128], rhs=wbig[0][:, kc, :],
                                                                 start=(kc == 0), stop=(kc == 15)), reads=[t_wbig[0], t_h[tt]], writes=[tps])
            tb = tt % 2
            tmp, ttmp = F.rs[tb], F.tk[2][tb]
            P.op("act", lambda e, ps=ps, tmp=tmp: e.activation(out=tmp[:, 0:128], in_=ps[:, 0:128], func=AF.Exp, scale=-1.0), reads=[tps], writes=[ttmp])
            P.op("dve", lambda e, tmp=tmp: e.tensor_scalar(out=tmp[:, 0:128], in0=tmp[:, 0:128], scalar1=1.0, scalar2=None, op0=ALU.add), reads=[ttmp], writes=[ttmp])
            P.op("dve", lambda e, tmp=tmp: e.reciprocal(out=tmp[:, 0:128], in_=tmp[:, 0:128]), reads=[ttmp], writes=[ttmp])
            P.op("dve", lambda e, ps=ps, tmp=tmp, tt=tt: e.tensor_tensor(out=qs_all[:, tt, :], in0=ps[:, 0:128], in1=tmp[:, 0:128], op=ALU.mult),
                 reads=[tps, ttmp], writes=[t_qs[tt]])
            P.op("act", lambda e, ps=ps, tt=tt: e.activation(out=zf_all[:, tt, :], in_=ps[:, 128:256], func=AF.Copy), reads=[tps], writes=[t_zf[tt]])
            P.op("act", lambda e, ps=ps, tt=tt: e.activation(out=zb_all[:, tt, :], in_=ps[:, 256:384], func=AF.Copy), reads=[tps], writes=[t_zb[tt]])
            P.op("act", lambda e, ps=ps, tt=tt: e.activation(out=v_all[:, tt, :], in_=ps[:, 384:512], func=AF.Copy), reads=[tps], writes=[t_va[tt]])
        gens = [hg_chain(hl, 0, CH[0]), hg_chain(hl, 1, CH[1])]
        while gens:
            for g in list(gens):
                try:
                    next(g)
                except StopIteration:
                    gens.remove(g)
        for ti, (s, n) in enumerate(atiles):
            b = F.k % 2
            F.k += 1
            P.op("act", lambda e, s=s, n=n, b=b: e.activation(out=F.sq[b][:, :n], in_=oacc[:, s:s + n], func=AF.Square), reads=t_oacc, writes=[F.tk[1][b]])
            ps2, tps2 = C.bank()
            P.op("pe", lambda e, ps2=ps2, n=n, b=b: e.matmul(ps2[:, :n], lhsT=ones, rhs=F.sq[b][:, :n], start=True, stop=True), reads=[F.tk[1][b], t_c], writes=[tps2])
            P.op("dve", lambda e, ps2=ps2, n=n: e.tensor_scalar(out=rec[:, :n], in0=ps2[:, :n], scalar1=1.0 / 128.0, scalar2=EPS, op0=ALU.mult, op1=ALU.add),
                 reads=[tps2], writes=[t_rec])
            P.op("act", lambda e, n=n: e.activation(out=rec[:, :n], in_=rec[:, :n], func=AF.Ln), reads=[t_rec], writes=[t_rec])
            P.op("act", lambda e, n=n: e.activation(out=rec[:, :n], in_=rec[:, :n], func=AF.Exp, scale=-0.5), reads=[t_rec], writes=[t_rec])
            P.op("dve", lambda e, s=s, n=n: e.scalar_tensor_tensor(out=rec[:, :n], in0=oacc[:, s:s + n], scalar=gn, in1=rec[:, :n], op0=ALU.mult, op1=ALU.mult),
                 reads=t_oacc + [t_rec, t_c], writes=[t_rec])
            cb = ti % 2
            P.op("dve", lambda e, s=s, n=n, cb=cb: e.tensor_tensor(out=catb[cb][:, :n], in0=rec[:, :n], in1=sgb[:, s:s + n], op=ALU.mult),
                 reads=[t_rec, t_sgb[ti]], writes=[t_catb[cb]])
            dst = io.given["cat_ch"].loc_j(4 + hl, s, s + n) if "cat_ch" in io.given else cat_d[:, 4 + hl, s:s + n]
            outs.append(P.dma("sp", dst, catb[cb][:, :n], reads=[t_catb[cb]]))
    if own:
        P.emit(final_wait=outs)
        return nc
    return outs


def mix0_consts():
    s = np.arange(128)
    same = (s[:, None] // 16) == (s[None, :] // 16)
    tri_f = (same & (s[:, None] <= s[None, :])).astype(np.float32)
    tri_b = (same & (s[:, None] >= s[None, :])).astype(np.float32)
    blk = same.astype(np.float32)
    ci = np.zeros((128, 128), np.float32)
    ci[s, s // 16] = 1.0
    hgc = np.stack([tri_f, tri_b, blk, blk - tri_f, blk - tri_b, np.eye(128, dtype=np.float32), ci], 1)
    col = np.arange(64)
    c0 = np.clip(col - 8, 0, 48)
    cm = ((col[:, None] >= c0[None, :]) & (col[:, None] < c0[None, :] + 16)).astype(np.float32)
    cm = np.tile(cm[:, None, :], (2, 15, 1)).reshape(128, 960)
    return {"hgc": np.ascontiguousarray(hgc), "colmask": np.ascontiguousarray(cm)}


def mix0_inputs(ev_w_in, na_q_g, na_k_g, na_rpb, lb_logits, gn_g, hh):
    hs = [4 * hh + j for j in range(4)]
    cols = []
    for sec in range(3):
        for h in hs:
            cols += list(range(sec * 1024 + h * 128, sec * 1024 + (h + 1) * 128))
    for h in hs:
        for sec in (3, 4, 5, 6, 7):
            cols += list(range(sec * 1024 + h * 128, sec * 1024 + (h + 1) * 128))
    kc = np.arange(64)[:, None]
    qc = np.arange(64)[None, :]
    dc = np.clip(kc - qc + 15, 0, 30)
    G = np.empty((64, 4, 15, 64), np.float32)
    for j, h in enumerate(hs):
        for jp in range(15):
            G[:, j, jp, :] = na_rpb[h, 14 - jp][dc]
    lbl = np.ascontiguousarray(lb_logits[:, :, hh * 512:(hh + 1) * 512]).reshape(1, -1)
    return {
        "w_in": np.ascontiguousarray(ev_w_in[:, cols]),
        "gq": np.ascontiguousarray(na_q_g.reshape(128, 1)), "gk": np.ascontiguousarray(na_k_g.reshape(128, 1)),
        "gn": np.ascontiguousarray(gn_g.reshape(128, 1)),
        "rpbG": np.ascontiguousarray(G.reshape(64, 4, 960)), "lbl": lbl,
    }


def build_mod(C=None, io=None):
    own = C is None
    if own:
        nc = bass.Bass("TRN2", target_bir_lowering=False)
        C = Ctx(nc)
        io = IO(nc)
    nc = C.nc
    C.reset()
    P = C.P
    cT = io.inp("cT", [128, 80])
    adaw = io.inp("adaw", [2, D, 1536])
    adab = io.inp("adab", [2, 128, 12])
    modT = io.out("modT", [2, 128, 60])
    c32 = C.carve([128, 80], F32)
    sc = C.carve([128, 80], BF16)
    bias = C.carve([128, 24], F32)
    w = [C.carve([128, 16, 1536], BF16) for l in range(2)]
    res = C.carve([128, 2, 60], F32)
    t_c32, t_sc, t_bias, t_res = Tok(), Tok(), Tok(), Tok()
    t_w = [[Tok() for _ in range(3)] for _ in range(2)]
    P.dma("sp", c32, cT, writes=[t_c32])
    P.dma("sp", bias.rearrange("p (l j) -> p l j", l=2), adab.rearrange("l p j -> p l j"), writes=[t_bias])
    P.op("act", lambda e: e.activation(out=sc, in_=c32, func=AF.Silu), reads=[t_c32], writes=[t_sc])
    for l in range(2):
        for j in range(3):
            P.dma("pool", w[l][:, :, j * 512:(j + 1) * 512],
                  adaw[l, :, j * 512:(j + 1) * 512].rearrange("(kc p) n -> p kc n", p=128), writes=[t_w[l][j]])
    for l in range(2):
        for jj in range(12):
            ps, tps = C.bank()
            for kc in range(16):
                P.op("pe", lambda e, l=l, jj=jj, kc=kc, ps=ps: e.matmul(
                    ps[:, 0:5], lhsT=w[l][:, kc, jj * 128:(jj + 1) * 128], rhs=sc[:, kc * 5:(kc + 1) * 5],
                    start=(kc == 0), stop=(kc == 15)), reads=[t_w[l][jj // 4], t_sc], writes=[tps])
            P.op("dve", lambda e, l=l, jj=jj, ps=ps: e.tensor_scalar(
                out=res[:, l, jj * 5:(jj + 1) * 5], in0=ps[:, 0:5], scalar1=bias[:, l * 12 + jj:l * 12 + jj + 1],
                scalar2=None, op0=ALU.add), reads=[tps, t_bias], writes=[t_res])
    outs = [P.dma("sp", modT.rearrange("l p j -> p l j"), res, reads=[t_res])]
    if own:
        P.emit(final_wait=outs)
        return nc
    return outs


PAIRS = [[0, 1], [2, 3], [4, 5], [6, 7]]


class Chunked:
    def __init__(self, nc, name, nj, per, T, dtype):
        self.per, self.T, self.n = per, T, nj // per
        self.loc = [nc.dram_tensor("%s_l%d" % (name, c), [128, per * T], dtype).ap() for c in range(self.n)]
        self.all = [nc.dram_tensor("%s_a%d" % (name, c), [256, per * T], dtype).ap() for c in range(self.n)]

    def loc_view(self, c):
        return self.loc[c].rearrange("p (k t) -> p k t", k=self.per)

    def all_view(self, c):
        return self.all[c].rearrange("(r p) (k t) -> r p k t", r=2, k=self.per)

    def loc_j(self, j, t0, t1):
        return self.loc_view(j // self.per)[:, j % self.per, t0:t1]


def build_fused(stop=99):
    nc = bass.Bass("TRN2", target_bir_lowering=False)
    C = Ctx(nc)
    io = IO(nc)
    P = C.P
    modtab = [C.carve([128, 96, 2], F32) for _ in range(2)]
    ohs = C.carve([128, 8], F32)
    C.base = C.off
    ohs_d = dram_in(nc, "ohs", [128, 8])
    mod_part = nc.dram_tensor("mod_part", [256, 60], F32).ap()
    mod_all = nc.dram_tensor("mod_all", [2048, 60], F32).ap()
    cat0 = Chunked(nc, "cat0", 8, 2, S_ALL, BF16)
    h1 = Chunked(nc, "h1", 16, 4, 1152, BF16)
    cat1 = Chunked(nc, "cat1", 8, 2, S_LAT, BF16)
    x1_loc = nc.dram_tensor("x1_loc", [1152, D], F32).ap()

    def gather_ch(ch):
        P.barrier()
        for c in range(ch.n):
            P.cc(lambda e, c=c: e.collective_compute("AllGather", ALU.bypass, replica_groups=PAIRS, ins=[ch.loc[c]], outs=[ch.all[c]]))
        P.barrier()

    def gather(src, dst, groups):
        P.barrier()
        P.cc(lambda e: e.collective_compute("AllGather", ALU.bypass, replica_groups=groups, ins=[src], outs=[dst]))
        P.barrier()

    io.prefix = "p0_"
    io.given = {"modT": mod_part.rearrange("(l p) c -> l p c", l=2)}
    build_mod(C, io)
    gather(mod_part, mod_all, [list(range(8))])
    C.reset()
    t_m = Tok()
    P.dma("sp", ohs, ohs_d, writes=[t_m])
    for l in range(2):
        mod5 = C.carve([128, 96, 5], F32)
        P.dma("sp", mod5.rearrange("p (r j) f -> p r (j f)", r=8), mod_all.rearrange("(r l p) c -> l p r c", r=8, l=2)[l], writes=[t_m])
        P.op("dve", lambda e, l=l, mod5=mod5: e.tensor_copy(out=modtab[l][:, :, 1], in_=mod5[:, :, 4]), reads=[t_m], writes=[t_m])
        P.op("dve", lambda e, l=l, mod5=mod5: e.tensor_scalar(out=modtab[l][:, :, 0], in0=mod5[:, :, 0], scalar1=ohs[:, 0:1], scalar2=None, op0=ALU.mult),
             reads=[t_m], writes=[t_m])
        for r in range(1, 4):
            P.op("dve", lambda e, l=l, mod5=mod5, r=r: e.scalar_tensor_tensor(out=modtab[l][:, :, 0], in0=mod5[:, :, r], scalar=ohs[:, r:r + 1],
                                                                              in1=modtab[l][:, :, 0], op0=ALU.mult, op1=ALU.add), reads=[t_m], writes=[t_m])
    P.barrier()
    if stop == 0:
        dbg = dram_out(nc, "dbg", [2, 128, 192])
        outs = [P.dma("sp", dbg[l], modtab[l].rearrange("p a b -> p (a b)")) for l in range(2)]
        P.emit(final_wait=outs)
        return nc
    io.prefix = "p1_"
    io.given = {"mod_sb": modtab[0], "modT": None, "catT": None, "cat_ch": cat0}
    build_mix0(C, io)
    gather_ch(cat0)
    io.prefix = "p2_"
    io.given = {"mod_sb": modtab[0], "modn_sb": modtab[1], "modT": None, "modTn": None,
                "cat_all": cat0, "cat_kind": 0, "oh_pair": ohs[:, 4:6],
                "x_out": x1_loc, "hnT": None, "hn_ch": h1}
    build_post(1024, 128, False, C, io)
    gather_ch(h1)
    io.prefix = "p3_"
    io.given = {"h_all": h1, "catT": None, "cat_ch": cat1}
    build_mix1(C, io)
    gather_ch(cat1)
    io.prefix = "p4_"
    io.given = {"mod_sb": modtab[1], "modn_sb": modtab[1], "modT": None, "modTn": None,
                "cat_all": cat1, "cat_kind": 1, "oh_pair": ohs[:, 4:6],
                "x_tok": x1_loc}
    outs = build_post(1024, 0, True, C, io)
    P.emit(final_wait=outs)
    return nc


_PROGS = {}


def _prog(name, fn, *args):
    key = (name,) + args
    if key not in _PROGS:
        _PROGS[key] = fn(*args)
    return _PROGS[key]


def _gtab(g):
    return np.ascontiguousarray(np.asarray(g, np.float32).reshape(16, 128).T)


def _run(nc, in_maps):
    res = run_bass_kernel_spmd(nc, in_maps, core_ids=list(range(8)))
    return res.results


def kernel_unfused(x, c, ctx, c_ctx, ada_w, ada_b, norm_mix_g, norm_mlp_g, mlp_w1, mlp_w2,
           ev_w_in, ev_w_out, na_q_g, na_k_g, na_rpb, hg_lb_logits, hg_gnorm_g,
           od_w_in, od_w_out, df_q_g, df_k_g, df_lambda, df_subln_g):
    f32 = lambda a: np.asarray(a, dtype=np.float32)
    x, c, ctx, c_ctx, ada_w, ada_b = map(f32, (x, c, ctx, c_ctx, ada_w, ada_b))
    norm_mix_g, norm_mlp_g, mlp_w1, mlp_w2 = map(f32, (norm_mix_g, norm_mlp_g, mlp_w1, mlp_w2))
    ev_w_in, ev_w_out, na_q_g, na_k_g, na_rpb, hg_lb_logits, hg_gnorm_g = map(
        f32, (ev_w_in, ev_w_out, na_q_g, na_k_g, na_rpb, hg_lb_logits, hg_gnorm_g))
    od_w_in, od_w_out, df_q_g, df_k_g, df_lambda, df_subln_g = map(
        f32, (od_w_in, od_w_out, df_q_g, df_k_g, df_lambda, df_subln_g))
    ident = np.eye(128, dtype=np.float32)

    c_all = np.concatenate([c, c_ctx[None]], 0)
    cT = np.ascontiguousarray(c_all.reshape(5, 16, 128).transpose(2, 1, 0)).reshape(128, 80)
    in_maps = []
    for i in range(8):
        cols = slice(i * 1536, (i + 1) * 1536)
        in_maps.append({"cT": cT, "adaw": np.ascontiguousarray(ada_w[:, :, cols]),
                        "adab": np.ascontiguousarray(ada_b[:, cols].reshape(2, 12, 128).transpose(0, 2, 1))})
    r = _run(_prog("mod", build_mod), in_maps)
    mod_all = np.concatenate([r[i]["modT"].reshape(2, 128, 12, 5) for i in range(8)], axis=2)

    def modtab(l, b):
        return np.ascontiguousarray(mod_all[l][:, :, [b, 4]])

    c0 = mix0_consts()
    in_maps = []
    for core in range(8):
        b, hh = core // 2, core % 2
        m = {"x_all": np.ascontiguousarray(np.concatenate([x[b], ctx[b]], 0)), "modT": modtab(0, b), "g_mix": _gtab(norm_mix_g[0])}
        m.update(mix0_inputs(ev_w_in[0], na_q_g[0], na_k_g[0], na_rpb[0], hg_lb_logits, hg_gnorm_g[0], hh))
        m.update(c0)
        in_maps.append(m)
    r = _run(_prog("mix0", build_mix0), in_maps)
    cat0 = np.empty((4, 128, 16, S_ALL), NPBF)
    for core in range(8):
        b, hh = core // 2, core % 2
        o = r[core]["catT"]
        cat0[b][:, 4 * hh:4 * hh + 4] = o[:, 0:4]
        cat0[b][:, 8 + 4 * hh:8 + 4 * hh + 4] = o[:, 4:8]

    in_maps = []
    for core in range(8):
        b, hh = core // 2, core % 2
        lat = slice(hh * 1024, (hh + 1) * 1024)
        cx = slice(hh * 128, (hh + 1) * 128)
        cxa = slice(S_LAT + hh * 128, S_LAT + (hh + 1) * 128)
        in_maps.append({
            "catT": np.ascontiguousarray(np.concatenate([cat0[b][:, :, lat], cat0[b][:, :, cxa]], 2)),
            "x_tok": np.ascontiguousarray(np.concatenate([x[b][lat], ctx[b][cx]], 0)),
            "modT": modtab(0, b), "modTn": modtab(1, b), "g_mlp": _gtab(norm_mlp_g[0]), "g_next": _gtab(norm_mix_g[1]),
            "ident": ident, "w_out": ev_w_out[0], "w1": mlp_w1[0], "w2": mlp_w2[0]})
    r = _run(_prog("post", build_post, 1024, 128, False), in_maps)
    h1 = np.empty((4, 128, 16, S_ALL), NPBF)
    x1 = []
    for core in range(8):
        b, hh = core // 2, core % 2
        hn = r[core]["hnT"]
        h1[b][:, :, hh * 1024:(hh + 1) * 1024] = hn[:, :, 0:1024]
        h1[b][:, :, S_LAT + hh * 128:S_LAT + (hh + 1) * 128] = hn[:, :, 1024:1152]
        x1.append(r[core]["x_out"][0:1024])

    rc = rope_consts()
    in_maps = []
    for core in range(8):
        b, hh = core // 2, core % 2
        m = {"hT": np.ascontiguousarray(h1[b])}
        m.update(mix1_inputs(od_w_in[0], df_q_g[0], df_k_g[0], df_lambda[0], df_subln_g[0], hh))
        m.update(rc)
        in_maps.append(m)
    r = _run(_prog("mix1", build_mix1), in_maps)
    cat1 = np.empty((4, 128, 16, S_LAT), NPBF)
    for core in range(8):
        b, hh = core // 2, core % 2
        cat1[b][:, 8 * hh:8 * hh + 8] = r[core]["catT"]

    in_maps = []
    for core in range(8):
        b, hh = core // 2, core % 2
        lat = slice(hh * 1024, (hh + 1) * 1024)
        in_maps.append({
            "catT": np.ascontiguousarray(cat1[b][:, :, lat]), "x_tok": np.ascontiguousarray(x1[core]),
            "modT": modtab(1, b), "modTn": modtab(1, b), "g_mlp": _gtab(norm_mlp_g[1]), "g_next": _gtab(norm_mix_g[1]),
            "ident": ident, "w_out": od_w_out[0], "w1": mlp_w1[1], "w2": mlp_w2[1]})
    r = _run(_prog("post", build_post, 1024, 0, True), in_maps)
    out = np.empty((4, S_LAT, D), np.float32)
    for core in range(8):
        b, hh = core // 2, core % 2
        out[b, hh * 1024:(hh + 1) * 1024] = r[core]["x_out"]
    return out


def kernel(x, c, ctx, c_ctx, ada_w, ada_b, norm_mix_g, norm_mlp_g, mlp_w1, mlp_w2,
           ev_w_in, ev_w_out, na_q_g, na_k_g, na_rpb, hg_lb_logits, hg_gnorm_g,
           od_w_in, od_w_out, df_q_g, df_k_g, df_lambda, df_subln_g):
    f32 = lambda a: np.asarray(a, dtype=np.float32)
    x, c, ctx, c_ctx, ada_w, ada_b = map(f32, (x, c, ctx, c_ctx, ada_w, ada_b))
    norm_mix_g, norm_mlp_g, mlp_w1, mlp_w2 = map(f32, (norm_mix_g, norm_mlp_g, mlp_w1, mlp_w2))
    ev_w_in, ev_w_out, na_q_g, na_k_g, na_rpb, hg_lb_logits, hg_gnorm_g = map(
        f32, (ev_w_in, ev_w_out, na_q_g, na_k_g, na_rpb, hg_lb_logits, hg_gnorm_g))
    od_w_in, od_w_out, df_q_g, df_k_g, df_lambda, df_subln_g = map(
        f32, (od_w_in, od_w_out, df_q_g, df_k_g, df_lambda, df_subln_g))
    ident = np.eye(128, dtype=np.float32)
    c_all = np.concatenate([c, c_ctx[None]], 0)
    cT = np.ascontiguousarray(c_all.reshape(5, 16, 128).transpose(2, 1, 0)).reshape(128, 80)
    c0 = mix0_consts()
    rc = rope_consts()
    in_maps = []
    for core in range(8):
        b, hh = core // 2, core % 2
        lat = slice(hh * 1024, (hh + 1) * 1024)
        cx = slice(hh * 128, (hh + 1) * 128)
        cols = slice(core * 1536, (core + 1) * 1536)
        ohs = np.zeros((128, 8), np.float32)
        ohs[:, b] = 1.0
        ohs[:, 4 + hh] = 1.0
        m = {"ohs": ohs,
             "p0_cT": cT, "p0_adaw": np.ascontiguousarray(ada_w[:, :, cols]),
             "p0_adab": np.ascontiguousarray(ada_b[:, cols].reshape(2, 12, 128).transpose(0, 2, 1)),
             "p1_x_all": np.ascontiguousarray(np.concatenate([x[b], ctx[b]], 0)), "p1_g_mix": _gtab(norm_mix_g[0]),
             "p2_x_tok": np.ascontiguousarray(np.concatenate([x[b][lat], ctx[b][cx]], 0)),
             "p2_g_mlp": _gtab(norm_mlp_g[0]), "p2_g_next": _gtab(norm_mix_g[1]), "p2_ident": ident,
             "p2_w_out": ev_w_out[0], "p2_w1": mlp_w1[0], "p2_w2": mlp_w2[0],
             "p4_g_mlp": _gtab(norm_mlp_g[1]), "p4_g_next": _gtab(norm_mix_g[1]), "p4_ident": ident,
             "p4_w_out": od_w_out[0], "p4_w1": mlp_w1[1], "p4_w2": mlp_w2[1]}
        for k, v in mix0_inputs(ev_w_in[0], na_q_g[0], na_k_g[0], na_rpb[0], hg_lb_logits, hg_gnorm_g[0], hh).items():
            m["p1_" + k] = v
        for k, v in c0.items():
            m["p1_" + k] = v
        for k, v in mix1_inputs(od_w_in[0], df_q_g[0], df_k_g[0], df_lambda[0], df_subln_g[0], hh).items():
            m["p3_" + k] = v
        for k, v in rc.items():
            m["p3_" + k] = v
        in_maps.append(m)
    r = _run(_prog("fused", build_fused), in_maps)
    out = np.empty((4, S_LAT, D), np.float32)
    for core in range(8):
        b, hh = core // 2, core % 2
        out[b, hh * 1024:(hh + 1) * 1024] = r[core]["p4_x_out"]
    return out
```
